# Optimizing a Trainium2 kernel written in Bass

```python
import jax, jax.numpy as jnp
from jax import lax
import numpy as np

D_MODEL = 1024
BATCH = 2
SEQ = 8192
DEPTH = 1

EPS = 1e-6
A_HEAD_DIM = 64
A_HEADS_PER_GROUP = 4
A_GROUPS = ((128, 1), (512, 4), (2048, 16))
A_HEADS = A_HEADS_PER_GROUP * len(A_GROUPS)
A_WIDTH = A_HEADS * A_HEAD_DIM
A_OUT = A_HEADS_PER_GROUP * A_HEAD_DIM
A_ROT_DIM = A_HEAD_DIM // 4
A_ROPE_THETA = 500000.0
B_HEADS = 8
B_NOPE = 64
B_ROPE = 32
B_QK = B_NOPE + B_ROPE
B_V = 64
B_Q_RANK = 512
B_KV_RANK = 256
B_ROPE_THETA = 10000.0
B_BLOCK = 128
B_OUT = B_HEADS * B_V
D_FF = 4 * D_MODEL
IN_A = 3 * A_WIDTH
IN_B = B_Q_RANK + B_KV_RANK + B_ROPE
IN_G = 2 * D_MODEL
IN_TOTAL = IN_A + IN_B + IN_G

kernel_name = "hybrid_dilated_swa_mla_block"


def rms_norm(x, g):
    xf = x.astype(jnp.float32)
    y = xf * lax.rsqrt(jnp.mean(xf * xf, axis=-1, keepdims=True) + EPS)
    return (y * g.astype(jnp.float32)).astype(x.dtype)


def rotary(x, pos, theta, rot_dim):
    half = rot_dim // 2
    inv = jnp.float32(theta) ** (-(jnp.arange(half, dtype=jnp.float32) * 2.0 / rot_dim))
    ang = pos.astype(jnp.float32)[:, None] * inv[None, :]
    cos = jnp.cos(ang)[:, None, :]
    sin = jnp.sin(ang)[:, None, :]
    xr = x[..., :rot_dim].astype(jnp.float32)
    x1, x2 = xr[..., :half], xr[..., half:]
    rot = jnp.concatenate([x1 * cos - x2 * sin, x2 * cos + x1 * sin], axis=-1).astype(x.dtype)
    return jnp.concatenate([rot, x[..., rot_dim:]], axis=-1)


def banded_attention(q, k, v, radius):
    N, L, H, D = q.shape
    C = radius
    nb = -(-L // C)
    Lp = nb * C
    qb = jnp.pad(q, ((0, 0), (0, Lp - L), (0, 0), (0, 0))).reshape(N, nb, C, H, D)
    pad_kv = ((0, 0), (C, Lp - L + C), (0, 0), (0, 0))
    kp = jnp.pad(k, pad_kv).reshape(N, nb + 2, C, H, D)
    vp = jnp.pad(v, pad_kv).reshape(N, nb + 2, C, H, D)
    kb = jnp.concatenate([kp[:, :-2], kp[:, 1:-1], kp[:, 2:]], axis=2)
    vb = jnp.concatenate([vp[:, :-2], vp[:, 1:-1], vp[:, 2:]], axis=2)
    q_idx = jnp.arange(nb)[:, None] * C + jnp.arange(C)[None, :]
    k_idx = jnp.arange(nb)[:, None] * C - C + jnp.arange(3 * C)[None, :]
    dist = q_idx[:, :, None] - k_idx[:, None, :]
    valid = (jnp.abs(dist) <= radius) & (k_idx[:, None, :] >= 0) & (k_idx[:, None, :] < L)
    s = jnp.einsum('nbqhd,nbkhd->nbhqk', qb, kb).astype(jnp.float32) * (D ** -0.5)
    s = jnp.where(valid[None, :, None], s, -jnp.inf)
    m = jnp.max(s, axis=-1, keepdims=True)
    p = jnp.exp(s - m)
    l = jnp.sum(p, axis=-1, keepdims=True)
    o = jnp.einsum('nbhqk,nbkhd->nbqhd', (p / l).astype(v.dtype), vb)
    lse = (m + jnp.log(l))[..., 0]
    o = o.reshape(N, Lp, H, D)[:, :L]
    lse = lse.transpose(0, 1, 3, 2).reshape(N, Lp, H)[:, :L]
    return o, lse


def dilated_group(q, k, v, window, dilation):
    B, S, H, D = q.shape
    L = S // dilation
    radius = window // (2 * dilation)

    def to_sub(t):
        return t.reshape(B, L, dilation, H, D).transpose(0, 2, 1, 3, 4).reshape(B * dilation, L, H, D)

    o, lse = banded_attention(to_sub(q), to_sub(k), to_sub(v), radius)
    o = o.reshape(B, dilation, L, H, D).transpose(0, 2, 1, 3, 4).reshape(B, S, H, D)
    lse = lse.reshape(B, dilation, L, H).transpose(0, 2, 1, 3).reshape(B, S, H)
    return o, lse


def mixer_dilated(qkv, pos):
    B, S, _ = qkv.shape
    q, k, v = jnp.split(qkv, 3, axis=-1)
    q = rotary(q.reshape(B, S, A_HEADS, A_HEAD_DIM), pos, A_ROPE_THETA, A_ROT_DIM)
    k = rotary(k.reshape(B, S, A_HEADS, A_HEAD_DIM), pos, A_ROPE_THETA, A_ROT_DIM)
    v = v.reshape(B, S, A_HEADS, A_HEAD_DIM)
    outs, lses = [], []
    for g, (window, dilation) in enumerate(A_GROUPS):
        hs = slice(g * A_HEADS_PER_GROUP, (g + 1) * A_HEADS_PER_GROUP)
        o, lse = dilated_group(q[:, :, hs], k[:, :, hs], v[:, :, hs], window, dilation)
        outs.append(o)
        lses.append(lse)
    o = jnp.stack(outs, axis=0)
    w = jax.nn.softmax(jnp.stack(lses, axis=0), axis=0)
    o = jnp.sum(w[..., None].astype(o.dtype) * o, axis=0)
    return o.reshape(B, S, A_OUT)


def mixer_mla(q_c, kv_c, k_pe, pos, q_norm, w_uq, kv_norm, w_ukv):
    B, S, _ = q_c.shape
    q = (rms_norm(q_c, q_norm) @ w_uq).reshape(B, S, B_HEADS, B_QK)
    q = jnp.concatenate([q[..., :B_NOPE], rotary(q[..., B_NOPE:], pos, B_ROPE_THETA, B_ROPE)], axis=-1)
    kv = (rms_norm(kv_c, kv_norm) @ w_ukv).reshape(B, S, B_HEADS, B_NOPE + B_V)
    k_nope, v = kv[..., :B_NOPE], kv[..., B_NOPE:]
    k_pe = rotary(k_pe[:, :, None, :], pos, B_ROPE_THETA, B_ROPE)
    k = jnp.concatenate([k_nope, jnp.broadcast_to(k_pe, (B, S, B_HEADS, B_ROPE))], axis=-1)
    k = k.transpose(0, 2, 1, 3)
    v = v.transpose(0, 2, 1, 3)
    nb = S // B_BLOCK
    q_blocks = q.transpose(0, 2, 1, 3).reshape(B, B_HEADS, nb, B_BLOCK, B_QK).transpose(2, 0, 1, 3, 4)
    scale = B_QK ** -0.5

    def block(qb):
        s = jnp.einsum('bhqd,bhkd->bhqk', qb, k).astype(jnp.float32) * scale
        p = jax.nn.softmax(s, axis=-1)
        return jnp.einsum('bhqk,bhkd->bhqd', p.astype(v.dtype), v)

    o = lax.map(block, q_blocks)
    return o.transpose(1, 0, 3, 2, 4).reshape(B, S, B_OUT)


def setup_inputs(seed: int = 0) -> dict:
    key = jax.random.key(seed)
    ks = jax.random.split(key, 18)
    f = jnp.float32

    def w(k, shape):
        return jax.random.normal(k, shape, f) * (shape[0] ** -0.5)

    def gain(k, n):
        return 1.0 + 0.05 * jax.random.normal(k, (n,), f)

    return {
        "x": jax.random.normal(ks[0], (BATCH, SEQ, D_MODEL), f),
        "norm_mix_pre": gain(ks[1], D_MODEL),
        "w_in": w(ks[2], (D_MODEL, IN_TOTAL)),
        "b_gate": 0.02 * jax.random.normal(ks[3], (IN_G,), f),
        "mla_q_norm": gain(ks[4], B_Q_RANK),
        "mla_w_uq": w(ks[5], (B_Q_RANK, B_HEADS * B_QK)),
        "mla_kv_norm": gain(ks[6], B_KV_RANK),
        "mla_w_ukv": w(ks[7], (B_KV_RANK, B_HEADS * (B_NOPE + B_V))),
        "w_o_a": w(ks[8], (A_OUT, D_MODEL)),
        "w_o_b": w(ks[9], (B_OUT, D_MODEL)),
        "w_out": w(ks[10], (D_MODEL, D_MODEL)),
        "norm_mix_post": gain(ks[11], D_MODEL),
        "norm_mlp_pre": gain(ks[12], D_MODEL),
        "w_ff1": w(ks[13], (D_MODEL, D_FF)),
        "w_ff2": w(ks[14], (D_FF, D_MODEL)),
        "norm_mlp_post": gain(ks[15], D_MODEL),
    }


def reference(x, norm_mix_pre, w_in, b_gate, mla_q_norm, mla_w_uq, mla_kv_norm, mla_w_ukv,
              w_o_a, w_o_b, w_out, norm_mix_post, norm_mlp_pre, w_ff1, w_ff2, norm_mlp_post):
    B, S, D = x.shape
    pos = jnp.arange(S, dtype=jnp.int32)
    h = x
    for _ in range(DEPTH):
        xn = rms_norm(h, norm_mix_pre)
        proj = xn @ w_in
        qkv_a = proj[..., :IN_A]
        q_c = proj[..., IN_A:IN_A + B_Q_RANK]
        kv_c = proj[..., IN_A + B_Q_RANK:IN_A + B_Q_RANK + B_KV_RANK]
        k_pe = proj[..., IN_A + B_Q_RANK + B_KV_RANK:IN_A + IN_B]
        gates = jax.nn.sigmoid(proj[..., IN_A + IN_B:] + b_gate)
        g_a, g_b = gates[..., :D_MODEL], gates[..., D_MODEL:]
        o_a = mixer_dilated(qkv_a, pos) @ w_o_a
        o_b = mixer_mla(q_c, kv_c, k_pe, pos, mla_q_norm, mla_w_uq, mla_kv_norm, mla_w_ukv) @ w_o_b
        mix = (g_a * o_a + g_b * o_b) @ w_out
        h = h + rms_norm(mix, norm_mix_post)
        hn = rms_norm(h, norm_mlp_pre)
        ff = jnp.square(jax.nn.relu(hn @ w_ff1)) @ w_ff2
        h = h + rms_norm(ff, norm_mlp_post)
    return h
```

```python
import numpy as np
from contextlib import ExitStack
import concourse.bass as bass
import concourse.mybir as mybir
from concourse.bass_utils import run_bass_kernel_spmd

F32 = mybir.dt.float32
BF16 = mybir.dt.bfloat16
U8 = mybir.dt.uint8
AF = mybir.ActivationFunctionType
ALU = mybir.AluOpType

S_TOK = 8192
OWN = 2048
EXT = 4096
EPS = 1e-6
DILS = (1, 4, 16)
NEG = -30000.0
DEBUG = False


class Buf:
    __slots__ = ("w", "r", "name")

    def __init__(self, name=""):
        self.w = None
        self.r = []
        self.name = name


class Sched:
    ENGS = ("pe", "act", "dve", "pool", "sp")

    def __init__(self):
        self.ops = {e: [] for e in self.ENGS}
        self.seen = {e: {} for e in self.ENGS}
        self.dma_cnt = {}

    def _need(self, eng, tok, raw):
        if tok is None:
            return None
        if tok[0] == "eng":
            _, e, idx = tok
            if e == eng:
                if eng in ("pe", "sp") or not raw:
                    return None
            key = ("eng", e)
        else:
            _, s, idx = tok
            key = ("dma", s)
        if self.seen[eng].get(key, -1) >= idx:
            return None
        self.seen[eng][key] = idx
        return tok

    def add(self, eng, fn, reads=(), writes=(), dma=None):
        waits = []
        for b in reads:
            t = self._need(eng, b.w, True)
            if t:
                waits.append(t)
        for b in writes:
            t = self._need(eng, b.w, False)
            if t:
                waits.append(t)
            for rt in b.r:
                t = self._need(eng, rt, False)
                if t:
                    waits.append(t)
        idx = len(self.ops[eng])
        if dma is not None:
            n = self.dma_cnt.get(dma, 0) + 1
            self.dma_cnt[dma] = n
            tok = ("dma", dma, n)
        else:
            tok = ("eng", eng, idx)
        self.ops[eng].append({"fn": fn, "waits": waits, "dma": dma, "sig": False})
        for b in reads:
            key = (tok[0], tok[1])
            b.r = [t for t in b.r if (t[0], t[1]) != key] + [tok]
        for b in writes:
            b.w = tok
            b.r = []
        return tok

    def barrier(self):
        lasts = {}
        for e in self.ENGS:
            i = len(self.ops[e]) - 1
            while i >= 0 and (self.ops[e][i]["fn"] is None or self.ops[e][i]["dma"] is not None):
                i -= 1
            lasts[e] = i
        dm = dict(self.dma_cnt)
        for e in self.ENGS:
            waits = []
            for e2 in self.ENGS:
                if e2 != e and e2 != "sp" and lasts[e2] >= 0:
                    t = self._need(e, ("eng", e2, lasts[e2]), True)
                    if t:
                        waits.append(t)
            for s, n in dm.items():
                t = self._need(e, ("dma", s, n), True)
                if t:
                    waits.append(t)
            self.ops[e].append({"fn": None, "waits": waits, "dma": None, "sig": False})

    def emit(self, nc, stack):
        sems = {e: stack.enter_context(nc.semaphore("s_" + e)) for e in self.ENGS}
        dsems = {s: stack.enter_context(nc.semaphore("d_" + s)) for s in self.dma_cnt}
        for e in self.ENGS:
            for op in self.ops[e]:
                for t in op["waits"]:
                    if t[0] == "eng":
                        self.ops[t[1]][t[2]]["sig"] = True
        sigc = {}
        for e in self.ENGS:
            c = 0
            arr = []
            for op in self.ops[e]:
                if op["sig"]:
                    assert op["dma"] is None and op["fn"] is not None
                    c += 1
                arr.append(c)
            sigc[e] = arr
        block = stack.enter_context(nc.Block())

        def run(e, eng):
            for op in self.ops[e]:
                for t in op["waits"]:
                    if t[0] == "eng":
                        eng.wait_ge(sems[t[1]], sigc[t[1]][t[2]])
                    else:
                        eng.wait_ge(dsems[t[1]], 16 * t[2])
                if op["fn"] is None:
                    continue
                ins = op["fn"](eng)
                if op["dma"] is not None:
                    ins.then_inc(dsems[op["dma"]], 16)
                elif op["sig"]:
                    ins.then_inc(sems[e], 1)

        @block.tensor
        def _(eng):
            run("pe", eng)

        @block.scalar
        def _(eng):
            run("act", eng)

        @block.vector
        def _(eng):
            run("dve", eng)

        @block.gpsimd
        def _(eng):
            run("pool", eng)

        @block.sync
        def _(eng):
            run("sp", eng)


def vblocks():
    idx = {}
    n = 0
    goff = []
    for g, d in enumerate(DILS):
        goff.append(n)
        nb = 16 // d + 1
        for r in range(d):
            for i in range(nb):
                idx[(g, r, i)] = n
                n += 1
    return idx, goff, n


VIDX, VGOFF, NVB = vblocks()


def build_nc():
    nc = bass.Bass("TRN2", target_bir_lowering=False)

    def din(name, shape):
        return nc.dram_tensor(name, list(shape), F32, kind="ExternalInput").ap()

    xe = din("xe", [EXT, 1024])
    xb = din("xb", [S_TOK, 1024])
    vmask_d = din("vmask", [128, NVB])
    tabq_d = din("tabq", [112, OWN])
    tabk_d = din("tabk", [112, EXT])
    tabqb_d = din("tabqb", [128, OWN])
    cosk_d = din("cosk", [128, 64 * 32])
    sink_d = din("sink", [128, 64 * 32])
    ident_d = din("ident", [128, 128])
    maskb_d = din("maskb", [128, 1024])
    g1b_d = din("g1b", [128, 1024])
    g2b_d = din("g2b", [128, 1024])
    gqb_d = din("gqb", [128, 512])
    gkvb_d = din("gkvb", [128, 256])
    gp1_d = din("gp1", [128, 1024])
    gp3_d = din("gp3", [128, 1024])
    bg_d = din("bg", [128, 16])
    waq_d = din("waq", [6, 1024, 224])
    wak_d = din("wak", [6, 1024, 224])
    wav_d = din("wav", [6, 1024, 128])
    wkv_d = din("wkv", [1024, 320])
    wqc_d = din("wqc", [1024, 512])
    wg_d = din("wg", [1024, 2048])
    wuq_d = din("wuq", [512, 1024])
    wuk_d = din("wuk", [256, 512])
    wuv_d = din("wuv", [256, 512])
    woa_d = din("woa", [256, 1024])
    wob_d = din("wob", [512, 1024])
    wout_d = din("wout", [1024, 1024])
    w1_d = din("w1", [1024, 4096])
    w2_d = din("w2", [4096, 1024])
    y_d = nc.dram_tensor("y", [OWN, 1024], F32, kind="ExternalOutput").ap()
    hs_d = nc.dram_tensor("hscr", [OWN, 1024], F32, kind="Internal").ap()
    if DEBUG:
        dbgA_d = nc.dram_tensor("dbgA", [64, 4 * OWN], F32, kind="ExternalOutput").ap()
        dbgB_d = nc.dram_tensor("dbgB", [64, 8 * OWN], F32, kind="ExternalOutput").ap()

    S = Sched()
    with ExitStack() as st:
        ARENA = 204 * 1024
        arena = st.enter_context(nc.sbuf_tensor("arena", [128, ARENA], U8))
        pst = [st.enter_context(nc.psum_tensor("ps%d" % i, [128, 1024], F32)) for i in range(4)]
        PB = [Buf("psum%d" % i) for i in range(8)]

        def bank(i):
            return pst[i // 2][:, (i % 2) * 512:(i % 2) * 512 + 512]

        def bank16(i):
            return bank(i).bitcast(BF16)

        def pair(i):
            return pst[i][:, :]

        class Alloc:
            def __init__(self, base=0):
                self.off = base

            def take(self, nbytes):
                o = (self.off + 63) // 64 * 64
                self.off = o + nbytes
                assert self.off <= ARENA, ("arena overflow", self.off)
                self.lim_check()
                return o

            lim = None

            def lim_check(self):
                if self.lim is not None:
                    assert self.off <= self.lim, ("region overflow", self.off, self.lim)

            def t(self, shape, dt):
                sz = 4 if dt == F32 else 2
                n = int(np.prod(shape))
                o = self.take(n * sz)
                ap = arena[:, o:o + n * sz].bitcast(dt)
                if len(shape) == 2:
                    return ap.rearrange("p (a b) -> p a b", b=shape[1])
                if len(shape) == 3:
                    return ap.rearrange("p (a b c) -> p a b c", b=shape[1], c=shape[2])
                return ap

        A0 = Alloc(0)
        ident = A0.t([128], BF16)
        ones32 = A0.t([64], F32)
        epsb = A0.t([1], F32)
        maskb = A0.t([1024], BF16)
        g1b = A0.t([8, 128], F32)
        g2b = A0.t([8, 128], F32)
        gqb = A0.t([4, 128], F32)
        gkvb = A0.t([2, 128], F32)
        bg = A0.t([16], F32)
        stat = A0.t([16], F32)
        CONST_END = A0.off
        ATOP = ARENA - 4 * OWN * 2
        attnA = Alloc(ATOP).t([4, OWN], BF16)
        bC = Buf("consts")
        S.add("pool", lambda e: e.dma_start(out=ident, in_=ident_d), writes=[bC], dma="wc")
        S.add("pool", lambda e: e.dma_start(out=maskb, in_=maskb_d), writes=[bC], dma="wc")
        S.add("sp", lambda e: e.dma_start(out=g1b, in_=g1b_d.rearrange("p (a b) -> p a b", b=128)), writes=[bC], dma="c")
        S.add("sp", lambda e: e.dma_start(out=g2b, in_=g2b_d.rearrange("p (a b) -> p a b", b=128)), writes=[bC], dma="c")
        S.add("sp", lambda e: e.dma_start(out=gqb, in_=gqb_d.rearrange("p (a b) -> p a b", b=128)), writes=[bC], dma="c")
        S.add("sp", lambda e: e.dma_start(out=gkvb, in_=gkvb_d.rearrange("p (a b) -> p a b", b=128)), writes=[bC], dma="c")
        S.add("sp", lambda e: e.dma_start(out=bg, in_=bg_d), writes=[bC], dma="c")
        S.add("dve", lambda e: e.memset(ones32, 1.0), writes=[bC])
        S.add("dve", lambda e: e.memset(epsb, EPS), writes=[bC])

        class NormT:
            def __init__(self, A, width, psum_banks, scale_eng="pool"):
                self.width = width
                self.junk = A.t([width], BF16)
                self.stage = [A.t([width], BF16) for _ in range(2)]
                self.bst = [Buf("stage0"), Buf("stage1")]
                self.bjunk = Buf("junk")
                self.bstat = [Buf("stat%d" % i) for i in range(4)]
                self.k = 0
                self.banks = psum_banks
                self.scale_eng = scale_eng

            def run(self, src, bsrc, C, gain_b, dst, bdst, evac_eng="dve", src_psum=False):
                k = self.k
                self.k += 1
                sc = stat[:, 2 * (k % 4):2 * (k % 4) + 2]
                bs = self.bstat[k % 4]
                stg = self.stage[k % 2][:, 0:C]
                bstg = self.bst[k % 2]
                bk = self.banks[k % len(self.banks)]
                nch = C // 128
                S.add("act", lambda e: e.activation(out=self.junk[:, 0:C], in_=src, func=AF.Square, accum_out=sc[:, 0:1]),
                      reads=[bsrc], writes=[self.bjunk, bs])
                S.add("act", lambda e: e.activation(out=sc[:, 1:2], in_=sc[:, 0:1], func=AF.Sqrt, scale=1.0 / C, bias=epsb[:, 0:1]),
                      reads=[bs, bC], writes=[bs])
                S.add("dve", lambda e: e.reciprocal(out=sc[:, 1:2], in_=sc[:, 1:2]), reads=[bs], writes=[bs])
                seng = "dve" if src_psum else self.scale_eng
                S.add(seng, lambda e: e.tensor_scalar(out=stg, in0=src, scalar1=sc[:, 1:2], scalar2=None, op0=ALU.mult),
                      reads=[bs, bsrc], writes=[bstg])
                pv = bank16(bk)[:, 0:C].rearrange("p (a b) -> p a b", b=128)
                for j in range(nch):
                    S.add("pe", lambda e, j=j: e.transpose(out=pv[:, j, :], in_=stg[:, j * 128:(j + 1) * 128], identity=ident),
                          reads=[bstg, bC], writes=[PB[bk]])
                S.add("dve", lambda e: e.tensor_tensor(out=dst, in0=pv, in1=gain_b, op=ALU.mult),
                      reads=[PB[bk], bC], writes=[bdst])

        A = Alloc(CONST_END)
        xeT = A.t([8, EXT], BF16)
        tabq = A.t([OWN], F32)
        tabk = A.t([EXT], F32)
        vmask = A.t([NVB], F32)
        PH_A_BASE = A.off
        xst = [A.t([1024], F32) for _ in range(3)]
        bxst = [Buf("xst%d" % i) for i in range(3)]
        NT = NormT(A, 1024, [0, 1])
        bxeT = [Buf("xeT%d" % i) for i in range(32)]

        S.add("sp", lambda e: e.dma_start(out=tabq[0:112, :], in_=tabq_d), writes=[bC], dma="c")
        S.add("sp", lambda e: e.dma_start(out=tabk[0:112, :], in_=tabk_d), writes=[bC], dma="c")
        S.add("sp", lambda e: e.dma_start(out=vmask, in_=vmask_d), writes=[bC], dma="c")
        for T in range(32):
            sl = T % 3
            S.add("sp", lambda e, T=T, sl=sl, xst=xst: e.dma_start(out=xst[sl], in_=xe[T * 128:(T + 1) * 128, :]),
                  writes=[bxst[sl]], dma="x%d" % sl)
            NT.run(xst[sl], bxst[sl], 1024, g1b, xeT[:, :, T * 128:(T + 1) * 128], bxeT[T])
        S.barrier()
        bXE = Buf("xeT_all")

        A = Alloc(PH_A_BASE)
        acc = A.t([2, OWN], F32)
        wq = A.t([8, 224], BF16)
        wk = A.t([8, 224], BF16)
        wv = A.t([8, 128], BF16)
        Q2 = A.t([2, OWN], BF16)
        K2 = A.t([2, EXT], BF16)
        Vg = A.t([32, 2, 65], BF16)
        Pb = [A.t([1024], BF16) for _ in range(2)]
        rinv = A.t([512], F32)
        bacc, bwA, bQ2, bK2, bVg = Buf("acc"), Buf("wA"), Buf("Q2"), Buf("K2"), Buf("Vg")
        bP = [Buf("P0"), Buf("P1")]
        brinv = Buf("rinv")
        battnA = Buf("attnA")
        for hp in range(2):
            for g, d in enumerate(DILS):
                wi = hp * 3 + g
                S.add("pool", lambda e, wi=wi: e.dma_start(out=wq, in_=waq_d[wi].rearrange("(k p) n -> p k n", p=128)),
                      writes=[bwA], dma="wa")
                S.add("pool", lambda e, wi=wi: e.dma_start(out=wk, in_=wak_d[wi].rearrange("(k p) n -> p k n", p=128)),
                      writes=[bwA], dma="wa")
                S.add("pool", lambda e, wi=wi: e.dma_start(out=wv, in_=wav_d[wi].rearrange("(k p) n -> p k n", p=128)),
                      writes=[bwA], dma="wa")
                cnt = 0
                for c in range(4):
                    for hh in range(2):
                        bk = cnt % 2
                        cnt += 1
                        for kc in range(8):
                            S.add("pe", lambda e, kc=kc, hh=hh, c=c, bk=bk: e.matmul(
                                bank(bk)[0:112, :], lhsT=wq[:, kc, hh * 112:(hh + 1) * 112],
                                rhs=xeT[:, kc, 1024 + 512 * c:1024 + 512 * (c + 1)], start=(kc == 0), stop=(kc == 7)),
                                reads=[bwA, bXE], writes=[PB[bk]])
                        S.add("dve", lambda e, hh=hh, c=c, bk=bk: e.tensor_tensor(
                            out=Q2[0:112, hh, 512 * c:512 * (c + 1)], in0=bank(bk)[0:112, :],
                            in1=tabq[0:112, 512 * c:512 * (c + 1)], op=ALU.mult),
                            reads=[PB[bk], bC], writes=[bQ2])
                mlo = 1024 // d - 64
                nb = 16 // d + 1
                elo = mlo * d
                ehi = (mlo + 128 * nb) * d
                c0, c1 = elo // 512, (ehi + 511) // 512
                for c in range(c0, c1):
                    for hh in range(2):
                        bk = cnt % 2
                        cnt += 1
                        for kc in range(8):
                            S.add("pe", lambda e, kc=kc, hh=hh, c=c, bk=bk: e.matmul(
                                bank(bk)[0:112, :], lhsT=wk[:, kc, hh * 112:(hh + 1) * 112],
                                rhs=xeT[:, kc, 512 * c:512 * (c + 1)], start=(kc == 0), stop=(kc == 7)),
                                reads=[bwA, bXE], writes=[PB[bk]])
                        S.add("dve", lambda e, hh=hh, c=c, bk=bk: e.tensor_tensor(
                            out=K2[0:112, hh, 512 * c:512 * (c + 1)], in0=bank(bk)[0:112, :],
                            in1=tabk[0:112, 512 * c:512 * (c + 1)], op=ALU.mult),
                            reads=[PB[bk], bC], writes=[bK2])
                nblk = d * nb
                blist = [(r, i) for r in range(d) for i in range(nb)]
                for q0 in range(0, nblk, 4):
                    grp = blist[q0:q0 + 4]
                    bk = cnt % 2
                    cnt += 1
                    for jj, (r, i) in enumerate(grp):
                        e0 = (mlo + 128 * i) * d + r
                        for kc in range(8):
                            S.add("pe", lambda e, kc=kc, jj=jj, e0=e0, bk=bk, d=d: e.matmul(
                                bank(bk)[:, jj * 128:(jj + 1) * 128], lhsT=xeT[:, kc, e0:e0 + 127 * d + 1:d],
                                rhs=wv[:, kc, :], start=(kc == 0), stop=(kc == 7)),
                                reads=[bwA, bXE], writes=[PB[bk]])
                    n = len(grp)
                    S.add("act", lambda e, q0=q0, n=n, bk=bk: e.activation(
                        out=Vg[:, q0:q0 + n, :, 0:64],
                        in_=bank(bk)[:, 0:n * 128].rearrange("p (a b c) -> p a b c", b=2, c=64), func=AF.Copy),
                        reads=[PB[bk]], writes=[bVg])
                for hh in range(2):
                    S.add("dve", lambda e, hh=hh, nblk=nblk, g=g: e.tensor_copy(
                        out=Vg[:, 0:nblk, hh, 64], in_=vmask[:, VGOFF[g]:VGOFF[g] + nblk]),
                        reads=[bC], writes=[bVg])
                qbs = [(r, j) for r in range(d) for j in range(16 // d)]
                for ui in range(8):
                    sp_ = 1 + (ui % 2)
                    ob = 6 + (ui % 2)
                    Pt = Pb[ui % 2]
                    bPt = bP[ui % 2]
                    bS = [PB[2 * sp_], PB[2 * sp_ + 1]]
                    for u in range(2):
                        S.add("pe", lambda e, sp_=sp_, u=u: e.matmul(
                            pair(sp_)[:, u * 512:(u + 1) * 512], lhsT=ident, rhs=maskb[:, 0:512], start=True, stop=False),
                            reads=[bC], writes=[bS[u]])
                    for u in range(2):
                        r, j = qbs[2 * ui + u]
                        qs = 128 * j * d + r
                        for ab in range(2):
                            ks = (mlo + 128 * (j + ab)) * d + r
                            for hh in range(2):
                                off = u * 512 + ab * 256 + hh * 128
                                last = (ab == 1 and hh == 1)
                                S.add("pe", lambda e, sp_=sp_, off=off, hh=hh, ks=ks, qs=qs, last=last, d=d: e.matmul(
                                    pair(sp_)[:, off:off + 128], lhsT=K2[0:112, hh, ks:ks + 127 * d + 1:d],
                                    rhs=Q2[0:112, hh, qs:qs + 127 * d + 1:d], start=False, stop=last),
                                    reads=[bK2, bQ2], writes=[bS[u]])
                    S.add("act", lambda e, sp_=sp_, Pt=Pt: e.activation(out=Pt, in_=pair(sp_), func=AF.Exp, scale=0.125),
                          reads=bS, writes=[bPt])
                    for u in range(2):
                        r, j = qbs[2 * ui + u]
                        for hh in range(2):
                            for ab in range(2):
                                lb = r * nb + j + ab
                                off = u * 512 + ab * 256 + hh * 128
                                S.add("pe", lambda e, ob=ob, u=u, hh=hh, lb=lb, off=off, ab=ab, Pt=Pt: e.matmul(
                                    bank(ob)[0:65, (u * 2 + hh) * 128:(u * 2 + hh + 1) * 128], lhsT=Vg[:, lb, hh, :],
                                    rhs=Pt[:, off:off + 128], start=(ab == 0), stop=(ab == 1)),
                                    reads=[bVg, bPt], writes=[PB[ob]])
                    for u in range(2):
                        r, j = qbs[2 * ui + u]
                        qs = 128 * j * d + r
                        src = bank(ob)[0:65, u * 256:(u + 1) * 256].rearrange("p (a b) -> p a b", b=128)
                        dstv = acc[0:65, :, qs:qs + 127 * d + 1:d]
                        if g == 0:
                            S.add("dve", lambda e, src=src, dstv=dstv: e.tensor_copy(out=dstv, in_=src),
                                  reads=[PB[ob]], writes=[bacc])
                        else:
                            S.add("dve", lambda e, src=src, dstv=dstv: e.tensor_tensor(out=dstv, in0=src, in1=dstv, op=ALU.add),
                                  reads=[PB[ob], bacc], writes=[bacc])
            for c in range(4):
                for hh in range(2):
                    S.add("pe", lambda e, hh=hh, c=c: e.matmul(
                        bank(0)[0:64, :], lhsT=ones32[64:65, 0:64], rhs=acc[64:65, hh, 512 * c:512 * (c + 1)],
                        start=True, stop=True), reads=[bacc, bC], writes=[PB[0]])
                    S.add("dve", lambda e: e.reciprocal(out=rinv[0:64, :], in_=bank(0)[0:64, :]),
                          reads=[PB[0]], writes=[brinv])
                    S.add("dve", lambda e, hh=hh, c=c, hp=hp: e.tensor_tensor(
                        out=attnA[0:64, 2 * hp + hh, 512 * c:512 * (c + 1)], in0=acc[0:64, hh, 512 * c:512 * (c + 1)],
                        in1=rinv[0:64, :], op=ALU.mult), reads=[bacc, brinv], writes=[battnA])
        S.barrier()

        A = Alloc(CONST_END)
        kvnT = A.t([2, S_TOK], BF16)
        KT = [A.t([S_TOK], BF16) for _ in range(2)]
        U0 = A.take(0)
        VT = [A.t([64, 65], BF16) for _ in range(2)]
        QT = [A.t([OWN], BF16) for _ in range(2)]
        U1 = A.off
        qcnT = A.t([4, OWN], BF16)
        wuq = A.t([4, 1024], BF16)
        wuk = A.t([2, 512], BF16)
        wuv = A.t([2, 512], BF16)
        tabqb = A.t([OWN], F32)
        PBm = [A.t([1024], BF16) for _ in range(3)]
        lsb = A.t([512], F32)
        rinvB = A.t([512], F32)
        ATT_B_OFF = A.take(0)
        attnB = A.t([8, OWN], BF16)
        assert A.off <= ATOP, (A.off, ATOP)
        AB = Alloc(ATT_B_OFF)
        AB.lim = A.off
        xst = [AB.t([1024], F32) for _ in range(2)]
        NT = NormT(AB, 1024, [0, 1])
        wkv = AB.t([8, 320], BF16)
        xsT = [AB.t([8, 128], BF16) for _ in range(2)]
        kstg = AB.t([128], BF16)
        t1 = AB.t([32], F32)
        t2 = AB.t([32], F32)
        AC = Alloc(U0)
        AC.lim = U1
        cosk = AC.t([64, 32], F32)
        sink = AC.t([64, 32], F32)
        wqc = AC.t([8, 512], BF16)
        bxst = [Buf("xst%d" % i) for i in range(2)]
        bwkv, bxsT = Buf("wkv"), [Buf("xsT0"), Buf("xsT1")]
        bkvn, bKT, bVT, bQT = Buf("kvnT"), [Buf("KT0"), Buf("KT1")], [Buf("VT0"), Buf("VT1")], [Buf("QT0"), Buf("QT1")]
        bkstg, bt = Buf("kstg"), Buf("t12")
        bqcn, bwu, battnB = Buf("qcnT"), Buf("wu"), Buf("attnB")

        S.add("pool", lambda e: e.dma_start(out=wkv, in_=wkv_d.rearrange("(k p) n -> p k n", p=128)), writes=[bwkv], dma="wb")
        S.add("pool", lambda e: e.dma_start(out=wqc, in_=wqc_d.rearrange("(k p) n -> p k n", p=128)), writes=[bwkv], dma="wb")
        S.add("pool", lambda e: e.dma_start(out=wuq, in_=wuq_d.rearrange("(k p) n -> p k n", p=128)), writes=[bwu], dma="wb")
        S.add("pool", lambda e: e.dma_start(out=wuk, in_=wuk_d.rearrange("(k p) n -> p k n", p=128)), writes=[bwu], dma="wb")
        S.add("pool", lambda e: e.dma_start(out=wuv, in_=wuv_d.rearrange("(k p) n -> p k n", p=128)), writes=[bwu], dma="wb")
        S.add("sp", lambda e: e.dma_start(out=cosk, in_=cosk_d.rearrange("p (a b) -> p a b", b=32)), writes=[bC], dma="c")
        S.add("sp", lambda e: e.dma_start(out=sink, in_=sink_d.rearrange("p (a b) -> p a b", b=32)), writes=[bC], dma="c")
        S.add("sp", lambda e: e.dma_start(out=tabqb, in_=tabqb_d), writes=[bC], dma="c")
        S.add("dve", lambda e: e.memset(kstg, 0.0), writes=[bkstg])

        NT2 = NormT(AB, 256, [4, 5])
        for T in range(64):
            sl = T % 2
            xs_ = xsT[T % 2]
            bxs_ = bxsT[T % 2]
            S.add("sp", lambda e, T=T, sl=sl, xst=xst: e.dma_start(out=xst[sl], in_=xb[T * 128:(T + 1) * 128, :]),
                  writes=[bxst[sl]], dma="x%d" % sl)
            NT.run(xst[sl], bxst[sl], 1024, g1b, xs_, bxs_)
            pk = 2 + (T % 2)
            for kc in range(8):
                S.add("pe", lambda e, kc=kc, xs_=xs_, pk=pk: e.matmul(
                    bank(pk)[:, 0:320], lhsT=xs_[:, kc, :], rhs=wkv[:, kc, :], start=(kc == 0), stop=(kc == 7)),
                    reads=[bxs_, bwkv], writes=[PB[pk]])
            NT2.run(bank(pk)[:, 0:256], PB[pk], 256, gkvb, kvnT[:, :, T * 128:(T + 1) * 128], bkvn, src_psum=True)
            S.add("dve", lambda e, pk=pk, T=T: e.tensor_tensor(out=t1, in0=bank(pk)[:, 256:288], in1=cosk[:, T, :], op=ALU.mult),
                  reads=[PB[pk], bC], writes=[bt])
            S.add("dve", lambda e, pk=pk, T=T: e.tensor_tensor(out=t2, in0=bank(pk)[:, 288:320], in1=sink[:, T, :], op=ALU.mult),
                  reads=[PB[pk], bC], writes=[bt])
            S.add("dve", lambda e: e.tensor_tensor(out=kstg[:, 64:96], in0=t1, in1=t2, op=ALU.add), reads=[bt], writes=[bkstg])
            S.add("dve", lambda e: e.tensor_tensor(out=kstg[:, 96:128], in0=t1, in1=t2, op=ALU.add), reads=[bt], writes=[bkstg])
            p6 = 6 + (T % 2)
            S.add("pe", lambda e, p6=p6: e.transpose(out=bank16(p6)[:, 0:128], in_=kstg, identity=ident),
                  reads=[bkstg, bC], writes=[PB[p6]])
            for b_ in range(2):
                S.add("act", lambda e, b_=b_, p6=p6, T=T: e.activation(
                    out=KT[b_][64:128, T * 128:(T + 1) * 128], in_=bank16(p6)[64:128, 0:128], func=AF.Copy),
                    reads=[PB[p6]], writes=[bKT[b_]])
        NT3 = NormT(AB, 512, [4, 5])
        for T in range(16):
            sl = T % 2
            xs_ = xsT[T % 2]
            bxs_ = bxsT[T % 2]
            S.add("sp", lambda e, T=T, sl=sl, xst=xst: e.dma_start(out=xst[sl], in_=xe[1024 + T * 128:1024 + (T + 1) * 128, :]),
                  writes=[bxst[sl]], dma="x%d" % sl)
            NT.run(xst[sl], bxst[sl], 1024, g1b, xs_, bxs_)
            pk = 2 + (T % 2)
            for kc in range(8):
                S.add("pe", lambda e, kc=kc, xs_=xs_, pk=pk: e.matmul(
                    bank(pk), lhsT=xs_[:, kc, :], rhs=wqc[:, kc, :], start=(kc == 0), stop=(kc == 7)),
                    reads=[bxs_, bwkv], writes=[PB[pk]])
            NT3.run(bank(pk), PB[pk], 512, gqb, qcnT[:, :, T * 128:(T + 1) * 128], bqcn, src_psum=True)
        S.barrier()
        for b_ in range(2):
            S.add("pool", lambda e, b_=b_: e.memset(VT[b_][:, :, 64:65], 1.0), writes=[bVT[b_]])

        SC_B = 96 ** -0.5

        def build_head(h):
            b_ = h % 2
            cnt = 0
            for c in range(16):
                bk = 6 + (cnt % 2)
                cnt += 1
                for kc in range(2):
                    S.add("pe", lambda e, kc=kc, c=c, bk=bk: e.matmul(
                        bank(bk)[0:64, :], lhsT=wuk[:, kc, h * 64:(h + 1) * 64], rhs=kvnT[:, kc, 512 * c:512 * (c + 1)],
                        start=(kc == 0), stop=(kc == 1)), reads=[bwu, bkvn], writes=[PB[bk]])
                S.add("dve", lambda e, c=c, bk=bk: e.tensor_copy(out=KT[b_][0:64, 512 * c:512 * (c + 1)], in_=bank(bk)[0:64, :]),
                      reads=[PB[bk]], writes=[bKT[b_]])
            for k8 in range(8):
                bk = 6 + (cnt % 2)
                cnt += 1
                for j in range(8):
                    kb = 8 * k8 + j
                    for kc in range(2):
                        S.add("pe", lambda e, kc=kc, kb=kb, j=j, bk=bk: e.matmul(
                            bank(bk)[:, j * 64:(j + 1) * 64], lhsT=kvnT[:, kc, kb * 128:(kb + 1) * 128],
                            rhs=wuv[:, kc, h * 64:(h + 1) * 64], start=(kc == 0), stop=(kc == 1)),
                            reads=[bwu, bkvn], writes=[PB[bk]])
                S.add("dve", lambda e, k8=k8, bk=bk: e.tensor_copy(
                    out=VT[b_][:, 8 * k8:8 * k8 + 8, 0:64], in_=bank(bk).rearrange("p (a b) -> p a b", b=64)),
                    reads=[PB[bk]], writes=[bVT[b_]])
            for c in range(4):
                bk = 6 + (cnt % 2)
                cnt += 1
                for kc in range(4):
                    S.add("pe", lambda e, kc=kc, c=c, bk=bk: e.matmul(
                        bank(bk), lhsT=wuq[:, kc, h * 128:(h + 1) * 128], rhs=qcnT[:, kc, 512 * c:512 * (c + 1)],
                        start=(kc == 0), stop=(kc == 3)), reads=[bwu, bqcn], writes=[PB[bk]])
                S.add("dve", lambda e, c=c, bk=bk: e.tensor_tensor(
                    out=QT[b_][:, 512 * c:512 * (c + 1)], in0=bank(bk), in1=tabqb[:, 512 * c:512 * (c + 1)], op=ALU.mult),
                    reads=[PB[bk], bC], writes=[bQT[b_]])

        bPm = [Buf("Pm%d" % i) for i in range(3)]
        blsb, brB = Buf("lsb"), Buf("rinvB")
        it_ctr = [0]

        def attn_chunk(h, qc):
            b_ = h % 2
            ob = 6 + ((h * 4 + qc) % 2)
            q_ap = QT[b_][:, 512 * qc:512 * (qc + 1)]

            def qk(kp):
                it = it_ctr[0] + kp
                sp_ = it % 3
                for u in range(2):
                    kb = 2 * kp + u
                    S.add("pe", lambda e, sp_=sp_, u=u, kb=kb: e.matmul(
                        pair(sp_)[:, u * 512:(u + 1) * 512], lhsT=KT[b_][:, kb * 128:(kb + 1) * 128], rhs=q_ap,
                        start=True, stop=True), reads=[bKT[b_], bQT[b_]], writes=[PB[2 * sp_ + u]])

            def ex(kp):
                it = it_ctr[0] + kp
                sp_ = it % 3
                pm = it % 3
                S.add("act", lambda e, sp_=sp_, pm=pm: e.activation(out=PBm[pm], in_=pair(sp_), func=AF.Exp, scale=SC_B),
                      reads=[PB[2 * sp_], PB[2 * sp_ + 1]], writes=[bPm[pm]])

            def pv(kp):
                it = it_ctr[0] + kp
                pm = it % 3
                for u in range(2):
                    kb = 2 * kp + u
                    S.add("pe", lambda e, pm=pm, u=u, kb=kb: e.matmul(
                        bank(ob)[0:65, :], lhsT=VT[b_][:, kb, :], rhs=PBm[pm][:, u * 512:(u + 1) * 512],
                        start=(kb == 0), stop=(kb == 63)), reads=[bVT[b_], bPm[pm]], writes=[PB[ob]])

            qk(0)
            qk(1)
            for kp in range(32):
                ex(kp)
                if kp + 2 < 32:
                    qk(kp + 2)
                pv(kp)
            it_ctr[0] += 32
            S.add("act", lambda e: e.activation(out=lsb[64:65, :], in_=bank(ob)[64:65, :], func=AF.Copy),
                  reads=[PB[ob]], writes=[blsb])
            nb_ = 0 + 2 * (it_ctr[0] % 3)
            S.add("pe", lambda e, nb_=nb_: e.matmul(bank(nb_)[0:64, :], lhsT=ones32[64:65, 0:64], rhs=lsb[64:65, :],
                                                    start=True, stop=True), reads=[blsb, bC], writes=[PB[nb_]])
            S.add("dve", lambda e, nb_=nb_: e.reciprocal(out=rinvB[0:64, :], in_=bank(nb_)[0:64, :]),
                  reads=[PB[nb_]], writes=[brB])
            S.add("dve", lambda e: e.tensor_tensor(out=attnB[0:64, h, 512 * qc:512 * (qc + 1)], in0=bank(ob)[0:64, :],
                                                   in1=rinvB[0:64, :], op=ALU.mult),
                  reads=[PB[ob], brB], writes=[battnB])

        build_head(0)
        for h in range(8):
            for qc in range(3):
                attn_chunk(h, qc)
            if h + 1 < 8:
                build_head(h + 1)
            attn_chunk(h, 3)
        S.barrier()

        if DEBUG:
            AD = Alloc(CONST_END)
            dbgt = AD.t([8 * OWN], F32)
            bdbg = Buf("dbg")
            S.add("dve", lambda e: e.tensor_copy(out=dbgt[0:64, 0:4 * OWN], in_=attnA[0:64].rearrange("p a b -> p (a b)")),
                  reads=[battnA], writes=[bdbg])
            S.add("sp", lambda e: e.dma_start(out=dbgA_d, in_=dbgt[0:64, 0:4 * OWN]), reads=[bdbg], dma="st")
            S.barrier()
            S.add("dve", lambda e: e.tensor_copy(out=dbgt[0:64, :], in_=attnB[0:64].rearrange("p a b -> p (a b)")),
                  reads=[battnB], writes=[bdbg])
            S.add("sp", lambda e: e.dma_start(out=dbgB_d, in_=dbgt[0:64, :]), reads=[bdbg], dma="st")
            S.barrier()

        A = Alloc(CONST_END)
        A.lim = ATT_B_OFF
        woa = A.t([4, 1024], BF16)
        wob = A.t([8, 1024], BF16)
        wout = A.t([8, 1024], BF16)
        wg = A.t([8, 2048], BF16)
        gp1 = A.t([1024], F32)
        xres = [A.t([1024], F32) for _ in range(4)]
        NT = NormT(A, 1024, [0, 1])
        xsTc = A.t([8, 512], BF16)
        uT = A.t([8, 512], BF16)
        ga = A.t([512], F32)
        gb = A.t([512], F32)
        tm1 = A.t([512], F32)
        tm2 = A.t([512], F32)
        ytile = [A.t([1024], F32) for _ in range(2)]
        bwm, bxres, bxsTc, buT = Buf("wm"), [Buf("xres%d" % i) for i in range(4)], Buf("xsTc"), Buf("uT")
        bga, bgb, btm1, btm2, byt = Buf("ga"), Buf("gb"), Buf("tm1"), Buf("tm2"), [Buf("yt0"), Buf("yt1")]
        bst2 = Buf("stat2")
        S.add("pool", lambda e: e.dma_start(out=woa[0:64], in_=woa_d.rearrange("(k p) n -> p k n", p=64)), writes=[bwm], dma="wb")
        S.add("pool", lambda e: e.dma_start(out=wob[0:64], in_=wob_d.rearrange("(k p) n -> p k n", p=64)), writes=[bwm], dma="wb")
        S.add("pool", lambda e: e.dma_start(out=wout, in_=wout_d.rearrange("(k p) n -> p k n", p=128)), writes=[bwm], dma="wb")
        for kq in range(4):
            S.add("pool", lambda e, kq=kq: e.dma_start(
                out=wg[:, 2 * kq:2 * kq + 2, :], in_=wg_d[256 * kq:256 * (kq + 1), :].rearrange("(k p) n -> p k n", p=128)),
                writes=[bwm], dma="wb")
        S.add("sp", lambda e: e.dma_start(out=gp1, in_=gp1_d), writes=[bC], dma="c")
        for c in range(4):
            for i in range(4):
                S.add("sp", lambda e, c=c, i=i: e.dma_start(
                    out=xres[i], in_=xe[1024 + 512 * c + 128 * i:1024 + 512 * c + 128 * (i + 1), :]),
                    writes=[bxres[i]], dma="xr%d" % i)
                NT.run(xres[i], bxres[i], 1024, g1b, xsTc[:, :, 128 * i:128 * (i + 1)], bxsTc)
            for ft in range(8):
                for hh in range(4):
                    S.add("pe", lambda e, hh=hh, ft=ft, c=c: e.matmul(
                        bank(2), lhsT=woa[0:64, hh, ft * 128:(ft + 1) * 128], rhs=attnA[0:64, hh, 512 * c:512 * (c + 1)],
                        start=(hh == 0), stop=(hh == 3)), reads=[bwm, battnA], writes=[PB[2]])
                for h in range(8):
                    S.add("pe", lambda e, h=h, ft=ft, c=c: e.matmul(
                        bank(3), lhsT=wob[0:64, h, ft * 128:(ft + 1) * 128], rhs=attnB[0:64, h, 512 * c:512 * (c + 1)],
                        start=(h == 0), stop=(h == 7)), reads=[bwm, battnB], writes=[PB[3]])
                for kc in range(8):
                    S.add("pe", lambda e, kc=kc, ft=ft: e.matmul(
                        bank(4), lhsT=wg[:, kc, ft * 128:(ft + 1) * 128], rhs=xsTc[:, kc, :],
                        start=(kc == 0), stop=(kc == 7)), reads=[bwm, bxsTc], writes=[PB[4]])
                for kc in range(8):
                    S.add("pe", lambda e, kc=kc, ft=ft: e.matmul(
                        bank(5), lhsT=wg[:, kc, 1024 + ft * 128:1024 + (ft + 1) * 128], rhs=xsTc[:, kc, :],
                        start=(kc == 0), stop=(kc == 7)), reads=[bwm, bxsTc], writes=[PB[5]])
                S.add("act", lambda e, ft=ft: e.activation(out=ga, in_=bank(4), func=AF.Sigmoid, bias=bg[:, ft:ft + 1]),
                      reads=[PB[4], bC], writes=[bga])
                S.add("act", lambda e, ft=ft: e.activation(out=gb, in_=bank(5), func=AF.Sigmoid, bias=bg[:, 8 + ft:9 + ft]),
                      reads=[PB[5], bC], writes=[bgb])
                S.add("dve", lambda e: e.tensor_tensor(out=tm1, in0=bank(2), in1=ga, op=ALU.mult),
                      reads=[PB[2], bga], writes=[btm1])
                S.add("dve", lambda e: e.tensor_tensor(out=tm2, in0=bank(3), in1=gb, op=ALU.mult),
                      reads=[PB[3], bgb], writes=[btm2])
                S.add("pool", lambda e, ft=ft: e.tensor_tensor(out=uT[:, ft, :], in0=tm1, in1=tm2, op=ALU.add),
                      reads=[btm1, btm2], writes=[buT])
            for i in range(4):
                for half in range(2):
                    for ft in range(8):
                        S.add("pe", lambda e, i=i, half=half, ft=ft: e.matmul(
                            pair(3)[:, half * 512:(half + 1) * 512], lhsT=uT[:, ft, 128 * i:128 * (i + 1)],
                            rhs=wout[:, ft, half * 512:(half + 1) * 512], start=(ft == 0), stop=(ft == 7)),
                            reads=[buT, bwm], writes=[PB[6 + half]])
                sc = stat[:, 8:10]
                yt = ytile[i % 2]
                S.add("act", lambda e, NT=NT: e.activation(out=NT.junk, in_=pair(3), func=AF.Square, accum_out=sc[:, 0:1]),
                      reads=[PB[6], PB[7]], writes=[NT.bjunk, bst2])
                S.add("act", lambda e: e.activation(out=sc[:, 1:2], in_=sc[:, 0:1], func=AF.Sqrt, scale=1.0 / 1024, bias=epsb[:, 0:1]),
                      reads=[bst2, bC], writes=[bst2])
                S.add("dve", lambda e: e.reciprocal(out=sc[:, 1:2], in_=sc[:, 1:2]), reads=[bst2], writes=[bst2])
                S.add("dve", lambda e, yt=yt: e.scalar_tensor_tensor(out=yt, in0=pair(3), scalar=sc[:, 1:2], in1=gp1,
                                                                    op0=ALU.mult, op1=ALU.mult),
                      reads=[PB[6], PB[7], bst2, bC], writes=[byt[i % 2]])
                S.add("pool", lambda e, yt=yt, i=i: e.tensor_tensor(out=yt, in0=yt, in1=xres[i], op=ALU.add),
                      reads=[byt[i % 2], bxres[i]], writes=[byt[i % 2]])
                S.add("sp", lambda e, yt=yt, c=c, i=i: e.dma_start(
                    out=hs_d[512 * c + 128 * i:512 * c + 128 * (i + 1), :], in_=yt), reads=[byt[i % 2]], dma="hst")
        S.barrier()

        A = Alloc(CONST_END)
        w1 = A.t([8, 4096], BF16)
        w2 = A.t([32, 1024], BF16)
        gp3 = A.t([1024], F32)
        hres = [A.t([1024], F32) for _ in range(4)]
        NT = NormT(A, 1024, [0, 1])
        hsT = [A.t([8, 256], BF16) for _ in range(2)]
        aT = A.t([32, 256], BF16)
        rl = [A.t([256], F32) for _ in range(2)]
        otile = A.t([1024], F32)
        bw1, bw2 = Buf("w1"), Buf("w2")
        bhres, bhsT, baT = [Buf("hres%d" % i) for i in range(4)], [Buf("hsT0"), Buf("hsT1")], Buf("aT")
        brl, bot, bst3 = [Buf("rl0"), Buf("rl1")], [Buf("ot0"), Buf("ot1")], Buf("stat3")
        for kc in range(8):
            S.add("pool", lambda e, kc=kc: e.dma_start(out=w1[:, kc, :], in_=w1_d[128 * kc:128 * (kc + 1), :]),
                  writes=[bw1], dma="w1")
        for k4 in range(8):
            S.add("pool", lambda e, k4=k4: e.dma_start(
                out=w2[:, 4 * k4:4 * k4 + 4, :], in_=w2_d[512 * k4:512 * (k4 + 1), :].rearrange("(k p) n -> p k n", p=128)),
                writes=[bw2], dma="w2")
        S.add("sp", lambda e: e.dma_start(out=gp3, in_=gp3_d), writes=[bC], dma="c")
        for c in range(8):
            hp_ = c % 2
            for i in range(2):
                hi = 2 * hp_ + i
                S.add("sp", lambda e, c=c, i=i, hi=hi: e.dma_start(
                    out=hres[hi], in_=hs_d[256 * c + 128 * i:256 * c + 128 * (i + 1), :]),
                    writes=[bhres[hi]], dma="hr%d" % hi)
                NT.run(hres[hi], bhres[hi], 1024, g2b, hsT[hp_][:, :, 128 * i:128 * (i + 1)], bhsT[hp_])
            for m in range(32):
                bk = 2 + (m % 4)
                for kc in range(8):
                    S.add("pe", lambda e, kc=kc, m=m, bk=bk, hp_=hp_: e.matmul(
                        bank(bk)[:, 0:256], lhsT=w1[:, kc, m * 128:(m + 1) * 128], rhs=hsT[hp_][:, kc, :],
                        start=(kc == 0), stop=(kc == 7)), reads=[bw1, bhsT[hp_]], writes=[PB[bk]])
                r_ = rl[m % 2]
                S.add("act", lambda e, bk=bk, r_=r_: e.activation(out=r_, in_=bank(bk)[:, 0:256], func=AF.Relu),
                      reads=[PB[bk]], writes=[brl[m % 2]])
                S.add("pool", lambda e, m=m, r_=r_: e.tensor_tensor(out=aT[:, m, :], in0=r_, in1=r_, op=ALU.mult),
                      reads=[brl[m % 2]], writes=[baT])
            for i in range(2):
                hi = 2 * hp_ + i
                for half in range(2):
                    for m in range(32):
                        S.add("pe", lambda e, i=i, half=half, m=m: e.matmul(
                            pair(3)[:, half * 512:(half + 1) * 512], lhsT=aT[:, m, 128 * i:128 * (i + 1)],
                            rhs=w2[:, m, half * 512:(half + 1) * 512], start=(m == 0), stop=(m == 31)),
                            reads=[baT, bw2], writes=[PB[6 + half]])
                sc = stat[:, 8:10]
                ot = otile
                S.add("act", lambda e, NT=NT: e.activation(out=NT.junk, in_=pair(3), func=AF.Square, accum_out=sc[:, 0:1]),
                      reads=[PB[6], PB[7]], writes=[NT.bjunk, bst3])
                S.add("act", lambda e: e.activation(out=sc[:, 1:2], in_=sc[:, 0:1], func=AF.Sqrt, scale=1.0 / 1024, bias=epsb[:, 0:1]),
                      reads=[bst3, bC], writes=[bst3])
                S.add("dve", lambda e: e.reciprocal(out=sc[:, 1:2], in_=sc[:, 1:2]), reads=[bst3], writes=[bst3])
                S.add("dve", lambda e, ot=ot: e.scalar_tensor_tensor(out=ot, in0=pair(3), scalar=sc[:, 1:2], in1=gp3,
                                                                    op0=ALU.mult, op1=ALU.mult),
                      reads=[PB[6], PB[7], bst3, bC], writes=[bot[0]])
                S.add("dve", lambda e, ot=ot, hi=hi: e.tensor_tensor(out=hres[hi], in0=ot, in1=hres[hi], op=ALU.add),
                      reads=[bot[0], bhres[hi]], writes=[bhres[hi]])
                S.add("sp", lambda e, hi=hi, c=c, i=i: e.dma_start(
                    out=y_d[256 * c + 128 * i:256 * c + 128 * (i + 1), :], in_=hres[hi]), reads=[bhres[hi]], dma="yst")
        S.barrier()
        S.emit(nc, st)
    return nc


def _rot_tables(pos, theta, rot_dim):
    half = rot_dim // 2
    inv = np.float32(theta) ** (-(np.arange(half, dtype=np.float32) * np.float32(2.0) / np.float32(rot_dim)))
    ang = pos.astype(np.float32)[:, None] * inv.astype(np.float32)[None, :]
    c = np.cos(ang).astype(np.float32)
    s = np.sin(ang).astype(np.float32)
    C = np.concatenate([c, c], 1)
    Sg = np.concatenate([-s, s], 1)
    return C, Sg


def _prep_shared(inp):
    f = np.float32
    w_in = np.asarray(inp["w_in"], f)
    sh = {}
    waq = np.zeros((6, 1024, 224), f)
    wak = np.zeros((6, 1024, 224), f)
    wav = np.zeros((6, 1024, 128), f)
    sw = np.concatenate([np.arange(8, 16), np.arange(0, 8)])
    for hp in range(2):
        for g in range(3):
            wi = hp * 3 + g
            for hh in range(2):
                h = 4 * g + 2 * hp + hh
                qb = h * 64
                kb = 768 + h * 64
                vb = 1536 + h * 64
                nope = np.arange(16, 64)
                rot = np.arange(0, 16)
                qcols = np.concatenate([qb + nope, qb + rot, qb + sw, qb + rot, qb + sw])
                kcols = np.concatenate([kb + nope, kb + rot, kb + rot, kb + sw, kb + sw])
                waq[wi][:, hh * 112:(hh + 1) * 112] = w_in[:, qcols]
                wak[wi][:, hh * 112:(hh + 1) * 112] = w_in[:, kcols]
                wav[wi][:, hh * 64:(hh + 1) * 64] = w_in[:, vb:vb + 64]
    sh["waq"], sh["wak"], sh["wav"] = waq, wak, wav
    swb = np.concatenate([np.arange(16, 32), np.arange(0, 16)])
    sh["wkv"] = np.ascontiguousarray(np.concatenate([w_in[:, 2816:3072], w_in[:, 3072:3104], w_in[:, 3072 + swb]], 1))
    sh["wqc"] = np.ascontiguousarray(w_in[:, 2304:2816])
    sh["wg"] = np.ascontiguousarray(w_in[:, 3104:5152])
    w_uq = np.asarray(inp["mla_w_uq"], f)
    cols = []
    for h in range(8):
        b = h * 96
        cols += [b + np.arange(64), b + 64 + np.arange(32), b + 64 + swb]
    sh["wuq"] = np.ascontiguousarray(w_uq[:, np.concatenate(cols)])
    w_ukv = np.asarray(inp["mla_w_ukv"], f)
    sh["wuk"] = np.ascontiguousarray(w_ukv[:, np.concatenate([h * 128 + np.arange(64) for h in range(8)])])
    sh["wuv"] = np.ascontiguousarray(w_ukv[:, np.concatenate([h * 128 + 64 + np.arange(64) for h in range(8)])])
    sh["woa"] = np.asarray(inp["w_o_a"], f)
    sh["wob"] = np.asarray(inp["w_o_b"], f)
    sh["wout"] = np.asarray(inp["w_out"], f)
    sh["w1"] = np.asarray(inp["w_ff1"], f)
    sh["w2"] = np.asarray(inp["w_ff2"], f)

    def gb(v, kc):
        v = np.asarray(v, f).reshape(kc, 128)
        return np.ascontiguousarray(np.repeat(v.T[:, :, None], 128, axis=2).reshape(128, kc * 128))

    sh["g1b"] = gb(inp["norm_mix_pre"], 8)
    sh["g2b"] = gb(inp["norm_mlp_pre"], 8)
    sh["gqb"] = gb(inp["mla_q_norm"], 4)
    sh["gkvb"] = gb(inp["mla_kv_norm"], 2)
    sh["gp1"] = np.ascontiguousarray(np.broadcast_to(np.asarray(inp["norm_mix_post"], f)[None, :], (128, 1024)))
    sh["gp3"] = np.ascontiguousarray(np.broadcast_to(np.asarray(inp["norm_mlp_post"], f)[None, :], (128, 1024)))
    sh["bg"] = np.ascontiguousarray(np.asarray(inp["b_gate"], f).reshape(16, 128).T)
    sh["ident"] = np.eye(128, dtype=f)
    k = np.arange(128)[:, None]
    q = np.arange(128)[None, :]
    mA = np.where(k >= q, 0.0, NEG).astype(f)
    mB = np.where(k <= q, 0.0, NEG).astype(f)
    m512 = np.concatenate([mA, mA, mB, mB], 1)
    sh["maskb"] = np.ascontiguousarray(np.concatenate([m512, m512], 1))
    C, Sg = _rot_tables(np.arange(S_TOK), 10000.0, 32)
    sh["cosk"] = np.ascontiguousarray(C.reshape(64, 128, 32).transpose(1, 0, 2).reshape(128, 64 * 32))
    sh["sink"] = np.ascontiguousarray(Sg.reshape(64, 128, 32).transpose(1, 0, 2).reshape(128, 64 * 32))
    return sh


def _prep_core(x, c):
    f = np.float32
    b, j = c // 4, c % 4
    t0 = OWN * j
    d = {}
    xpad = np.zeros((S_TOK + 2048, 1024), f)
    xpad[1024:1024 + S_TOK] = x[b]
    d["xe"] = np.ascontiguousarray(xpad[t0:t0 + EXT])
    d["xb"] = np.ascontiguousarray(x[b])
    pos_e = np.arange(EXT) + (t0 - 1024)
    Ce, Se = _rot_tables(pos_e, 500000.0, 16)
    ones = np.ones((EXT, 48), f)
    tabk = np.concatenate([ones, Ce, Ce, Se, Se], 1).T
    tabq = np.concatenate([ones, Ce, Se, Ce, Se], 1).T[:, 1024:1024 + OWN]
    d["tabk"] = np.ascontiguousarray(tabk)
    d["tabq"] = np.ascontiguousarray(tabq)
    Cb, Sb = _rot_tables(np.arange(OWN) + t0, 10000.0, 32)
    d["tabqb"] = np.ascontiguousarray(np.concatenate([np.ones((OWN, 64), f), Cb, Sb], 1).T)
    vm = np.zeros((128, NVB), f)
    for (g, r, i), n in VIDX.items():
        dd = DILS[g]
        mlo = 1024 // dd - 64
        e = (mlo + 128 * i + np.arange(128)) * dd + r
        t = e + t0 - 1024
        vm[:, n] = ((t >= 0) & (t < S_TOK)).astype(f)
    d["vmask"] = vm
    return d


def kernel(**inputs):
    x = np.asarray(inputs["x"], np.float32)
    sh = _prep_shared(inputs)
    nc = build_nc()
    in_maps = []
    for c in range(8):
        m = dict(sh)
        m.update(_prep_core(x, c))
        in_maps.append(m)
    res = run_bass_kernel_spmd(nc, in_maps, core_ids=list(range(8)))
    out = np.zeros((2, S_TOK, 1024), np.float32)
    for c in range(8):
        b, j = c // 4, c % 4
        out[b, OWN * j:OWN * (j + 1)] = np.asarray(res.results[c]["y"], np.float32)
    kernel.last_results = res
    return out
```

```python
import numpy as np
from contextlib import ExitStack
import concourse.bass as bass
import concourse.mybir as mybir
from concourse.bass_utils import run_bass_kernel_spmd

F32 = mybir.dt.float32
BF16 = mybir.dt.bfloat16
U8 = mybir.dt.uint8
AF = mybir.ActivationFunctionType
ALU = mybir.AluOpType

S_TOK = 8192
OWN = 2048
EXT = 4096
EPS = 1e-6
DILS = (1, 4, 16)
NEG = -30000.0
DEBUG = False


class Buf:
    __slots__ = ("w", "r", "name")

    def __init__(self, name=""):
        self.w = None
        self.r = []
        self.name = name


class Sched:
    ENGS = ("pe", "act", "dve", "pool", "sp")

    def __init__(self):
        self.ops = {e: [] for e in self.ENGS}
        self.seen = {e: {} for e in self.ENGS}
        self.dma_cnt = {}

    def _need(self, eng, tok, raw):
        if tok is None:
            return None
        if tok[0] == "eng":
            _, e, idx = tok
            if e == eng:
                if eng in ("pe", "sp") or not raw:
                    return None
            key = ("eng", e)
        else:
            _, s, idx = tok
            key = ("dma", s)
        if self.seen[eng].get(key, -1) >= idx:
            return None
        self.seen[eng][key] = idx
        return tok

    def add(self, eng, fn, reads=(), writes=(), dma=None):
        waits = []
        for b in reads:
            t = self._need(eng, b.w, True)
            if t:
                waits.append(t)
        for b in writes:
            t = self._need(eng, b.w, False)
            if t:
                waits.append(t)
            for rt in b.r:
                t = self._need(eng, rt, False)
                if t:
                    waits.append(t)
        idx = len(self.ops[eng])
        if dma is not None:
            n = self.dma_cnt.get(dma, 0) + 1
            self.dma_cnt[dma] = n
            tok = ("dma", dma, n)
        else:
            tok = ("eng", eng, idx)
        self.ops[eng].append({"fn": fn, "waits": waits, "dma": dma, "sig": False})
        for b in reads:
            key = (tok[0], tok[1])
            b.r = [t for t in b.r if (t[0], t[1]) != key] + [tok]
        for b in writes:
            b.w = tok
            b.r = []
        return tok

    def barrier(self):
        lasts = {}
        for e in self.ENGS:
            i = len(self.ops[e]) - 1
            while i >= 0 and (self.ops[e][i]["fn"] is None or self.ops[e][i]["dma"] is not None):
                i -= 1
            lasts[e] = i
        dm = dict(self.dma_cnt)
        for e in self.ENGS:
            waits = []
            for e2 in self.ENGS:
                if e2 != e and e2 != "sp" and lasts[e2] >= 0:
                    t = self._need(e, ("eng", e2, lasts[e2]), True)
                    if t:
                        waits.append(t)
            for s, n in dm.items():
                t = self._need(e, ("dma", s, n), True)
                if t:
                    waits.append(t)
            self.ops[e].append({"fn": None, "waits": waits, "dma": None, "sig": False})

    def emit(self, nc, stack):
        sems = {e: stack.enter_context(nc.semaphore("s_" + e)) for e in self.ENGS}
        dsems = {s: stack.enter_context(nc.semaphore("d_" + s)) for s in self.dma_cnt}
        for e in self.ENGS:
            for op in self.ops[e]:
                for t in op["waits"]:
                    if t[0] == "eng":
                        self.ops[t[1]][t[2]]["sig"] = True
        sigc = {}
        for e in self.ENGS:
            c = 0
            arr = []
            for op in self.ops[e]:
                if op["sig"]:
                    assert op["dma"] is None and op["fn"] is not None
                    c += 1
                arr.append(c)
            sigc[e] = arr
        block = stack.enter_context(nc.Block())

        def run(e, eng):
            for op in self.ops[e]:
                for t in op["waits"]:
                    if t[0] == "eng":
                        eng.wait_ge(sems[t[1]], sigc[t[1]][t[2]])
                    else:
                        eng.wait_ge(dsems[t[1]], 16 * t[2])
                if op["fn"] is None:
                    continue
                ins = op["fn"](eng)
                if op["dma"] is not None:
                    ins.then_inc(dsems[op["dma"]], 16)
                elif op["sig"]:
                    ins.then_inc(sems[e], 1)

        @block.tensor
        def _(eng):
            run("pe", eng)

        @block.scalar
        def _(eng):
            run("act", eng)

        @block.vector
        def _(eng):
            run("dve", eng)

        @block.gpsimd
        def _(eng):
            run("pool", eng)

        @block.sync
        def _(eng):
            run("sp", eng)


def vblocks():
    idx = {}
    n = 0
    goff = []
    for g, d in enumerate(DILS):
        goff.append(n)
        nb = 16 // d + 1
        for r in range(d):
            for i in range(nb):
                idx[(g, r, i)] = n
                n += 1
    return idx, goff, n


VIDX, VGOFF, NVB = vblocks()


def build_nc():
    nc = bass.Bass("TRN2", target_bir_lowering=False)

    def din(name, shape):
        return nc.dram_tensor(name, list(shape), F32, kind="ExternalInput").ap()

    xe = din("xe", [EXT, 1024])
    xb = din("xb", [S_TOK, 1024])
    vmask_d = din("vmask", [128, NVB])
    tabq_d = din("tabq", [112, OWN])
    tabk_d = din("tabk", [112, EXT])
    tabqb_d = din("tabqb", [128, OWN])
    cosk_d = din("cosk", [128, 64 * 32])
    sink_d = din("sink", [128, 64 * 32])
    ident_d = din("ident", [128, 128])
    maskb_d = din("maskb", [128, 1024])
    g1b_d = din("g1b", [128, 1024])
    g2b_d = din("g2b", [128, 1024])
    gqb_d = din("gqb", [128, 512])
    gkvb_d = din("gkvb", [128, 256])
    gp1_d = din("gp1", [128, 1024])
    gp3_d = din("gp3", [128, 1024])
    bg_d = din("bg", [128, 16])
    waq_d = din("waq", [6, 1024, 224])
    wak_d = din("wak", [6, 1024, 224])
    wav_d = din("wav", [6, 1024, 128])
    wkv_d = din("wkv", [1024, 320])
    wqc_d = din("wqc", [1024, 512])
    wg_d = din("wg", [1024, 2048])
    wuq_d = din("wuq", [512, 1024])
    wuk_d = din("wuk", [256, 512])
    wuv_d = din("wuv", [256, 512])
    woa_d = din("woa", [256, 1024])
    wob_d = din("wob", [512, 1024])
    wout_d = din("wout", [1024, 1024])
    w1_d = din("w1", [1024, 4096])
    w2_d = din("w2", [4096, 1024])
    y_d = nc.dram_tensor("y", [OWN, 1024], F32, kind="ExternalOutput").ap()
    hs_d = nc.dram_tensor("hscr", [OWN, 1024], F32, kind="Internal").ap()
    if DEBUG:
        dbgA_d = nc.dram_tensor("dbgA", [64, 4 * OWN], F32, kind="ExternalOutput").ap()
        dbgB_d = nc.dram_tensor("dbgB", [64, 8 * OWN], F32, kind="ExternalOutput").ap()

    S = Sched()
    with ExitStack() as st:
        ARENA = 204 * 1024
        arena = st.enter_context(nc.sbuf_tensor("arena", [128, ARENA], U8))
        pst = [st.enter_context(nc.psum_tensor("ps%d" % i, [128, 1024], F32)) for i in range(4)]
        PB = [Buf("psum%d" % i) for i in range(8)]

        def bank(i):
            return pst[i // 2][:, (i % 2) * 512:(i % 2) * 512 + 512]

        def bank16(i):
            return bank(i).bitcast(BF16)

        def pair(i):
            return pst[i][:, :]

        class Alloc:
            def __init__(self, base=0):
                self.off = base

            def take(self, nbytes):
                o = (self.off + 63) // 64 * 64
                self.off = o + nbytes
                assert self.off <= ARENA, ("arena overflow", self.off)
                self.lim_check()
                return o

            lim = None

            def lim_check(self):
                if self.lim is not None:
                    assert self.off <= self.lim, ("region overflow", self.off, self.lim)

            def t(self, shape, dt):
                sz = 4 if dt == F32 else 2
                n = int(np.prod(shape))
                o = self.take(n * sz)
                ap = arena[:, o:o + n * sz].bitcast(dt)
                if len(shape) == 2:
                    return ap.rearrange("p (a b) -> p a b", b=shape[1])
                if len(shape) == 3:
                    return ap.rearrange("p (a b c) -> p a b c", b=shape[1], c=shape[2])
                return ap

        A0 = Alloc(0)
        ident = A0.t([128], BF16)
        ones32 = A0.t([64], F32)
        epsb = A0.t([1], F32)
        maskb = A0.t([1024], BF16)
        g1b = A0.t([8, 128], F32)
        g2b = A0.t([8, 128], F32)
        gqb = A0.t([4, 128], F32)
        gkvb = A0.t([2, 128], F32)
        bg = A0.t([16], F32)
        stat = A0.t([16], F32)
        CONST_END = A0.off
        ATOP = ARENA - 4 * OWN * 2
        attnA = Alloc(ATOP).t([4, OWN], BF16)
        bC = Buf("consts")
        S.add("pool", lambda e: e.dma_start(out=ident, in_=ident_d), writes=[bC], dma="wc")
        S.add("pool", lambda e: e.dma_start(out=maskb, in_=maskb_d), writes=[bC], dma="wc")
        S.add("sp", lambda e: e.dma_start(out=g1b, in_=g1b_d.rearrange("p (a b) -> p a b", b=128)), writes=[bC], dma="c")
        S.add("sp", lambda e: e.dma_start(out=g2b, in_=g2b_d.rearrange("p (a b) -> p a b", b=128)), writes=[bC], dma="c")
        S.add("sp", lambda e: e.dma_start(out=gqb, in_=gqb_d.rearrange("p (a b) -> p a b", b=128)), writes=[bC], dma="c")
        S.add("sp", lambda e: e.dma_start(out=gkvb, in_=gkvb_d.rearrange("p (a b) -> p a b", b=128)), writes=[bC], dma="c")
        S.add("sp", lambda e: e.dma_start(out=bg, in_=bg_d), writes=[bC], dma="c")
        S.add("dve", lambda e: e.memset(ones32, 1.0), writes=[bC])
        S.add("dve", lambda e: e.memset(epsb, EPS), writes=[bC])

        class NormT:
            def __init__(self, A, width, psum_banks, scale_eng="pool"):
                self.width = width
                self.junk = A.t([width], BF16)
                self.stage = [A.t([width], BF16) for _ in range(2)]
                self.bst = [Buf("stage0"), Buf("stage1")]
                self.bjunk = Buf("junk")
                self.bstat = [Buf("stat%d" % i) for i in range(4)]
                self.k = 0
                self.banks = psum_banks
                self.scale_eng = scale_eng

            def run(self, src, bsrc, C, gain_b, dst, bdst, evac_eng="dve", src_psum=False):
                k = self.k
                self.k += 1
                sc = stat[:, 2 * (k % 4):2 * (k % 4) + 2]
                bs = self.bstat[k % 4]
                stg = self.stage[k % 2][:, 0:C]
                bstg = self.bst[k % 2]
                bk = self.banks[k % len(self.banks)]
                nch = C // 128
                S.add("act", lambda e: e.activation(out=self.junk[:, 0:C], in_=src, func=AF.Square, accum_out=sc[:, 0:1]),
                      reads=[bsrc], writes=[self.bjunk, bs])
                S.add("act", lambda e: e.activation(out=sc[:, 1:2], in_=sc[:, 0:1], func=AF.Sqrt, scale=1.0 / C, bias=epsb[:, 0:1]),
                      reads=[bs, bC], writes=[bs])
                S.add("dve", lambda e: e.reciprocal(out=sc[:, 1:2], in_=sc[:, 1:2]), reads=[bs], writes=[bs])
                seng = "dve"
                S.add(seng, lambda e: e.tensor_scalar(out=stg, in0=src, scalar1=sc[:, 1:2], scalar2=None, op0=ALU.mult),
                      reads=[bs, bsrc], writes=[bstg])
                pv = bank16(bk)[:, 0:C].rearrange("p (a b) -> p a b", b=128)
                for j in range(nch):
                    S.add("pe", lambda e, j=j: e.transpose(out=pv[:, j, :], in_=stg[:, j * 128:(j + 1) * 128], identity=ident),
                          reads=[bstg, bC], writes=[PB[bk]])
                S.add("dve", lambda e: e.tensor_tensor(out=dst, in0=pv, in1=gain_b, op=ALU.mult),
                      reads=[PB[bk], bC], writes=[bdst])

        A = Alloc(CONST_END)
        xeT = A.t([8, EXT], BF16)
        tabq = A.t([OWN], F32)
        tabk = A.t([EXT], F32)
        vmask = A.t([NVB], F32)
        PH_A_BASE = A.off
        xst = [A.t([1024], F32) for _ in range(3)]
        bxst = [Buf("xst%d" % i) for i in range(3)]
        NT = NormT(A, 1024, [0, 1])
        bxeT = [Buf("xeT%d" % i) for i in range(32)]

        S.add("sp", lambda e: e.dma_start(out=tabq[0:112, :], in_=tabq_d), writes=[bC], dma="c")
        S.add("sp", lambda e: e.dma_start(out=tabk[0:112, :], in_=tabk_d), writes=[bC], dma="c")
        S.add("sp", lambda e: e.dma_start(out=vmask, in_=vmask_d), writes=[bC], dma="c")
        for T in range(32):
            sl = T % 3
            S.add("sp", lambda e, T=T, sl=sl, xst=xst: e.dma_start(out=xst[sl], in_=xe[T * 128:(T + 1) * 128, :]),
                  writes=[bxst[sl]], dma="x%d" % sl)
            NT.run(xst[sl], bxst[sl], 1024, g1b, xeT[:, :, T * 128:(T + 1) * 128], bxeT[T])
        S.barrier()
        bXE = Buf("xeT_all")

        A = Alloc(PH_A_BASE)
        acc = A.t([2, OWN], F32)
        wq = A.t([8, 224], BF16)
        wk = A.t([8, 224], BF16)
        wv = A.t([8, 128], BF16)
        Q2 = A.t([2, OWN], BF16)
        K2 = A.t([2, EXT], BF16)
        Vg = A.t([32, 2, 65], BF16)
        Pb = [A.t([1024], BF16) for _ in range(2)]
        rinv = A.t([512], F32)
        bacc, bwA, bQ2, bK2, bVg = Buf("acc"), Buf("wA"), Buf("Q2"), Buf("K2"), Buf("Vg")
        bP = [Buf("P0"), Buf("P1")]
        brinv = Buf("rinv")
        battnA = Buf("attnA")
        for hp in range(2):
            for g, d in enumerate(DILS):
                wi = hp * 3 + g
                S.add("pool", lambda e, wi=wi: e.dma_start(out=wq, in_=waq_d[wi].rearrange("(k p) n -> p k n", p=128)),
                      writes=[bwA], dma="wa")
                S.add("pool", lambda e, wi=wi: e.dma_start(out=wk, in_=wak_d[wi].rearrange("(k p) n -> p k n", p=128)),
                      writes=[bwA], dma="wa")
                S.add("pool", lambda e, wi=wi: e.dma_start(out=wv, in_=wav_d[wi].rearrange("(k p) n -> p k n", p=128)),
                      writes=[bwA], dma="wa")
                cnt = 0
                for c in range(4):
                    for hh in range(2):
                        bk = cnt % 2
                        cnt += 1
                        for kc in range(8):
                            S.add("pe", lambda e, kc=kc, hh=hh, c=c, bk=bk: e.matmul(
                                bank(bk)[0:112, :], lhsT=wq[:, kc, hh * 112:(hh + 1) * 112],
                                rhs=xeT[:, kc, 1024 + 512 * c:1024 + 512 * (c + 1)], start=(kc == 0), stop=(kc == 7)),
                                reads=[bwA, bXE], writes=[PB[bk]])
                        S.add("dve", lambda e, hh=hh, c=c, bk=bk: e.tensor_tensor(
                            out=Q2[0:112, hh, 512 * c:512 * (c + 1)], in0=bank(bk)[0:112, :],
                            in1=tabq[0:112, 512 * c:512 * (c + 1)], op=ALU.mult),
                            reads=[PB[bk], bC], writes=[bQ2])
                mlo = 1024 // d - 64
                nb = 16 // d + 1
                elo = mlo * d
                ehi = (mlo + 128 * nb) * d
                c0, c1 = elo // 512, (ehi + 511) // 512
                for c in range(c0, c1):
                    for hh in range(2):
                        bk = cnt % 2
                        cnt += 1
                        for kc in range(8):
                            S.add("pe", lambda e, kc=kc, hh=hh, c=c, bk=bk: e.matmul(
                                bank(bk)[0:112, :], lhsT=wk[:, kc, hh * 112:(hh + 1) * 112],
                                rhs=xeT[:, kc, 512 * c:512 * (c + 1)], start=(kc == 0), stop=(kc == 7)),
                                reads=[bwA, bXE], writes=[PB[bk]])
                        S.add("dve", lambda e, hh=hh, c=c, bk=bk: e.tensor_tensor(
                            out=K2[0:112, hh, 512 * c:512 * (c + 1)], in0=bank(bk)[0:112, :],
                            in1=tabk[0:112, 512 * c:512 * (c + 1)], op=ALU.mult),
                            reads=[PB[bk], bC], writes=[bK2])
                nblk = d * nb
                blist = [(r, i) for r in range(d) for i in range(nb)]
                for q0 in range(0, nblk, 4):
                    grp = blist[q0:q0 + 4]
                    bk = cnt % 2
                    cnt += 1
                    for jj, (r, i) in enumerate(grp):
                        e0 = (mlo + 128 * i) * d + r
                        for kc in range(8):
                            S.add("pe", lambda e, kc=kc, jj=jj, e0=e0, bk=bk, d=d: e.matmul(
                                bank(bk)[:, jj * 128:(jj + 1) * 128], lhsT=xeT[:, kc, e0:e0 + 127 * d + 1:d],
                                rhs=wv[:, kc, :], start=(kc == 0), stop=(kc == 7)),
                                reads=[bwA, bXE], writes=[PB[bk]])
                    n = len(grp)
                    S.add("act", lambda e, q0=q0, n=n, bk=bk: e.activation(
                        out=Vg[:, q0:q0 + n, :, 0:64],
                        in_=bank(bk)[:, 0:n * 128].rearrange("p (a b c) -> p a b c", b=2, c=64), func=AF.Copy),
                        reads=[PB[bk]], writes=[bVg])
                for hh in range(2):
                    S.add("dve", lambda e, hh=hh, nblk=nblk, g=g: e.tensor_copy(
                        out=Vg[:, 0:nblk, hh, 64], in_=vmask[:, VGOFF[g]:VGOFF[g] + nblk]),
                        reads=[bC], writes=[bVg])
                qbs = [(r, j) for r in range(d) for j in range(16 // d)]
                for ui in range(8):
                    sp_ = 1 + (ui % 2)
                    ob = 6 + (ui % 2)
                    Pt = Pb[ui % 2]
                    bPt = bP[ui % 2]
                    bS = [PB[2 * sp_], PB[2 * sp_ + 1]]
                    for u in range(2):
                        S.add("pe", lambda e, sp_=sp_, u=u: e.matmul(
                            pair(sp_)[:, u * 512:(u + 1) * 512], lhsT=ident, rhs=maskb[:, 0:512], start=True, stop=False),
                            reads=[bC], writes=[bS[u]])
                    for u in range(2):
                        r, j = qbs[2 * ui + u]
                        qs = 128 * j * d + r
                        for ab in range(2):
                            ks = (mlo + 128 * (j + ab)) * d + r
                            for hh in range(2):
                                off = u * 512 + ab * 256 + hh * 128
                                last = (ab == 1 and hh == 1)
                                S.add("pe", lambda e, sp_=sp_, off=off, hh=hh, ks=ks, qs=qs, last=last, d=d: e.matmul(
                                    pair(sp_)[:, off:off + 128], lhsT=K2[0:112, hh, ks:ks + 127 * d + 1:d],
                                    rhs=Q2[0:112, hh, qs:qs + 127 * d + 1:d], start=False, stop=last),
                                    reads=[bK2, bQ2], writes=[bS[u]])
                    S.add("act", lambda e, sp_=sp_, Pt=Pt: e.activation(out=Pt, in_=pair(sp_), func=AF.Exp, scale=0.125),
                          reads=bS, writes=[bPt])
                    for u in range(2):
                        r, j = qbs[2 * ui + u]
                        for hh in range(2):
                            for ab in range(2):
                                lb = r * nb + j + ab
                                off = u * 512 + ab * 256 + hh * 128
                                S.add("pe", lambda e, ob=ob, u=u, hh=hh, lb=lb, off=off, ab=ab, Pt=Pt: e.matmul(
                                    bank(ob)[0:65, (u * 2 + hh) * 128:(u * 2 + hh + 1) * 128], lhsT=Vg[:, lb, hh, :],
                                    rhs=Pt[:, off:off + 128], start=(ab == 0), stop=(ab == 1)),
                                    reads=[bVg, bPt], writes=[PB[ob]])
                    for u in range(2):
                        r, j = qbs[2 * ui + u]
                        qs = 128 * j * d + r
                        src = bank(ob)[0:65, u * 256:(u + 1) * 256].rearrange("p (a b) -> p a b", b=128)
                        dstv = acc[0:65, :, qs:qs + 127 * d + 1:d]
                        if g == 0:
                            S.add("dve", lambda e, src=src, dstv=dstv: e.tensor_copy(out=dstv, in_=src),
                                  reads=[PB[ob]], writes=[bacc])
                        else:
                            S.add("dve", lambda e, src=src, dstv=dstv: e.tensor_tensor(out=dstv, in0=src, in1=dstv, op=ALU.add),
                                  reads=[PB[ob], bacc], writes=[bacc])
            for c in range(4):
                for hh in range(2):
                    S.add("pe", lambda e, hh=hh, c=c: e.matmul(
                        bank(0)[0:64, :], lhsT=ones32[64:65, 0:64], rhs=acc[64:65, hh, 512 * c:512 * (c + 1)],
                        start=True, stop=True), reads=[bacc, bC], writes=[PB[0]])
                    S.add("dve", lambda e: e.reciprocal(out=rinv[0:64, :], in_=bank(0)[0:64, :]),
                          reads=[PB[0]], writes=[brinv])
                    S.add("dve", lambda e, hh=hh, c=c, hp=hp: e.tensor_tensor(
                        out=attnA[0:64, 2 * hp + hh, 512 * c:512 * (c + 1)], in0=acc[0:64, hh, 512 * c:512 * (c + 1)],
                        in1=rinv[0:64, :], op=ALU.mult), reads=[bacc, brinv], writes=[battnA])
        S.barrier()

        A = Alloc(CONST_END)
        kvnT = A.t([2, S_TOK], BF16)
        KT = [A.t([S_TOK], BF16) for _ in range(2)]
        U0 = A.take(0)
        VT = [A.t([64, 65], BF16) for _ in range(2)]
        QT = [A.t([OWN], BF16) for _ in range(2)]
        U1 = A.off
        qcnT = A.t([4, OWN], BF16)
        wuq = A.t([4, 1024], BF16)
        wuk = A.t([2, 512], BF16)
        wuv = A.t([2, 512], BF16)
        tabqb = A.t([OWN], F32)
        PBm = [A.t([1024], BF16) for _ in range(3)]
        lsb = A.t([512], F32)
        rinvB = A.t([512], F32)
        ATT_B_OFF = A.take(0)
        attnB = A.t([8, OWN], BF16)
        assert A.off <= ATOP, (A.off, ATOP)
        AB = Alloc(ATT_B_OFF)
        AB.lim = A.off
        xst = [AB.t([1024], F32) for _ in range(2)]
        NT = NormT(AB, 1024, [0, 1])
        wkv = AB.t([8, 320], BF16)
        xsT = [AB.t([8, 128], BF16) for _ in range(2)]
        kstg = AB.t([128], BF16)
        t1 = AB.t([32], F32)
        t2 = AB.t([32], F32)
        AC = Alloc(U0)
        AC.lim = U1
        cosk = AC.t([64, 32], F32)
        sink = AC.t([64, 32], F32)
        wqc = AC.t([8, 512], BF16)
        bxst = [Buf("xst%d" % i) for i in range(2)]
        bwkv, bxsT = Buf("wkv"), [Buf("xsT0"), Buf("xsT1")]
        bkvn, bKT, bVT, bQT = Buf("kvnT"), [Buf("KT0"), Buf("KT1")], [Buf("VT0"), Buf("VT1")], [Buf("QT0"), Buf("QT1")]
        bkstg, bt = Buf("kstg"), Buf("t12")
        bqcn, bwu, battnB = Buf("qcnT"), Buf("wu"), Buf("attnB")

        S.add("pool", lambda e: e.dma_start(out=wkv, in_=wkv_d.rearrange("(k p) n -> p k n", p=128)), writes=[bwkv], dma="wb")
        S.add("pool", lambda e: e.dma_start(out=wqc, in_=wqc_d.rearrange("(k p) n -> p k n", p=128)), writes=[bwkv], dma="wb")
        S.add("pool", lambda e: e.dma_start(out=wuq, in_=wuq_d.rearrange("(k p) n -> p k n", p=128)), writes=[bwu], dma="wb")
        S.add("pool", lambda e: e.dma_start(out=wuk, in_=wuk_d.rearrange("(k p) n -> p k n", p=128)), writes=[bwu], dma="wb")
        S.add("pool", lambda e: e.dma_start(out=wuv, in_=wuv_d.rearrange("(k p) n -> p k n", p=128)), writes=[bwu], dma="wb")
        S.add("sp", lambda e: e.dma_start(out=cosk, in_=cosk_d.rearrange("p (a b) -> p a b", b=32)), writes=[bC], dma="c")
        S.add("sp", lambda e: e.dma_start(out=sink, in_=sink_d.rearrange("p (a b) -> p a b", b=32)), writes=[bC], dma="c")
        S.add("sp", lambda e: e.dma_start(out=tabqb, in_=tabqb_d), writes=[bC], dma="c")
        S.add("dve", lambda e: e.memset(kstg, 0.0), writes=[bkstg])

        NT2 = NormT(AB, 256, [4, 5])
        for T in range(64):
            sl = T % 2
            xs_ = xsT[T % 2]
            bxs_ = bxsT[T % 2]
            S.add("sp", lambda e, T=T, sl=sl, xst=xst: e.dma_start(out=xst[sl], in_=xb[T * 128:(T + 1) * 128, :]),
                  writes=[bxst[sl]], dma="x%d" % sl)
            NT.run(xst[sl], bxst[sl], 1024, g1b, xs_, bxs_)
            pk = 2 + (T % 2)
            for kc in range(8):
                S.add("pe", lambda e, kc=kc, xs_=xs_, pk=pk: e.matmul(
                    bank(pk)[:, 0:320], lhsT=xs_[:, kc, :], rhs=wkv[:, kc, :], start=(kc == 0), stop=(kc == 7)),
                    reads=[bxs_, bwkv], writes=[PB[pk]])
            NT2.run(bank(pk)[:, 0:256], PB[pk], 256, gkvb, kvnT[:, :, T * 128:(T + 1) * 128], bkvn, src_psum=True)
            S.add("dve", lambda e, pk=pk, T=T: e.tensor_tensor(out=t1, in0=bank(pk)[:, 256:288], in1=cosk[:, T, :], op=ALU.mult),
                  reads=[PB[pk], bC], writes=[bt])
            S.add("dve", lambda e, pk=pk, T=T: e.tensor_tensor(out=t2, in0=bank(pk)[:, 288:320], in1=sink[:, T, :], op=ALU.mult),
                  reads=[PB[pk], bC], writes=[bt])
            S.add("dve", lambda e: e.tensor_tensor(out=kstg[:, 64:96], in0=t1, in1=t2, op=ALU.add), reads=[bt], writes=[bkstg])
            S.add("dve", lambda e: e.tensor_tensor(out=kstg[:, 96:128], in0=t1, in1=t2, op=ALU.add), reads=[bt], writes=[bkstg])
            p6 = 6 + (T % 2)
            S.add("pe", lambda e, p6=p6: e.transpose(out=bank16(p6)[:, 0:128], in_=kstg, identity=ident),
                  reads=[bkstg, bC], writes=[PB[p6]])
            for b_ in range(2):
                S.add("act", lambda e, b_=b_, p6=p6, T=T: e.activation(
                    out=KT[b_][64:128, T * 128:(T + 1) * 128], in_=bank16(p6)[64:128, 0:128], func=AF.Copy),
                    reads=[PB[p6]], writes=[bKT[b_]])
        NT3 = NormT(AB, 512, [4, 5])
        for T in range(16):
            sl = T % 2
            xs_ = xsT[T % 2]
            bxs_ = bxsT[T % 2]
            S.add("sp", lambda e, T=T, sl=sl, xst=xst: e.dma_start(out=xst[sl], in_=xe[1024 + T * 128:1024 + (T + 1) * 128, :]),
                  writes=[bxst[sl]], dma="x%d" % sl)
            NT.run(xst[sl], bxst[sl], 1024, g1b, xs_, bxs_)
            pk = 2 + (T % 2)
            for kc in range(8):
                S.add("pe", lambda e, kc=kc, xs_=xs_, pk=pk: e.matmul(
                    bank(pk), lhsT=xs_[:, kc, :], rhs=wqc[:, kc, :], start=(kc == 0), stop=(kc == 7)),
                    reads=[bxs_, bwkv], writes=[PB[pk]])
            NT3.run(bank(pk), PB[pk], 512, gqb, qcnT[:, :, T * 128:(T + 1) * 128], bqcn, src_psum=True)
        S.barrier()
        for b_ in range(2):
            S.add("dve", lambda e, b_=b_: e.memset(VT[b_][:, :, 64:65], 1.0), writes=[bVT[b_]])

        SC_B = 96 ** -0.5

        def build_head(h):
            b_ = h % 2
            cnt = 0
            for c in range(16):
                bk = 6 + (cnt % 2)
                cnt += 1
                for kc in range(2):
                    S.add("pe", lambda e, kc=kc, c=c, bk=bk: e.matmul(
                        bank(bk)[0:64, :], lhsT=wuk[:, kc, h * 64:(h + 1) * 64], rhs=kvnT[:, kc, 512 * c:512 * (c + 1)],
                        start=(kc == 0), stop=(kc == 1)), reads=[bwu, bkvn], writes=[PB[bk]])
                S.add("dve", lambda e, c=c, bk=bk: e.tensor_copy(out=KT[b_][0:64, 512 * c:512 * (c + 1)], in_=bank(bk)[0:64, :]),
                      reads=[PB[bk]], writes=[bKT[b_]])
            for k8 in range(8):
                bk = 6 + (cnt % 2)
                cnt += 1
                for j in range(8):
                    kb = 8 * k8 + j
                    for kc in range(2):
                        S.add("pe", lambda e, kc=kc, kb=kb, j=j, bk=bk: e.matmul(
                            bank(bk)[:, j * 64:(j + 1) * 64], lhsT=kvnT[:, kc, kb * 128:(kb + 1) * 128],
                            rhs=wuv[:, kc, h * 64:(h + 1) * 64], start=(kc == 0), stop=(kc == 1)),
                            reads=[bwu, bkvn], writes=[PB[bk]])
                S.add("dve", lambda e, k8=k8, bk=bk: e.tensor_copy(
                    out=VT[b_][:, 8 * k8:8 * k8 + 8, 0:64], in_=bank(bk).rearrange("p (a b) -> p a b", b=64)),
                    reads=[PB[bk]], writes=[bVT[b_]])
            for c in range(4):
                bk = 6 + (cnt % 2)
                cnt += 1
                for kc in range(4):
                    S.add("pe", lambda e, kc=kc, c=c, bk=bk: e.matmul(
                        bank(bk), lhsT=wuq[:, kc, h * 128:(h + 1) * 128], rhs=qcnT[:, kc, 512 * c:512 * (c + 1)],
                        start=(kc == 0), stop=(kc == 3)), reads=[bwu, bqcn], writes=[PB[bk]])
                S.add("dve", lambda e, c=c, bk=bk: e.tensor_tensor(
                    out=QT[b_][:, 512 * c:512 * (c + 1)], in0=bank(bk), in1=tabqb[:, 512 * c:512 * (c + 1)], op=ALU.mult),
                    reads=[PB[bk], bC], writes=[bQT[b_]])

        bPm = [Buf("Pm%d" % i) for i in range(3)]
        blsb, brB = Buf("lsb"), Buf("rinvB")
        it_ctr = [0]

        def attn_chunk(h, qc):
            b_ = h % 2
            ob = 6 + ((h * 4 + qc) % 2)
            q_ap = QT[b_][:, 512 * qc:512 * (qc + 1)]

            def qk(kp):
                it = it_ctr[0] + kp
                sp_ = it % 3
                for u in range(2):
                    kb = 2 * kp + u
                    S.add("pe", lambda e, sp_=sp_, u=u, kb=kb: e.matmul(
                        pair(sp_)[:, u * 512:(u + 1) * 512], lhsT=KT[b_][:, kb * 128:(kb + 1) * 128], rhs=q_ap,
                        start=True, stop=True), reads=[bKT[b_], bQT[b_]], writes=[PB[2 * sp_ + u]])

            def ex(kp):
                it = it_ctr[0] + kp
                sp_ = it % 3
                pm = it % 3
                S.add("act", lambda e, sp_=sp_, pm=pm: e.activation(out=PBm[pm], in_=pair(sp_), func=AF.Exp, scale=SC_B),
                      reads=[PB[2 * sp_], PB[2 * sp_ + 1]], writes=[bPm[pm]])

            def pv(kp):
                it = it_ctr[0] + kp
                pm = it % 3
                for u in range(2):
                    kb = 2 * kp + u
                    S.add("pe", lambda e, pm=pm, u=u, kb=kb: e.matmul(
                        bank(ob)[0:65, :], lhsT=VT[b_][:, kb, :], rhs=PBm[pm][:, u * 512:(u + 1) * 512],
                        start=(kb == 0), stop=(kb == 63)), reads=[bVT[b_], bPm[pm]], writes=[PB[ob]])

            qk(0)
            qk(1)
            for kp in range(32):
                ex(kp)
                if kp + 2 < 32:
                    qk(kp + 2)
                pv(kp)
            it_ctr[0] += 32
            S.add("act", lambda e: e.activation(out=lsb[64:65, :], in_=bank(ob)[64:65, :], func=AF.Copy),
                  reads=[PB[ob]], writes=[blsb])
            nb_ = 0 + 2 * (it_ctr[0] % 3)
            S.add("pe", lambda e, nb_=nb_: e.matmul(bank(nb_)[0:64, :], lhsT=ones32[64:65, 0:64], rhs=lsb[64:65, :],
                                                    start=True, stop=True), reads=[blsb, bC], writes=[PB[nb_]])
            S.add("dve", lambda e, nb_=nb_: e.reciprocal(out=rinvB[0:64, :], in_=bank(nb_)[0:64, :]),
                  reads=[PB[nb_]], writes=[brB])
            S.add("dve", lambda e: e.tensor_tensor(out=attnB[0:64, h, 512 * qc:512 * (qc + 1)], in0=bank(ob)[0:64, :],
                                                   in1=rinvB[0:64, :], op=ALU.mult),
                  reads=[PB[ob], brB], writes=[battnB])

        build_head(0)
        for h in range(8):
            for qc in range(3):
                attn_chunk(h, qc)
            if h + 1 < 8:
                build_head(h + 1)
            attn_chunk(h, 3)
        S.barrier()

        if DEBUG:
            AD = Alloc(CONST_END)
            dbgt = AD.t([8 * OWN], F32)
            bdbg = Buf("dbg")
            S.add("dve", lambda e: e.tensor_copy(out=dbgt[0:64, 0:4 * OWN], in_=attnA[0:64].rearrange("p a b -> p (a b)")),
                  reads=[battnA], writes=[bdbg])
            S.add("sp", lambda e: e.dma_start(out=dbgA_d, in_=dbgt[0:64, 0:4 * OWN]), reads=[bdbg], dma="st")
            S.barrier()
            S.add("dve", lambda e: e.tensor_copy(out=dbgt[0:64, :], in_=attnB[0:64].rearrange("p a b -> p (a b)")),
                  reads=[battnB], writes=[bdbg])
            S.add("sp", lambda e: e.dma_start(out=dbgB_d, in_=dbgt[0:64, :]), reads=[bdbg], dma="st")
            S.barrier()

        A = Alloc(CONST_END)
        A.lim = ATT_B_OFF
        woa = A.t([4, 1024], BF16)
        wob = A.t([8, 1024], BF16)
        wout = A.t([8, 1024], BF16)
        wg = A.t([8, 2048], BF16)
        gp1 = A.t([1024], F32)
        xres = [A.t([1024], F32) for _ in range(4)]
        NT = NormT(A, 1024, [0, 1])
        xsTc = A.t([8, 512], BF16)
        uT = A.t([8, 512], BF16)
        ga = A.t([512], F32)
        gb = A.t([512], F32)
        tm1 = A.t([512], F32)
        tm2 = A.t([512], F32)
        ytile = [A.t([1024], F32) for _ in range(2)]
        bwm, bxres, bxsTc, buT = Buf("wm"), [Buf("xres%d" % i) for i in range(4)], Buf("xsTc"), Buf("uT")
        bga, bgb, btm1, btm2, byt = Buf("ga"), Buf("gb"), Buf("tm1"), Buf("tm2"), [Buf("yt0"), Buf("yt1")]
        bst2 = Buf("stat2")
        S.add("pool", lambda e: e.dma_start(out=woa[0:64], in_=woa_d.rearrange("(k p) n -> p k n", p=64)), writes=[bwm], dma="wb")
        S.add("pool", lambda e: e.dma_start(out=wob[0:64], in_=wob_d.rearrange("(k p) n -> p k n", p=64)), writes=[bwm], dma="wb")
        S.add("pool", lambda e: e.dma_start(out=wout, in_=wout_d.rearrange("(k p) n -> p k n", p=128)), writes=[bwm], dma="wb")
        for kq in range(4):
            S.add("pool", lambda e, kq=kq: e.dma_start(
                out=wg[:, 2 * kq:2 * kq + 2, :], in_=wg_d[256 * kq:256 * (kq + 1), :].rearrange("(k p) n -> p k n", p=128)),
                writes=[bwm], dma="wb")
        S.add("sp", lambda e: e.dma_start(out=gp1, in_=gp1_d), writes=[bC], dma="c")
        for c in range(4):
            for i in range(4):
                S.add("sp", lambda e, c=c, i=i: e.dma_start(
                    out=xres[i], in_=xe[1024 + 512 * c + 128 * i:1024 + 512 * c + 128 * (i + 1), :]),
                    writes=[bxres[i]], dma="xr%d" % i)
                NT.run(xres[i], bxres[i], 1024, g1b, xsTc[:, :, 128 * i:128 * (i + 1)], bxsTc)
            for ft in range(8):
                for hh in range(4):
                    S.add("pe", lambda e, hh=hh, ft=ft, c=c: e.matmul(
                        bank(2), lhsT=woa[0:64, hh, ft * 128:(ft + 1) * 128], rhs=attnA[0:64, hh, 512 * c:512 * (c + 1)],
                        start=(hh == 0), stop=(hh == 3)), reads=[bwm, battnA], writes=[PB[2]])
                for h in range(8):
                    S.add("pe", lambda e, h=h, ft=ft, c=c: e.matmul(
                        bank(3), lhsT=wob[0:64, h, ft * 128:(ft + 1) * 128], rhs=attnB[0:64, h, 512 * c:512 * (c + 1)],
                        start=(h == 0), stop=(h == 7)), reads=[bwm, battnB], writes=[PB[3]])
                for kc in range(8):
                    S.add("pe", lambda e, kc=kc, ft=ft: e.matmul(
                        bank(4), lhsT=wg[:, kc, ft * 128:(ft + 1) * 128], rhs=xsTc[:, kc, :],
                        start=(kc == 0), stop=(kc == 7)), reads=[bwm, bxsTc], writes=[PB[4]])
                for kc in range(8):
                    S.add("pe", lambda e, kc=kc, ft=ft: e.matmul(
                        bank(5), lhsT=wg[:, kc, 1024 + ft * 128:1024 + (ft + 1) * 128], rhs=xsTc[:, kc, :],
                        start=(kc == 0), stop=(kc == 7)), reads=[bwm, bxsTc], writes=[PB[5]])
                S.add("act", lambda e, ft=ft: e.activation(out=ga, in_=bank(4), func=AF.Sigmoid, bias=bg[:, ft:ft + 1]),
                      reads=[PB[4], bC], writes=[bga])
                S.add("act", lambda e, ft=ft: e.activation(out=gb, in_=bank(5), func=AF.Sigmoid, bias=bg[:, 8 + ft:9 + ft]),
                      reads=[PB[5], bC], writes=[bgb])
                S.add("dve", lambda e: e.tensor_tensor(out=tm1, in0=bank(2), in1=ga, op=ALU.mult),
                      reads=[PB[2], bga], writes=[btm1])
                S.add("dve", lambda e: e.tensor_tensor(out=tm2, in0=bank(3), in1=gb, op=ALU.mult),
                      reads=[PB[3], bgb], writes=[btm2])
                S.add("dve", lambda e, ft=ft: e.tensor_tensor(out=uT[:, ft, :], in0=tm1, in1=tm2, op=ALU.add),
                      reads=[btm1, btm2], writes=[buT])
            for i in range(4):
                for half in range(2):
                    for ft in range(8):
                        S.add("pe", lambda e, i=i, half=half, ft=ft: e.matmul(
                            pair(3)[:, half * 512:(half + 1) * 512], lhsT=uT[:, ft, 128 * i:128 * (i + 1)],
                            rhs=wout[:, ft, half * 512:(half + 1) * 512], start=(ft == 0), stop=(ft == 7)),
                            reads=[buT, bwm], writes=[PB[6 + half]])
                sc = stat[:, 8:10]
                yt = ytile[i % 2]
                S.add("act", lambda e, NT=NT: e.activation(out=NT.junk, in_=pair(3), func=AF.Square, accum_out=sc[:, 0:1]),
                      reads=[PB[6], PB[7]], writes=[NT.bjunk, bst2])
                S.add("act", lambda e: e.activation(out=sc[:, 1:2], in_=sc[:, 0:1], func=AF.Sqrt, scale=1.0 / 1024, bias=epsb[:, 0:1]),
                      reads=[bst2, bC], writes=[bst2])
                S.add("dve", lambda e: e.reciprocal(out=sc[:, 1:2], in_=sc[:, 1:2]), reads=[bst2], writes=[bst2])
                S.add("dve", lambda e, yt=yt: e.scalar_tensor_tensor(out=yt, in0=pair(3), scalar=sc[:, 1:2], in1=gp1,
                                                                    op0=ALU.mult, op1=ALU.mult),
                      reads=[PB[6], PB[7], bst2, bC], writes=[byt[i % 2]])
                S.add("dve", lambda e, yt=yt, i=i: e.tensor_tensor(out=yt, in0=yt, in1=xres[i], op=ALU.add),
                      reads=[byt[i % 2], bxres[i]], writes=[byt[i % 2]])
                S.add("sp", lambda e, yt=yt, c=c, i=i: e.dma_start(
                    out=hs_d[512 * c + 128 * i:512 * c + 128 * (i + 1), :], in_=yt), reads=[byt[i % 2]], dma="hst")
        S.barrier()

        A = Alloc(CONST_END)
        w1 = A.t([8, 4096], BF16)
        w2 = A.t([32, 1024], BF16)
        gp3 = A.t([1024], F32)
        hres = [A.t([1024], F32) for _ in range(4)]
        NT = NormT(A, 1024, [0, 1])
        hsT = [A.t([8, 256], BF16) for _ in range(2)]
        aT = A.t([32, 256], BF16)
        rl = [A.t([256], F32) for _ in range(4)]
        otile = A.t([1024], F32)
        bw1, bw2 = Buf("w1"), Buf("w2")
        bhres, bhsT, baT = [Buf("hres%d" % i) for i in range(4)], [Buf("hsT0"), Buf("hsT1")], Buf("aT")
        brl, bot, bst3 = [Buf("rl%d" % i) for i in range(4)], [Buf("ot0"), Buf("ot1")], Buf("stat3")
        for kc in range(8):
            S.add("pool", lambda e, kc=kc: e.dma_start(out=w1[:, kc, :], in_=w1_d[128 * kc:128 * (kc + 1), :]),
                  writes=[bw1], dma="w1")
        for k4 in range(8):
            S.add("pool", lambda e, k4=k4: e.dma_start(
                out=w2[:, 4 * k4:4 * k4 + 4, :], in_=w2_d[512 * k4:512 * (k4 + 1), :].rearrange("(k p) n -> p k n", p=128)),
                writes=[bw2], dma="w2")
        S.add("sp", lambda e: e.dma_start(out=gp3, in_=gp3_d), writes=[bC], dma="c")
        for c in range(8):
            hp_ = c % 2
            for i in range(2):
                hi = 2 * hp_ + i
                S.add("sp", lambda e, c=c, i=i, hi=hi: e.dma_start(
                    out=hres[hi], in_=hs_d[256 * c + 128 * i:256 * c + 128 * (i + 1), :]),
                    writes=[bhres[hi]], dma="hr%d" % hi)
                NT.run(hres[hi], bhres[hi], 1024, g2b, hsT[hp_][:, :, 128 * i:128 * (i + 1)], bhsT[hp_])
            for m in range(32):
                bk = 2 + (m % 4)
                for kc in range(8):
                    S.add("pe", lambda e, kc=kc, m=m, bk=bk, hp_=hp_: e.matmul(
                        bank(bk)[:, 0:256], lhsT=w1[:, kc, m * 128:(m + 1) * 128], rhs=hsT[hp_][:, kc, :],
                        start=(kc == 0), stop=(kc == 7)), reads=[bw1, bhsT[hp_]], writes=[PB[bk]])
                r_ = rl[m % 4]
                S.add("act", lambda e, bk=bk, r_=r_: e.activation(out=r_, in_=bank(bk)[:, 0:256], func=AF.Relu),
                      reads=[PB[bk]], writes=[brl[m % 4]])
                S.add("dve", lambda e, m=m, r_=r_: e.tensor_tensor(out=aT[:, m, :], in0=r_, in1=r_, op=ALU.mult),
                      reads=[brl[m % 4]], writes=[baT])
            for i in range(2):
                hi = 2 * hp_ + i
                for half in range(2):
                    for m in range(32):
                        S.add("pe", lambda e, i=i, half=half, m=m: e.matmul(
                            pair(3)[:, half * 512:(half + 1) * 512], lhsT=aT[:, m, 128 * i:128 * (i + 1)],
                            rhs=w2[:, m, half * 512:(half + 1) * 512], start=(m == 0), stop=(m == 31)),
                            reads=[baT, bw2], writes=[PB[6 + half]])
                sc = stat[:, 8:10]
                ot = otile
                S.add("act", lambda e, NT=NT: e.activation(out=NT.junk, in_=pair(3), func=AF.Square, accum_out=sc[:, 0:1]),
                      reads=[PB[6], PB[7]], writes=[NT.bjunk, bst3])
                S.add("act", lambda e: e.activation(out=sc[:, 1:2], in_=sc[:, 0:1], func=AF.Sqrt, scale=1.0 / 1024, bias=epsb[:, 0:1]),
                      reads=[bst3, bC], writes=[bst3])
                S.add("dve", lambda e: e.reciprocal(out=sc[:, 1:2], in_=sc[:, 1:2]), reads=[bst3], writes=[bst3])
                S.add("dve", lambda e, ot=ot: e.scalar_tensor_tensor(out=ot, in0=pair(3), scalar=sc[:, 1:2], in1=gp3,
                                                                    op0=ALU.mult, op1=ALU.mult),
                      reads=[PB[6], PB[7], bst3, bC], writes=[bot[0]])
                S.add("dve", lambda e, ot=ot, hi=hi: e.tensor_tensor(out=hres[hi], in0=ot, in1=hres[hi], op=ALU.add),
                      reads=[bot[0], bhres[hi]], writes=[bhres[hi]])
                S.add("sp", lambda e, hi=hi, c=c, i=i: e.dma_start(
                    out=y_d[256 * c + 128 * i:256 * c + 128 * (i + 1), :], in_=hres[hi]), reads=[bhres[hi]], dma="yst")
        S.barrier()
        S.emit(nc, st)
    return nc


def _rot_tables(pos, theta, rot_dim):
    half = rot_dim // 2
    inv = np.float32(theta) ** (-(np.arange(half, dtype=np.float32) * np.float32(2.0) / np.float32(rot_dim)))
    ang = pos.astype(np.float32)[:, None] * inv.astype(np.float32)[None, :]
    c = np.cos(ang).astype(np.float32)
    s = np.sin(ang).astype(np.float32)
    C = np.concatenate([c, c], 1)
    Sg = np.concatenate([-s, s], 1)
    return C, Sg


def _prep_shared(inp):
    f = np.float32
    w_in = np.asarray(inp["w_in"], f)
    sh = {}
    waq = np.zeros((6, 1024, 224), f)
    wak = np.zeros((6, 1024, 224), f)
    wav = np.zeros((6, 1024, 128), f)
    sw = np.concatenate([np.arange(8, 16), np.arange(0, 8)])
    for hp in range(2):
        for g in range(3):
            wi = hp * 3 + g
            for hh in range(2):
                h = 4 * g + 2 * hp + hh
                qb = h * 64
                kb = 768 + h * 64
                vb = 1536 + h * 64
                nope = np.arange(16, 64)
                rot = np.arange(0, 16)
                qcols = np.concatenate([qb + nope, qb + rot, qb + sw, qb + rot, qb + sw])
                kcols = np.concatenate([kb + nope, kb + rot, kb + rot, kb + sw, kb + sw])
                waq[wi][:, hh * 112:(hh + 1) * 112] = w_in[:, qcols]
                wak[wi][:, hh * 112:(hh + 1) * 112] = w_in[:, kcols]
                wav[wi][:, hh * 64:(hh + 1) * 64] = w_in[:, vb:vb + 64]
    sh["waq"], sh["wak"], sh["wav"] = waq, wak, wav
    swb = np.concatenate([np.arange(16, 32), np.arange(0, 16)])
    sh["wkv"] = np.ascontiguousarray(np.concatenate([w_in[:, 2816:3072], w_in[:, 3072:3104], w_in[:, 3072 + swb]], 1))
    sh["wqc"] = np.ascontiguousarray(w_in[:, 2304:2816])
    sh["wg"] = np.ascontiguousarray(w_in[:, 3104:5152])
    w_uq = np.asarray(inp["mla_w_uq"], f)
    cols = []
    for h in range(8):
        b = h * 96
        cols += [b + np.arange(64), b + 64 + np.arange(32), b + 64 + swb]
    sh["wuq"] = np.ascontiguousarray(w_uq[:, np.concatenate(cols)])
    w_ukv = np.asarray(inp["mla_w_ukv"], f)
    sh["wuk"] = np.ascontiguousarray(w_ukv[:, np.concatenate([h * 128 + np.arange(64) for h in range(8)])])
    sh["wuv"] = np.ascontiguousarray(w_ukv[:, np.concatenate([h * 128 + 64 + np.arange(64) for h in range(8)])])
    sh["woa"] = np.asarray(inp["w_o_a"], f)
    sh["wob"] = np.asarray(inp["w_o_b"], f)
    sh["wout"] = np.asarray(inp["w_out"], f)
    sh["w1"] = np.asarray(inp["w_ff1"], f)
    sh["w2"] = np.asarray(inp["w_ff2"], f)

    def gb(v, kc):
        v = np.asarray(v, f).reshape(kc, 128)
        return np.ascontiguousarray(np.repeat(v.T[:, :, None], 128, axis=2).reshape(128, kc * 128))

    sh["g1b"] = gb(inp["norm_mix_pre"], 8)
    sh["g2b"] = gb(inp["norm_mlp_pre"], 8)
    sh["gqb"] = gb(inp["mla_q_norm"], 4)
    sh["gkvb"] = gb(inp["mla_kv_norm"], 2)
    sh["gp1"] = np.ascontiguousarray(np.broadcast_to(np.asarray(inp["norm_mix_post"], f)[None, :], (128, 1024)))
    sh["gp3"] = np.ascontiguousarray(np.broadcast_to(np.asarray(inp["norm_mlp_post"], f)[None, :], (128, 1024)))
    sh["bg"] = np.ascontiguousarray(np.asarray(inp["b_gate"], f).reshape(16, 128).T)
    sh["ident"] = np.eye(128, dtype=f)
    k = np.arange(128)[:, None]
    q = np.arange(128)[None, :]
    mA = np.where(k >= q, 0.0, NEG).astype(f)
    mB = np.where(k <= q, 0.0, NEG).astype(f)
    m512 = np.concatenate([mA, mA, mB, mB], 1)
    sh["maskb"] = np.ascontiguousarray(np.concatenate([m512, m512], 1))
    C, Sg = _rot_tables(np.arange(S_TOK), 10000.0, 32)
    sh["cosk"] = np.ascontiguousarray(C.reshape(64, 128, 32).transpose(1, 0, 2).reshape(128, 64 * 32))
    sh["sink"] = np.ascontiguousarray(Sg.reshape(64, 128, 32).transpose(1, 0, 2).reshape(128, 64 * 32))
    return sh


def _prep_core(x, c):
    f = np.float32
    b, j = c // 4, c % 4
    t0 = OWN * j
    d = {}
    xpad = np.zeros((S_TOK + 2048, 1024), f)
    xpad[1024:1024 + S_TOK] = x[b]
    d["xe"] = np.ascontiguousarray(xpad[t0:t0 + EXT])
    d["xb"] = np.ascontiguousarray(x[b])
    pos_e = np.arange(EXT) + (t0 - 1024)
    Ce, Se = _rot_tables(pos_e, 500000.0, 16)
    ones = np.ones((EXT, 48), f)
    tabk = np.concatenate([ones, Ce, Ce, Se, Se], 1).T
    tabq = np.concatenate([ones, Ce, Se, Ce, Se], 1).T[:, 1024:1024 + OWN]
    d["tabk"] = np.ascontiguousarray(tabk)
    d["tabq"] = np.ascontiguousarray(tabq)
    Cb, Sb = _rot_tables(np.arange(OWN) + t0, 10000.0, 32)
    d["tabqb"] = np.ascontiguousarray(np.concatenate([np.ones((OWN, 64), f), Cb, Sb], 1).T)
    vm = np.zeros((128, NVB), f)
    for (g, r, i), n in VIDX.items():
        dd = DILS[g]
        mlo = 1024 // dd - 64
        e = (mlo + 128 * i + np.arange(128)) * dd + r
        t = e + t0 - 1024
        vm[:, n] = ((t >= 0) & (t < S_TOK)).astype(f)
    d["vmask"] = vm
    return d


def kernel(**inputs):
    x = np.asarray(inputs["x"], np.float32)
    sh = _prep_shared(inputs)
    nc = build_nc()
    in_maps = []
    for c in range(8):
        m = dict(sh)
        m.update(_prep_core(x, c))
        in_maps.append(m)
    res = run_bass_kernel_spmd(nc, in_maps, core_ids=list(range(8)))
    out = np.zeros((2, S_TOK, 1024), np.float32)
    for c in range(8):
        b, j = c // 4, c % 4
        out[b, OWN * j:OWN * (j + 1)] = np.asarray(res.results[c]["y"], np.float32)
    kernel.last_results = res
    return out
```

```python
import numpy as np
from contextlib import ExitStack
import concourse.bass as bass
import concourse.mybir as mybir
from concourse.bass_utils import run_bass_kernel_spmd

F32 = mybir.dt.float32
BF16 = mybir.dt.bfloat16
U8 = mybir.dt.uint8
AF = mybir.ActivationFunctionType
ALU = mybir.AluOpType

S_TOK = 8192
OWN = 2048
EXT = 4096
EPS = 1e-6
DILS = (1, 4, 16)
NEG = -30000.0
DEBUG = False


class Buf:
    __slots__ = ("w", "r", "name")

    def __init__(self, name=""):
        self.w = None
        self.r = []
        self.name = name


class Sched:
    ENGS = ("pe", "act", "dve", "pool", "sp")

    def __init__(self):
        self.ops = {e: [] for e in self.ENGS}
        self.seen = {e: {} for e in self.ENGS}
        self.dma_cnt = {}

    def _need(self, eng, tok, raw):
        if tok is None:
            return None
        if tok[0] == "eng":
            _, e, idx = tok
            if e == eng:
                if eng in ("pe", "sp") or not raw:
                    return None
            key = ("eng", e)
        else:
            _, s, idx = tok
            key = ("dma", s)
        if self.seen[eng].get(key, -1) >= idx:
            return None
        self.seen[eng][key] = idx
        return tok

    def add(self, eng, fn, reads=(), writes=(), dma=None):
        waits = []
        for b in reads:
            t = self._need(eng, b.w, True)
            if t:
                waits.append(t)
        for b in writes:
            t = self._need(eng, b.w, False)
            if t:
                waits.append(t)
            for rt in b.r:
                t = self._need(eng, rt, False)
                if t:
                    waits.append(t)
        idx = len(self.ops[eng])
        if dma is not None:
            n = self.dma_cnt.get(dma, 0) + 1
            self.dma_cnt[dma] = n
            tok = ("dma", dma, n)
        else:
            tok = ("eng", eng, idx)
        self.ops[eng].append({"fn": fn, "waits": waits, "dma": dma, "sig": False})
        for b in reads:
            key = (tok[0], tok[1])
            b.r = [t for t in b.r if (t[0], t[1]) != key] + [tok]
        for b in writes:
            b.w = tok
            b.r = []
        return tok

    def barrier(self):
        lasts = {}
        for e in self.ENGS:
            i = len(self.ops[e]) - 1
            while i >= 0 and (self.ops[e][i]["fn"] is None or self.ops[e][i]["dma"] is not None):
                i -= 1
            lasts[e] = i
        dm = dict(self.dma_cnt)
        for e in self.ENGS:
            waits = []
            for e2 in self.ENGS:
                if e2 != e and e2 != "sp" and lasts[e2] >= 0:
                    t = self._need(e, ("eng", e2, lasts[e2]), True)
                    if t:
                        waits.append(t)
            for s, n in dm.items():
                t = self._need(e, ("dma", s, n), True)
                if t:
                    waits.append(t)
            self.ops[e].append({"fn": None, "waits": waits, "dma": None, "sig": False})

    def emit(self, nc, stack):
        sems = {e: stack.enter_context(nc.semaphore("s_" + e)) for e in self.ENGS}
        dsems = {s: stack.enter_context(nc.semaphore("d_" + s)) for s in self.dma_cnt}
        for e in self.ENGS:
            for op in self.ops[e]:
                for t in op["waits"]:
                    if t[0] == "eng":
                        self.ops[t[1]][t[2]]["sig"] = True
        sigc = {}
        for e in self.ENGS:
            c = 0
            arr = []
            for op in self.ops[e]:
                if op["sig"]:
                    assert op["dma"] is None and op["fn"] is not None
                    c += 1
                arr.append(c)
            sigc[e] = arr
        block = stack.enter_context(nc.Block())

        def run(e, eng):
            for op in self.ops[e]:
                for t in op["waits"]:
                    if t[0] == "eng":
                        eng.wait_ge(sems[t[1]], sigc[t[1]][t[2]])
                    else:
                        eng.wait_ge(dsems[t[1]], 16 * t[2])
                if op["fn"] is None:
                    continue
                ins = op["fn"](eng)
                if op["dma"] is not None:
                    ins.then_inc(dsems[op["dma"]], 16)
                elif op["sig"]:
                    ins.then_inc(sems[e], 1)

        @block.tensor
        def _(eng):
            run("pe", eng)

        @block.scalar
        def _(eng):
            run("act", eng)

        @block.vector
        def _(eng):
            run("dve", eng)

        @block.gpsimd
        def _(eng):
            run("pool", eng)

        @block.sync
        def _(eng):
            run("sp", eng)


def vblocks():
    idx = {}
    n = 0
    goff = []
    for g, d in enumerate(DILS):
        goff.append(n)
        nb = 16 // d + 1
        for r in range(d):
            for i in range(nb):
                idx[(g, r, i)] = n
                n += 1
    return idx, goff, n


VIDX, VGOFF, NVB = vblocks()


def build_nc():
    nc = bass.Bass("TRN2", target_bir_lowering=False)

    def din(name, shape):
        return nc.dram_tensor(name, list(shape), F32, kind="ExternalInput").ap()

    xe = din("xe", [EXT, 1024])
    xb = din("xb", [S_TOK, 1024])
    vmask_d = din("vmask", [128, NVB])
    tabq_d = din("tabq", [112, OWN])
    tabk_d = din("tabk", [112, EXT])
    tabqb_d = din("tabqb", [128, OWN])
    cosk_d = din("cosk", [128, 64 * 32])
    sink_d = din("sink", [128, 64 * 32])
    ident_d = din("ident", [128, 128])
    maskb_d = din("maskb", [128, 1024])
    g1b_d = din("g1b", [128, 1024])
    g2b_d = din("g2b", [128, 1024])
    gqb_d = din("gqb", [128, 512])
    gkvb_d = din("gkvb", [128, 256])
    gp1_d = din("gp1", [128, 1024])
    gp3_d = din("gp3", [128, 1024])
    bg_d = din("bg", [128, 16])
    waq_d = din("waq", [6, 1024, 224])
    wak_d = din("wak", [6, 1024, 224])
    wav_d = din("wav", [6, 1024, 128])
    wkv_d = din("wkv", [1024, 320])
    wqc_d = din("wqc", [1024, 512])
    wg_d = din("wg", [1024, 2048])
    wuq_d = din("wuq", [512, 1024])
    wuk_d = din("wuk", [256, 512])
    wuv_d = din("wuv", [256, 512])
    woa_d = din("woa", [256, 1024])
    wob_d = din("wob", [512, 1024])
    wout_d = din("wout", [1024, 1024])
    w1_d = din("w1", [1024, 4096])
    w2_d = din("w2", [4096, 1024])
    y_d = nc.dram_tensor("y", [OWN, 1024], F32, kind="ExternalOutput").ap()
    hs_d = nc.dram_tensor("hscr", [OWN, 1024], F32, kind="Internal").ap()
    if DEBUG:
        dbgA_d = nc.dram_tensor("dbgA", [64, 4 * OWN], F32, kind="ExternalOutput").ap()
        dbgB_d = nc.dram_tensor("dbgB", [64, 8 * OWN], F32, kind="ExternalOutput").ap()

    S = Sched()
    with ExitStack() as st:
        ARENA = 204 * 1024
        arena = st.enter_context(nc.sbuf_tensor("arena", [128, ARENA], U8))
        pst = [st.enter_context(nc.psum_tensor("ps%d" % i, [128, 1024], F32)) for i in range(4)]
        PB = [Buf("psum%d" % i) for i in range(8)]

        def bank(i):
            return pst[i // 2][:, (i % 2) * 512:(i % 2) * 512 + 512]

        def bank16(i):
            return bank(i).bitcast(BF16)

        def pair(i):
            return pst[i][:, :]

        class Alloc:
            def __init__(self, base=0):
                self.off = base

            def take(self, nbytes):
                o = (self.off + 63) // 64 * 64
                self.off = o + nbytes
                assert self.off <= ARENA, ("arena overflow", self.off)
                self.lim_check()
                return o

            lim = None

            def lim_check(self):
                if self.lim is not None:
                    assert self.off <= self.lim, ("region overflow", self.off, self.lim)

            def t(self, shape, dt):
                sz = 4 if dt == F32 else 2
                n = int(np.prod(shape))
                o = self.take(n * sz)
                ap = arena[:, o:o + n * sz].bitcast(dt)
                if len(shape) == 2:
                    return ap.rearrange("p (a b) -> p a b", b=shape[1])
                if len(shape) == 3:
                    return ap.rearrange("p (a b c) -> p a b c", b=shape[1], c=shape[2])
                return ap

        A0 = Alloc(0)
        ident = A0.t([128], BF16)
        ones32 = A0.t([64], F32)
        epsb = A0.t([1], F32)
        maskb = A0.t([1024], BF16)
        g1b = A0.t([8, 128], F32)
        g2b = A0.t([8, 128], F32)
        gqb = A0.t([4, 128], F32)
        gkvb = A0.t([2, 128], F32)
        bg = A0.t([16], F32)
        stat = A0.t([16], F32)
        CONST_END = A0.off
        ATOP = ARENA - 4 * OWN * 2
        attnA = Alloc(ATOP).t([4, OWN], BF16)
        bC = Buf("consts")
        S.add("pool", lambda e: e.dma_start(out=ident, in_=ident_d), writes=[bC], dma="wc")
        S.add("pool", lambda e: e.dma_start(out=maskb, in_=maskb_d), writes=[bC], dma="wc")
        S.add("sp", lambda e: e.dma_start(out=g1b, in_=g1b_d.rearrange("p (a b) -> p a b", b=128)), writes=[bC], dma="c")
        S.add("sp", lambda e: e.dma_start(out=g2b, in_=g2b_d.rearrange("p (a b) -> p a b", b=128)), writes=[bC], dma="c")
        S.add("sp", lambda e: e.dma_start(out=gqb, in_=gqb_d.rearrange("p (a b) -> p a b", b=128)), writes=[bC], dma="c")
        S.add("sp", lambda e: e.dma_start(out=gkvb, in_=gkvb_d.rearrange("p (a b) -> p a b", b=128)), writes=[bC], dma="c")
        S.add("sp", lambda e: e.dma_start(out=bg, in_=bg_d), writes=[bC], dma="c")
        S.add("dve", lambda e: e.memset(ones32, 1.0), writes=[bC])
        S.add("dve", lambda e: e.memset(epsb, EPS), writes=[bC])

        class NormT:
            def __init__(self, A, width, psum_banks, scale_eng="pool"):
                self.width = width
                self.junk = A.t([width], BF16)
                self.stage = [A.t([width], BF16) for _ in range(2)]
                self.bst = [Buf("stage0"), Buf("stage1")]
                self.bjunk = Buf("junk")
                self.bstat = [Buf("stat%d" % i) for i in range(4)]
                self.k = 0
                self.banks = psum_banks
                self.scale_eng = scale_eng

            def run1(self, src, bsrc, C):
                k = self.k
                self.k += 1
                sc = stat[:, 2 * (k % 4):2 * (k % 4) + 2]
                bs = self.bstat[k % 4]
                stg = self.stage[k % 2][:, 0:C]
                bstg = self.bst[k % 2]
                S.add("act", lambda e: e.activation(out=self.junk[:, 0:C], in_=src, func=AF.Square, accum_out=sc[:, 0:1]),
                      reads=[bsrc], writes=[self.bjunk, bs])
                S.add("act", lambda e: e.activation(out=sc[:, 1:2], in_=sc[:, 0:1], func=AF.Sqrt, scale=1.0 / C, bias=epsb[:, 0:1]),
                      reads=[bs, bC], writes=[bs])
                S.add("dve", lambda e: e.reciprocal(out=sc[:, 1:2], in_=sc[:, 1:2]), reads=[bs], writes=[bs])
                S.add("dve", lambda e: e.tensor_scalar(out=stg, in0=src, scalar1=sc[:, 1:2], scalar2=None, op0=ALU.mult),
                      reads=[bs, bsrc], writes=[bstg])
                return (k, C)

            def run2(self, ctx, gain_b, dst, bdst):
                k, C = ctx
                stg = self.stage[k % 2][:, 0:C]
                bstg = self.bst[k % 2]
                bk = self.banks[k % len(self.banks)]
                pv = bank16(bk)[:, 0:C].rearrange("p (a b) -> p a b", b=128)
                for j in range(C // 128):
                    S.add("pe", lambda e, j=j: e.transpose(out=pv[:, j, :], in_=stg[:, j * 128:(j + 1) * 128], identity=ident),
                          reads=[bstg, bC], writes=[PB[bk]])
                S.add("dve", lambda e: e.tensor_tensor(out=dst, in0=pv, in1=gain_b, op=ALU.mult),
                      reads=[PB[bk], bC], writes=[bdst])

            def run(self, src, bsrc, C, gain_b, dst, bdst, src_psum=False):
                self.run2(self.run1(src, bsrc, C), gain_b, dst, bdst)

        A = Alloc(CONST_END)
        xeT = A.t([8, EXT], BF16)
        tabq = A.t([OWN], F32)
        tabk = A.t([EXT], F32)
        vmask = A.t([NVB], F32)
        PH_A_BASE = A.off
        xst = [A.t([1024], F32) for _ in range(3)]
        bxst = [Buf("xst%d" % i) for i in range(3)]
        NT = NormT(A, 1024, [0, 1])
        bxeT = [Buf("xeT%d" % i) for i in range(32)]

        S.add("sp", lambda e: e.dma_start(out=tabq[0:112, :], in_=tabq_d), writes=[bC], dma="c")
        S.add("sp", lambda e: e.dma_start(out=tabk[0:112, :], in_=tabk_d), writes=[bC], dma="c")
        S.add("sp", lambda e: e.dma_start(out=vmask, in_=vmask_d), writes=[bC], dma="c")
        ctxs = {}
        for st_ in range(33):
            if st_ < 32:
                T = st_
                sl = T % 3
                S.add("sp", lambda e, T=T, sl=sl, xst=xst: e.dma_start(out=xst[sl], in_=xe[T * 128:(T + 1) * 128, :]),
                      writes=[bxst[sl]], dma="x%d" % sl)
                ctxs[T] = NT.run1(xst[sl], bxst[sl], 1024)
            if st_ >= 1:
                T = st_ - 1
                NT.run2(ctxs[T], g1b, xeT[:, :, T * 128:(T + 1) * 128], bxeT[T])
        S.barrier()
        bXE = Buf("xeT_all")

        A = Alloc(PH_A_BASE)
        acc = A.t([2, OWN], F32)
        wq2 = [A.t([8, 224], BF16) for _ in range(2)]
        wk2 = [A.t([8, 224], BF16) for _ in range(2)]
        wv2 = [A.t([8, 128], BF16) for _ in range(2)]
        bwA2 = [Buf("wA0"), Buf("wA1")]
        Q2 = A.t([2, OWN], BF16)
        K2 = A.t([2, EXT], BF16)
        Vg = A.t([32, 2, 65], BF16)
        Pb = [A.t([1024], BF16) for _ in range(2)]
        rinv = A.t([512], F32)
        bacc, bwA, bQ2, bK2, bVg = Buf("acc"), Buf("wA"), Buf("Q2"), Buf("K2"), Buf("Vg")
        bP = [Buf("P0"), Buf("P1")]
        brinv = Buf("rinv")
        battnA = Buf("attnA")
        for hp in range(2):
            for g, d in enumerate(DILS):
                wi = hp * 3 + g
                wq, wk, wv, bwA = wq2[wi % 2], wk2[wi % 2], wv2[wi % 2], bwA2[wi % 2]

                def load_wA(wj):
                    q_, k_, v_, b_w = wq2[wj % 2], wk2[wj % 2], wv2[wj % 2], bwA2[wj % 2]
                    S.add("pool", lambda e: e.dma_start(out=q_, in_=waq_d[wj].rearrange("(k p) n -> p k n", p=128)),
                          writes=[b_w], dma="wa%d" % (wj % 2))
                    S.add("pool", lambda e: e.dma_start(out=k_, in_=wak_d[wj].rearrange("(k p) n -> p k n", p=128)),
                          writes=[b_w], dma="wa%d" % (wj % 2))
                    S.add("pool", lambda e: e.dma_start(out=v_, in_=wav_d[wj].rearrange("(k p) n -> p k n", p=128)),
                          writes=[b_w], dma="wa%d" % (wj % 2))

                if wi == 0:
                    load_wA(0)
                if wi + 1 < 6:
                    load_wA(wi + 1)
                cnt = 0
                for c in range(4):
                    for hh in range(2):
                        bk = cnt % 2
                        cnt += 1
                        for kc in range(8):
                            S.add("pe", lambda e, kc=kc, hh=hh, c=c, bk=bk, wq=wq: e.matmul(
                                bank(bk)[0:112, :], lhsT=wq[:, kc, hh * 112:(hh + 1) * 112],
                                rhs=xeT[:, kc, 1024 + 512 * c:1024 + 512 * (c + 1)], start=(kc == 0), stop=(kc == 7)),
                                reads=[bwA, bXE], writes=[PB[bk]])
                        S.add("dve", lambda e, hh=hh, c=c, bk=bk: e.tensor_tensor(
                            out=Q2[0:112, hh, 512 * c:512 * (c + 1)], in0=bank(bk)[0:112, :],
                            in1=tabq[0:112, 512 * c:512 * (c + 1)], op=ALU.mult),
                            reads=[PB[bk], bC], writes=[bQ2])
                mlo = 1024 // d - 64
                nb = 16 // d + 1
                elo = mlo * d
                ehi = (mlo + 128 * nb) * d
                c0, c1 = elo // 512, (ehi + 511) // 512
                for c in range(c0, c1):
                    for hh in range(2):
                        bk = cnt % 2
                        cnt += 1
                        for kc in range(8):
                            S.add("pe", lambda e, kc=kc, hh=hh, c=c, bk=bk, wk=wk: e.matmul(
                                bank(bk)[0:112, :], lhsT=wk[:, kc, hh * 112:(hh + 1) * 112],
                                rhs=xeT[:, kc, 512 * c:512 * (c + 1)], start=(kc == 0), stop=(kc == 7)),
                                reads=[bwA, bXE], writes=[PB[bk]])
                        S.add("dve", lambda e, hh=hh, c=c, bk=bk: e.tensor_tensor(
                            out=K2[0:112, hh, 512 * c:512 * (c + 1)], in0=bank(bk)[0:112, :],
                            in1=tabk[0:112, 512 * c:512 * (c + 1)], op=ALU.mult),
                            reads=[PB[bk], bC], writes=[bK2])
                nblk = d * nb
                blist = [(r, i) for r in range(d) for i in range(nb)]
                for q0 in range(0, nblk, 4):
                    grp = blist[q0:q0 + 4]
                    bk = cnt % 2
                    cnt += 1
                    for jj, (r, i) in enumerate(grp):
                        e0 = (mlo + 128 * i) * d + r
                        for kc in range(8):
                            S.add("pe", lambda e, kc=kc, jj=jj, e0=e0, bk=bk, d=d, wv=wv: e.matmul(
                                bank(bk)[:, jj * 128:(jj + 1) * 128], lhsT=xeT[:, kc, e0:e0 + 127 * d + 1:d],
                                rhs=wv[:, kc, :], start=(kc == 0), stop=(kc == 7)),
                                reads=[bwA, bXE], writes=[PB[bk]])
                    n = len(grp)
                    S.add("act", lambda e, q0=q0, n=n, bk=bk: e.activation(
                        out=Vg[:, q0:q0 + n, :, 0:64],
                        in_=bank(bk)[:, 0:n * 128].rearrange("p (a b c) -> p a b c", b=2, c=64), func=AF.Copy),
                        reads=[PB[bk]], writes=[bVg])
                for hh in range(2):
                    S.add("dve", lambda e, hh=hh, nblk=nblk, g=g: e.tensor_copy(
                        out=Vg[:, 0:nblk, hh, 64], in_=vmask[:, VGOFF[g]:VGOFF[g] + nblk]),
                        reads=[bC], writes=[bVg])
                qbs = [(r, j) for r in range(d) for j in range(16 // d)]
                for ui in range(8):
                    sp_ = 1 + (ui % 2)
                    ob = 6 + (ui % 2)
                    Pt = Pb[ui % 2]
                    bPt = bP[ui % 2]
                    bS = [PB[2 * sp_], PB[2 * sp_ + 1]]
                    for u in range(2):
                        S.add("pe", lambda e, sp_=sp_, u=u: e.matmul(
                            pair(sp_)[:, u * 512:(u + 1) * 512], lhsT=ident, rhs=maskb[:, 0:512], start=True, stop=False),
                            reads=[bC], writes=[bS[u]])
                    for u in range(2):
                        r, j = qbs[2 * ui + u]
                        qs = 128 * j * d + r
                        for ab in range(2):
                            ks = (mlo + 128 * (j + ab)) * d + r
                            for hh in range(2):
                                off = u * 512 + ab * 256 + hh * 128
                                last = (ab == 1 and hh == 1)
                                S.add("pe", lambda e, sp_=sp_, off=off, hh=hh, ks=ks, qs=qs, last=last, d=d: e.matmul(
                                    pair(sp_)[:, off:off + 128], lhsT=K2[0:112, hh, ks:ks + 127 * d + 1:d],
                                    rhs=Q2[0:112, hh, qs:qs + 127 * d + 1:d], start=False, stop=last),
                                    reads=[bK2, bQ2], writes=[bS[u]])
                    S.add("act", lambda e, sp_=sp_, Pt=Pt: e.activation(out=Pt, in_=pair(sp_), func=AF.Exp, scale=0.125),
                          reads=bS, writes=[bPt])
                    for u in range(2):
                        r, j = qbs[2 * ui + u]
                        for hh in range(2):
                            for ab in range(2):
                                lb = r * nb + j + ab
                                off = u * 512 + ab * 256 + hh * 128
                                S.add("pe", lambda e, ob=ob, u=u, hh=hh, lb=lb, off=off, ab=ab, Pt=Pt: e.matmul(
                                    bank(ob)[0:65, (u * 2 + hh) * 128:(u * 2 + hh + 1) * 128], lhsT=Vg[:, lb, hh, :],
                                    rhs=Pt[:, off:off + 128], start=(ab == 0), stop=(ab == 1)),
                                    reads=[bVg, bPt], writes=[PB[ob]])
                    for u in range(2):
                        r, j = qbs[2 * ui + u]
                        qs = 128 * j * d + r
                        src = bank(ob)[0:65, u * 256:(u + 1) * 256].rearrange("p (a b) -> p a b", b=128)
                        dstv = acc[0:65, :, qs:qs + 127 * d + 1:d]
                        if g == 0:
                            S.add("dve", lambda e, src=src, dstv=dstv: e.tensor_copy(out=dstv, in_=src),
                                  reads=[PB[ob]], writes=[bacc])
                        else:
                            S.add("dve", lambda e, src=src, dstv=dstv: e.tensor_tensor(out=dstv, in0=src, in1=dstv, op=ALU.add),
                                  reads=[PB[ob], bacc], writes=[bacc])
            for c in range(4):
                for hh in range(2):
                    S.add("pe", lambda e, hh=hh, c=c: e.matmul(
                        bank(0)[0:64, :], lhsT=ones32[64:65, 0:64], rhs=acc[64:65, hh, 512 * c:512 * (c + 1)],
                        start=True, stop=True), reads=[bacc, bC], writes=[PB[0]])
                    S.add("dve", lambda e: e.reciprocal(out=rinv[0:64, :], in_=bank(0)[0:64, :]),
                          reads=[PB[0]], writes=[brinv])
                    S.add("dve", lambda e, hh=hh, c=c, hp=hp: e.tensor_tensor(
                        out=attnA[0:64, 2 * hp + hh, 512 * c:512 * (c + 1)], in0=acc[0:64, hh, 512 * c:512 * (c + 1)],
                        in1=rinv[0:64, :], op=ALU.mult), reads=[bacc, brinv], writes=[battnA])
        S.barrier()

        A = Alloc(CONST_END)
        kvnT = A.t([2, S_TOK], BF16)
        qcnT = A.t([4, OWN], BF16)
        wuq = A.t([4, 1024], BF16)
        wuk = A.t([2, 512], BF16)
        wuv = A.t([2, 512], BF16)
        tabqb = A.t([OWN], F32)
        KT_OFF = A.take(0)
        KT = [A.t([S_TOK], BF16) for _ in range(2)]
        KT1_OFF = KT_OFF + S_TOK * 2
        U0 = A.take(0)
        VT = [A.t([64, 65], BF16) for _ in range(2)]
        QT = [A.t([OWN], BF16) for _ in range(2)]
        U1 = A.off
        PBm = [A.t([1024], BF16) for _ in range(3)]
        lsb = A.t([512], F32)
        rinvB = A.t([512], F32)
        ATT_B_OFF = A.take(0)
        attnB = A.t([8, OWN], BF16)
        assert A.off <= ATOP, (A.off, ATOP)
        AB = Alloc(ATT_B_OFF)
        AB.lim = A.off
        xst = [AB.t([1024], F32) for _ in range(2)]
        NT = NormT(AB, 1024, [0, 1])
        wkv = AB.t([8, 320], BF16)
        xsT = [AB.t([8, 128], BF16) for _ in range(2)]
        kstg = [AB.t([128], BF16) for _ in range(2)]
        t1 = [AB.t([32], F32) for _ in range(2)]
        t2 = [AB.t([32], F32) for _ in range(2)]
        NT2 = NormT(AB, 256, [4, 5])
        NT3 = NormT(AB, 512, [4, 5])
        AC = Alloc(U0)
        AC.lim = U1
        cosk = AC.t([64, 32], F32)
        sink = AC.t([64, 32], F32)
        wqc = AC.t([8, 512], BF16)
        bxst = [Buf("xst%d" % i) for i in range(2)]
        bwkv, bxsT = Buf("wkv"), [Buf("xsT0"), Buf("xsT1")]
        bkvn, bKT, bVT, bQT = Buf("kvnT"), [Buf("KT0"), Buf("KT1")], [Buf("VT0"), Buf("VT1")], [Buf("QT0"), Buf("QT1")]
        bkstg, bt = [Buf("kstg0"), Buf("kstg1")], [Buf("t12a"), Buf("t12b")]
        bqcn, bwu, battnB, btab = Buf("qcnT"), Buf("wu"), Buf("attnB"), Buf("tabqb")

        S.add("pool", lambda e: e.dma_start(out=wkv, in_=wkv_d.rearrange("(k p) n -> p k n", p=128)), writes=[bwkv], dma="wb")
        S.add("pool", lambda e: e.dma_start(out=wqc, in_=wqc_d.rearrange("(k p) n -> p k n", p=128)), writes=[bwkv], dma="wb")
        S.add("pool", lambda e: e.dma_start(out=wuq, in_=wuq_d.rearrange("(k p) n -> p k n", p=128)), writes=[bwu], dma="wb")
        S.add("pool", lambda e: e.dma_start(out=wuk, in_=wuk_d.rearrange("(k p) n -> p k n", p=128)), writes=[bwu], dma="wb")
        S.add("pool", lambda e: e.dma_start(out=wuv, in_=wuv_d.rearrange("(k p) n -> p k n", p=128)), writes=[bwu], dma="wb")
        S.add("sp", lambda e: e.dma_start(out=cosk, in_=cosk_d.rearrange("p (a b) -> p a b", b=32)), writes=[bC], dma="c")
        S.add("sp", lambda e: e.dma_start(out=sink, in_=sink_d.rearrange("p (a b) -> p a b", b=32)), writes=[bC], dma="c")
        S.add("sp", lambda e: e.dma_start(out=tabqb, in_=tabqb_d), writes=[btab], dma="c")
        for i_ in range(2):
            S.add("dve", lambda e, i_=i_: e.memset(kstg[i_], 0.0), writes=[bkstg[i_]])

        cA, cB = {}, {}

        def p1_A1(T, src_d, xst=xst):
            sl = T % 2
            S.add("sp", lambda e: e.dma_start(out=xst[sl], in_=src_d), writes=[bxst[sl]], dma="x%d" % sl)
            cA[T] = NT.run1(xst[sl], bxst[sl], 1024)

        def p1_A2(T, NT=NT):
            NT.run2(cA[T], g1b, xsT[T % 2], bxsT[T % 2])

        def p1_B(T):
            xs_, bxs_, pk, i_ = xsT[T % 2], bxsT[T % 2], 2 + (T % 2), T % 2
            for kc in range(8):
                S.add("pe", lambda e, kc=kc: e.matmul(
                    bank(pk)[:, 0:320], lhsT=xs_[:, kc, :], rhs=wkv[:, kc, :], start=(kc == 0), stop=(kc == 7)),
                    reads=[bxs_, bwkv], writes=[PB[pk]])
            cB[T] = NT2.run1(bank(pk)[:, 0:256], PB[pk], 256)
            S.add("dve", lambda e: e.tensor_tensor(out=t1[i_], in0=bank(pk)[:, 256:288], in1=cosk[:, T, :], op=ALU.mult),
                  reads=[PB[pk], bC], writes=[bt[i_]])
            S.add("dve", lambda e: e.tensor_tensor(out=t2[i_], in0=bank(pk)[:, 288:320], in1=sink[:, T, :], op=ALU.mult),
                  reads=[PB[pk], bC], writes=[bt[i_]])
            S.add("dve", lambda e: e.tensor_tensor(out=kstg[i_][:, 64:96], in0=t1[i_], in1=t2[i_], op=ALU.add),
                  reads=[bt[i_]], writes=[bkstg[i_]])
            S.add("dve", lambda e: e.tensor_tensor(out=kstg[i_][:, 96:128], in0=t1[i_], in1=t2[i_], op=ALU.add),
                  reads=[bt[i_]], writes=[bkstg[i_]])

        def p1_C(T):
            i_, p6 = T % 2, 6 + (T % 2)
            NT2.run2(cB[T], gkvb, kvnT[:, :, T * 128:(T + 1) * 128], bkvn)
            S.add("pe", lambda e: e.transpose(out=bank16(p6)[:, 0:128], in_=kstg[i_], identity=ident),
                  reads=[bkstg[i_], bC], writes=[PB[p6]])
            for b_ in range(2):
                S.add("act", lambda e, b_=b_: e.activation(
                    out=KT[b_][64:128, T * 128:(T + 1) * 128], in_=bank16(p6)[64:128, 0:128], func=AF.Copy),
                    reads=[PB[p6]], writes=[bKT[b_]])

        for st_ in range(64 + 3):
            if st_ < 64:
                p1_A1(st_, xb[st_ * 128:(st_ + 1) * 128, :])
            if 0 <= st_ - 1 < 64:
                p1_A2(st_ - 1)
            if 0 <= st_ - 2 < 64:
                p1_B(st_ - 2)
            if 0 <= st_ - 3 < 64:
                p1_C(st_ - 3)

        cQ = {}

        def p3_B(T):
            xs_, bxs_, pk = xsT[T % 2], bxsT[T % 2], 2 + (T % 2)
            for kc in range(8):
                S.add("pe", lambda e, kc=kc: e.matmul(
                    bank(pk), lhsT=xs_[:, kc, :], rhs=wqc[:, kc, :], start=(kc == 0), stop=(kc == 7)),
                    reads=[bxs_, bwkv], writes=[PB[pk]])
            cQ[T] = NT3.run1(bank(pk), PB[pk], 512)

        def p3_C(T):
            NT3.run2(cQ[T], gqb, qcnT[:, :, T * 128:(T + 1) * 128], bqcn)

        for st_ in range(16 + 3):
            if st_ < 16:
                p1_A1(100 + st_, xe[1024 + st_ * 128:1024 + (st_ + 1) * 128, :])
            if 0 <= st_ - 1 < 16:
                p1_A2(100 + st_ - 1)
            if 0 <= st_ - 2 < 16:
                p3_B(st_ - 2)
            if 0 <= st_ - 3 < 16:
                p3_C(st_ - 3)
        S.barrier()
        for b_ in range(2):
            S.add("dve", lambda e, b_=b_: e.memset(VT[b_][:, :, 64:65], 1.0), writes=[bVT[b_]])

        SC_B = 96 ** -0.5

        def build_head(h):
            b_ = h % 2
            cnt = 0
            for c in range(16):
                bk = 6 + (cnt % 2)
                cnt += 1
                for kc in range(2):
                    S.add("pe", lambda e, kc=kc, c=c, bk=bk: e.matmul(
                        bank(bk)[0:64, :], lhsT=wuk[:, kc, h * 64:(h + 1) * 64], rhs=kvnT[:, kc, 512 * c:512 * (c + 1)],
                        start=(kc == 0), stop=(kc == 1)), reads=[bwu, bkvn], writes=[PB[bk]])
                S.add("dve", lambda e, c=c, bk=bk: e.tensor_copy(out=KT[b_][0:64, 512 * c:512 * (c + 1)], in_=bank(bk)[0:64, :]),
                      reads=[PB[bk]], writes=[bKT[b_]])
            for k8 in range(8):
                bk = 6 + (cnt % 2)
                cnt += 1
                for j in range(8):
                    kb = 8 * k8 + j
                    for kc in range(2):
                        S.add("pe", lambda e, kc=kc, kb=kb, j=j, bk=bk: e.matmul(
                            bank(bk)[:, j * 64:(j + 1) * 64], lhsT=kvnT[:, kc, kb * 128:(kb + 1) * 128],
                            rhs=wuv[:, kc, h * 64:(h + 1) * 64], start=(kc == 0), stop=(kc == 1)),
                            reads=[bwu, bkvn], writes=[PB[bk]])
                S.add("dve", lambda e, k8=k8, bk=bk: e.tensor_copy(
                    out=VT[b_][:, 8 * k8:8 * k8 + 8, 0:64], in_=bank(bk).rearrange("p (a b) -> p a b", b=64)),
                    reads=[PB[bk]], writes=[bVT[b_]])
            for c in range(4):
                bk = 6 + (cnt % 2)
                cnt += 1
                for kc in range(4):
                    S.add("pe", lambda e, kc=kc, c=c, bk=bk: e.matmul(
                        bank(bk), lhsT=wuq[:, kc, h * 128:(h + 1) * 128], rhs=qcnT[:, kc, 512 * c:512 * (c + 1)],
                        start=(kc == 0), stop=(kc == 3)), reads=[bwu, bqcn], writes=[PB[bk]])
                S.add("dve", lambda e, c=c, bk=bk: e.tensor_tensor(
                    out=QT[b_][:, 512 * c:512 * (c + 1)], in0=bank(bk), in1=tabqb[:, 512 * c:512 * (c + 1)], op=ALU.mult),
                    reads=[PB[bk], btab], writes=[bQT[b_]])

        bPm = [Buf("Pm%d" % i) for i in range(3)]
        blsb, brB = Buf("lsb"), Buf("rinvB")
        it_ctr = [0]

        def attn_chunk(h, qc):
            b_ = h % 2
            ob = 6 + ((h * 4 + qc) % 2)
            q_ap = QT[b_][:, 512 * qc:512 * (qc + 1)]

            def qk(kp):
                it = it_ctr[0] + kp
                sp_ = it % 3
                for u in range(2):
                    kb = 2 * kp + u
                    S.add("pe", lambda e, sp_=sp_, u=u, kb=kb: e.matmul(
                        pair(sp_)[:, u * 512:(u + 1) * 512], lhsT=KT[b_][:, kb * 128:(kb + 1) * 128], rhs=q_ap,
                        start=True, stop=True), reads=[bKT[b_], bQT[b_]], writes=[PB[2 * sp_ + u]])

            def ex(kp):
                it = it_ctr[0] + kp
                sp_ = it % 3
                pm = it % 3
                S.add("act", lambda e, sp_=sp_, pm=pm: e.activation(out=PBm[pm], in_=pair(sp_), func=AF.Exp, scale=SC_B),
                      reads=[PB[2 * sp_], PB[2 * sp_ + 1]], writes=[bPm[pm]])

            def pv(kp):
                it = it_ctr[0] + kp
                pm = it % 3
                for u in range(2):
                    kb = 2 * kp + u
                    S.add("pe", lambda e, pm=pm, u=u, kb=kb: e.matmul(
                        bank(ob)[0:65, :], lhsT=VT[b_][:, kb, :], rhs=PBm[pm][:, u * 512:(u + 1) * 512],
                        start=(kb == 0), stop=(kb == 63)), reads=[bVT[b_], bPm[pm]], writes=[PB[ob]])

            qk(0)
            qk(1)
            for kp in range(32):
                ex(kp)
                if kp + 2 < 32:
                    qk(kp + 2)
                pv(kp)
            it_ctr[0] += 32
            S.add("act", lambda e: e.activation(out=lsb[64:65, :], in_=bank(ob)[64:65, :], func=AF.Copy),
                  reads=[PB[ob]], writes=[blsb])
            nb_ = 0 + 2 * (it_ctr[0] % 3)
            S.add("pe", lambda e, nb_=nb_: e.matmul(bank(nb_)[0:64, :], lhsT=ones32[64:65, 0:64], rhs=lsb[64:65, :],
                                                    start=True, stop=True), reads=[blsb, bC], writes=[PB[nb_]])
            S.add("dve", lambda e, nb_=nb_: e.reciprocal(out=rinvB[0:64, :], in_=bank(nb_)[0:64, :]),
                  reads=[PB[nb_]], writes=[brB])
            S.add("dve", lambda e: e.tensor_tensor(out=attnB[0:64, h, 512 * qc:512 * (qc + 1)], in0=bank(ob)[0:64, :],
                                                   in1=rinvB[0:64, :], op=ALU.mult),
                  reads=[PB[ob], brB], writes=[battnB])


        AC_ = Alloc(CONST_END)
        wg = AC_.t([8, 2048], BF16)
        wob = AC_.t([8, 1024], BF16)
        woa = AC_.t([4, 1024], BF16)
        xsTc = AC_.t([8, 512], BF16)
        W1_END = AC_.off
        assert W1_END - CONST_END >= 8 * 4096 * 2
        wout = AC_.t([8, 1024], BF16)
        assert AC_.off <= KT1_OFF, (AC_.off, KT1_OFF)
        bwg, bwob, bwoa, bwout, bxsTc = Buf("wg"), Buf("wob"), Buf("woa"), Buf("wout"), Buf("xsTc")

        def prefetch_mix_weights():
            for kq in range(4):
                S.add("pool", lambda e, kq=kq: e.dma_start(
                    out=wg[:, 2 * kq:2 * kq + 2, :], in_=wg_d[256 * kq:256 * (kq + 1), :].rearrange("(k p) n -> p k n", p=128)),
                    writes=[bwg, bkvn], dma="wm")
            S.add("pool", lambda e: e.dma_start(out=wob[0:64], in_=wob_d.rearrange("(k p) n -> p k n", p=64)),
                  writes=[bwob, bqcn], dma="wm")
            S.add("pool", lambda e: e.dma_start(out=woa[0:64], in_=woa_d.rearrange("(k p) n -> p k n", p=64)),
                  writes=[bwoa, bwu], dma="wm")
            S.add("pool", lambda e: e.dma_start(out=wout, in_=wout_d.rearrange("(k p) n -> p k n", p=128)),
                  writes=[bwout, bwu, btab, bKT[0]], dma="wm")

        build_head(0)
        for h in range(8):
            for qc in range(3):
                attn_chunk(h, qc)
            if h + 1 < 8:
                build_head(h + 1)
            attn_chunk(h, 3)
        S.barrier()
        prefetch_mix_weights()

        A = AC_
        A.lim = ATT_B_OFF
        gp1 = A.t([1024], F32)
        xres = [A.t([1024], F32) for _ in range(4)]
        xr2 = [A.t([1024], F32) for _ in range(2)]
        ytmp = A.t([1024], F32)
        NT = NormT(A, 1024, [0, 1])
        uT = A.t([8, 512], BF16)
        ga = A.t([512], F32)
        gb = A.t([512], F32)
        tm1 = A.t([512], F32)
        tm2 = A.t([512], F32)
        bxres, buT = [Buf("xres%d" % i) for i in range(4)], Buf("uT")
        bxr2, bytmp = [Buf("xr2_0"), Buf("xr2_1")], Buf("ytmp")
        bga, bgb, btm1, btm2 = Buf("ga"), Buf("gb"), Buf("tm1"), Buf("tm2")
        bst2 = Buf("stat2")
        S.add("sp", lambda e: e.dma_start(out=gp1, in_=gp1_d), writes=[bC], dma="c")

        def normC(c, NT=NT):
            cc = {}
            for st_ in range(5):
                if st_ < 4:
                    i = st_
                    S.add("sp", lambda e, i=i: e.dma_start(
                        out=xres[i], in_=xe[1024 + 512 * c + 128 * i:1024 + 512 * c + 128 * (i + 1), :]),
                        writes=[bxres[i]], dma="xr%d" % i)
                    cc[i] = NT.run1(xres[i], bxres[i], 1024)
                if st_ >= 1:
                    i = st_ - 1
                    NT.run2(cc[i], g1b, xsTc[:, :, 128 * i:128 * (i + 1)], bxsTc)

        def gatesC(c):
            for ft in range(8):
                for hh in range(4):
                    S.add("pe", lambda e, hh=hh, ft=ft: e.matmul(
                        bank(2), lhsT=woa[0:64, hh, ft * 128:(ft + 1) * 128], rhs=attnA[0:64, hh, 512 * c:512 * (c + 1)],
                        start=(hh == 0), stop=(hh == 3)), reads=[bwoa, battnA], writes=[PB[2]])
                for h in range(8):
                    S.add("pe", lambda e, h=h, ft=ft: e.matmul(
                        bank(3), lhsT=wob[0:64, h, ft * 128:(ft + 1) * 128], rhs=attnB[0:64, h, 512 * c:512 * (c + 1)],
                        start=(h == 0), stop=(h == 7)), reads=[bwob, battnB], writes=[PB[3]])
                for kc in range(8):
                    S.add("pe", lambda e, kc=kc, ft=ft: e.matmul(
                        bank(4), lhsT=wg[:, kc, ft * 128:(ft + 1) * 128], rhs=xsTc[:, kc, :],
                        start=(kc == 0), stop=(kc == 7)), reads=[bwg, bxsTc], writes=[PB[4]])
                for kc in range(8):
                    S.add("pe", lambda e, kc=kc, ft=ft: e.matmul(
                        bank(5), lhsT=wg[:, kc, 1024 + ft * 128:1024 + (ft + 1) * 128], rhs=xsTc[:, kc, :],
                        start=(kc == 0), stop=(kc == 7)), reads=[bwg, bxsTc], writes=[PB[5]])
                S.add("act", lambda e, ft=ft: e.activation(out=ga, in_=bank(4), func=AF.Sigmoid, bias=bg[:, ft:ft + 1]),
                      reads=[PB[4], bC], writes=[bga])
                S.add("act", lambda e, ft=ft: e.activation(out=gb, in_=bank(5), func=AF.Sigmoid, bias=bg[:, 8 + ft:9 + ft]),
                      reads=[PB[5], bC], writes=[bgb])
                S.add("dve", lambda e: e.tensor_tensor(out=tm1, in0=bank(2), in1=ga, op=ALU.mult),
                      reads=[PB[2], bga], writes=[btm1])
                S.add("dve", lambda e: e.tensor_tensor(out=tm2, in0=bank(3), in1=gb, op=ALU.mult),
                      reads=[PB[3], bgb], writes=[btm2])
                S.add("dve", lambda e, ft=ft: e.tensor_tensor(out=uT[:, ft, :], in0=tm1, in1=tm2, op=ALU.add),
                      reads=[btm1, btm2], writes=[buT])

        def tokC(c, NT=NT):
            sc = stat[:, 8:10]
            for i in range(4):
                xr = xr2[i % 2]
                bxr = bxr2[i % 2]
                S.add("sp", lambda e, i=i, xr=xr: e.dma_start(
                    out=xr, in_=xe[1024 + 512 * c + 128 * i:1024 + 512 * c + 128 * (i + 1), :]),
                    writes=[bxr], dma="xq%d" % (i % 2))
                for half in range(2):
                    for ft in range(8):
                        S.add("pe", lambda e, i=i, half=half, ft=ft: e.matmul(
                            pair(3)[:, half * 512:(half + 1) * 512], lhsT=uT[:, ft, 128 * i:128 * (i + 1)],
                            rhs=wout[:, ft, half * 512:(half + 1) * 512], start=(ft == 0), stop=(ft == 7)),
                            reads=[buT, bwout], writes=[PB[6 + half]])
                S.add("act", lambda e: e.activation(out=NT.junk, in_=pair(3), func=AF.Square, accum_out=sc[:, 0:1]),
                      reads=[PB[6], PB[7]], writes=[NT.bjunk, bst2])
                S.add("act", lambda e: e.activation(out=sc[:, 1:2], in_=sc[:, 0:1], func=AF.Sqrt, scale=1.0 / 1024, bias=epsb[:, 0:1]),
                      reads=[bst2, bC], writes=[bst2])
                S.add("dve", lambda e: e.reciprocal(out=sc[:, 1:2], in_=sc[:, 1:2]), reads=[bst2], writes=[bst2])
                S.add("dve", lambda e: e.scalar_tensor_tensor(out=ytmp, in0=pair(3), scalar=sc[:, 1:2], in1=gp1,
                                                              op0=ALU.mult, op1=ALU.mult),
                      reads=[PB[6], PB[7], bst2, bC], writes=[bytmp])
                S.add("dve", lambda e, xr=xr: e.tensor_tensor(out=xr, in0=ytmp, in1=xr, op=ALU.add),
                      reads=[bytmp, bxr], writes=[bxr])
                S.add("sp", lambda e, xr=xr, i=i: e.dma_start(
                    out=hs_d[512 * c + 128 * i:512 * c + 128 * (i + 1), :], in_=xr), reads=[bxr], dma="hst%d" % (i % 2))

        AD_ = Alloc(CONST_END)
        w1 = AD_.t([8, 4096], BF16)
        assert AD_.off <= W1_END
        bw1, bw2 = Buf("w1"), Buf("w2")

        def prefetch_w1():
            for kc in range(8):
                S.add("pool", lambda e, kc=kc: e.dma_start(out=w1[:, kc, :], in_=w1_d[128 * kc:128 * (kc + 1), :]),
                      writes=[bw1, bwg, bwob, bwoa, bxsTc], dma="w1")

        for c in range(4):
            normC(c)
            gatesC(c)
            tokC(c)
        S.barrier()
        prefetch_w1()

        A = AD_
        A.lim = ARENA
        w2 = A.t([32, 1024], BF16)
        gp3 = A.t([1024], F32)
        hres = [A.t([1024], F32) for _ in range(4)]
        NT = NormT(A, 1024, [0, 1])
        hsT = [A.t([8, 256], BF16) for _ in range(2)]
        aT = A.t([32, 256], BF16)
        rl = [A.t([256], F32) for _ in range(4)]
        otile = A.t([1024], F32)
        bhres, bhsT, baT = [Buf("hres%d" % i) for i in range(4)], [Buf("hsT0"), Buf("hsT1")], Buf("aT")
        brl, bot, bst3 = [Buf("rl%d" % i) for i in range(4)], Buf("ot"), Buf("stat3")
        for k4 in range(8):
            S.add("pool", lambda e, k4=k4: e.dma_start(
                out=w2[:, 4 * k4:4 * k4 + 4, :], in_=w2_d[512 * k4:512 * (k4 + 1), :].rearrange("(k p) n -> p k n", p=128)),
                writes=[bw2], dma="w2")
        S.add("sp", lambda e: e.dma_start(out=gp3, in_=gp3_d), writes=[bC], dma="c")

        def normD(c, NT=NT):
            hp_ = c % 2
            cc = {}
            for st_ in range(3):
                if st_ < 2:
                    i = st_
                    hi = 2 * hp_ + i
                    S.add("sp", lambda e, i=i, hi=hi: e.dma_start(
                        out=hres[hi], in_=hs_d[256 * c + 128 * i:256 * c + 128 * (i + 1), :]),
                        writes=[bhres[hi]], dma="hr%d" % hi)
                    cc[i] = NT.run1(hres[hi], bhres[hi], 1024)
                if st_ >= 1:
                    i = st_ - 1
                    NT.run2(cc[i], g2b, hsT[hp_][:, :, 128 * i:128 * (i + 1)], bhsT[hp_])

        def ff1(c):
            hp_ = c % 2
            for m in range(32):
                bk = 2 + (m % 4)
                for kc in range(8):
                    S.add("pe", lambda e, kc=kc, m=m, bk=bk: e.matmul(
                        bank(bk)[:, 0:256], lhsT=w1[:, kc, m * 128:(m + 1) * 128], rhs=hsT[hp_][:, kc, :],
                        start=(kc == 0), stop=(kc == 7)), reads=[bw1, bhsT[hp_]], writes=[PB[bk]])
                r_ = rl[m % 4]
                S.add("act", lambda e, bk=bk, r_=r_: e.activation(out=r_, in_=bank(bk)[:, 0:256], func=AF.Relu),
                      reads=[PB[bk]], writes=[brl[m % 4]])
                S.add("dve", lambda e, m=m, r_=r_: e.tensor_tensor(out=aT[:, m, :], in0=r_, in1=r_, op=ALU.mult),
                      reads=[brl[m % 4]], writes=[baT])

        def ff2(c, NT=NT):
            hp_ = c % 2
            sc = stat[:, 8:10]
            for i in range(2):
                hi = 2 * hp_ + i
                for half in range(2):
                    for m in range(32):
                        S.add("pe", lambda e, i=i, half=half, m=m: e.matmul(
                            pair(3)[:, half * 512:(half + 1) * 512], lhsT=aT[:, m, 128 * i:128 * (i + 1)],
                            rhs=w2[:, m, half * 512:(half + 1) * 512], start=(m == 0), stop=(m == 31)),
                            reads=[baT, bw2], writes=[PB[6 + half]])
                S.add("act", lambda e: e.activation(out=NT.junk, in_=pair(3), func=AF.Square, accum_out=sc[:, 0:1]),
                      reads=[PB[6], PB[7]], writes=[NT.bjunk, bst3])
                S.add("act", lambda e: e.activation(out=sc[:, 1:2], in_=sc[:, 0:1], func=AF.Sqrt, scale=1.0 / 1024, bias=epsb[:, 0:1]),
                      reads=[bst3, bC], writes=[bst3])
                S.add("dve", lambda e: e.reciprocal(out=sc[:, 1:2], in_=sc[:, 1:2]), reads=[bst3], writes=[bst3])
                S.add("dve", lambda e: e.scalar_tensor_tensor(out=otile, in0=pair(3), scalar=sc[:, 1:2], in1=gp3,
                                                              op0=ALU.mult, op1=ALU.mult),
                      reads=[PB[6], PB[7], bst3, bC], writes=[bot])
                S.add("dve", lambda e, hi=hi: e.tensor_tensor(out=hres[hi], in0=otile, in1=hres[hi], op=ALU.add),
                      reads=[bot, bhres[hi]], writes=[bhres[hi]])
                S.add("sp", lambda e, hi=hi, i=i: e.dma_start(
                    out=y_d[256 * c + 128 * i:256 * c + 128 * (i + 1), :], in_=hres[hi]), reads=[bhres[hi]], dma="yst%d" % hi)

        normD(0)
        for c in range(8):
            ff1(c)
            if c + 1 < 8:
                normD(c + 1)
            ff2(c)
        S.barrier()
        S.emit(nc, st)
    return nc


def _rot_tables(pos, theta, rot_dim):
    half = rot_dim // 2
    inv = np.float32(theta) ** (-(np.arange(half, dtype=np.float32) * np.float32(2.0) / np.float32(rot_dim)))
    ang = pos.astype(np.float32)[:, None] * inv.astype(np.float32)[None, :]
    c = np.cos(ang).astype(np.float32)
    s = np.sin(ang).astype(np.float32)
    C = np.concatenate([c, c], 1)
    Sg = np.concatenate([-s, s], 1)
    return C, Sg


def _prep_shared(inp):
    f = np.float32
    w_in = np.asarray(inp["w_in"], f)
    sh = {}
    waq = np.zeros((6, 1024, 224), f)
    wak = np.zeros((6, 1024, 224), f)
    wav = np.zeros((6, 1024, 128), f)
    sw = np.concatenate([np.arange(8, 16), np.arange(0, 8)])
    for hp in range(2):
        for g in range(3):
            wi = hp * 3 + g
            for hh in range(2):
                h = 4 * g + 2 * hp + hh
                qb = h * 64
                kb = 768 + h * 64
                vb = 1536 + h * 64
                nope = np.arange(16, 64)
                rot = np.arange(0, 16)
                qcols = np.concatenate([qb + nope, qb + rot, qb + sw, qb + rot, qb + sw])
                kcols = np.concatenate([kb + nope, kb + rot, kb + rot, kb + sw, kb + sw])
                waq[wi][:, hh * 112:(hh + 1) * 112] = w_in[:, qcols]
                wak[wi][:, hh * 112:(hh + 1) * 112] = w_in[:, kcols]
                wav[wi][:, hh * 64:(hh + 1) * 64] = w_in[:, vb:vb + 64]
    sh["waq"], sh["wak"], sh["wav"] = waq, wak, wav
    swb = np.concatenate([np.arange(16, 32), np.arange(0, 16)])
    sh["wkv"] = np.ascontiguousarray(np.concatenate([w_in[:, 2816:3072], w_in[:, 3072:3104], w_in[:, 3072 + swb]], 1))
    sh["wqc"] = np.ascontiguousarray(w_in[:, 2304:2816])
    sh["wg"] = np.ascontiguousarray(w_in[:, 3104:5152])
    w_uq = np.asarray(inp["mla_w_uq"], f)
    cols = []
    for h in range(8):
        b = h * 96
        cols += [b + np.arange(64), b + 64 + np.arange(32), b + 64 + swb]
    sh["wuq"] = np.ascontiguousarray(w_uq[:, np.concatenate(cols)])
    w_ukv = np.asarray(inp["mla_w_ukv"], f)
    sh["wuk"] = np.ascontiguousarray(w_ukv[:, np.concatenate([h * 128 + np.arange(64) for h in range(8)])])
    sh["wuv"] = np.ascontiguousarray(w_ukv[:, np.concatenate([h * 128 + 64 + np.arange(64) for h in range(8)])])
    sh["woa"] = np.asarray(inp["w_o_a"], f)
    sh["wob"] = np.asarray(inp["w_o_b"], f)
    sh["wout"] = np.asarray(inp["w_out"], f)
    sh["w1"] = np.asarray(inp["w_ff1"], f)
    sh["w2"] = np.asarray(inp["w_ff2"], f)

    def gb(v, kc):
        v = np.asarray(v, f).reshape(kc, 128)
        return np.ascontiguousarray(np.repeat(v.T[:, :, None], 128, axis=2).reshape(128, kc * 128))

    sh["g1b"] = gb(inp["norm_mix_pre"], 8)
    sh["g2b"] = gb(inp["norm_mlp_pre"], 8)
    sh["gqb"] = gb(inp["mla_q_norm"], 4)
    sh["gkvb"] = gb(inp["mla_kv_norm"], 2)
    sh["gp1"] = np.ascontiguousarray(np.broadcast_to(np.asarray(inp["norm_mix_post"], f)[None, :], (128, 1024)))
    sh["gp3"] = np.ascontiguousarray(np.broadcast_to(np.asarray(inp["norm_mlp_post"], f)[None, :], (128, 1024)))
    sh["bg"] = np.ascontiguousarray(np.asarray(inp["b_gate"], f).reshape(16, 128).T)
    sh["ident"] = np.eye(128, dtype=f)
    k = np.arange(128)[:, None]
    q = np.arange(128)[None, :]
    mA = np.where(k >= q, 0.0, NEG).astype(f)
    mB = np.where(k <= q, 0.0, NEG).astype(f)
    m512 = np.concatenate([mA, mA, mB, mB], 1)
    sh["maskb"] = np.ascontiguousarray(np.concatenate([m512, m512], 1))
    C, Sg = _rot_tables(np.arange(S_TOK), 10000.0, 32)
    sh["cosk"] = np.ascontiguousarray(C.reshape(64, 128, 32).transpose(1, 0, 2).reshape(128, 64 * 32))
    sh["sink"] = np.ascontiguousarray(Sg.reshape(64, 128, 32).transpose(1, 0, 2).reshape(128, 64 * 32))
    return sh


def _prep_core(x, c):
    f = np.float32
    b, j = c // 4, c % 4
    t0 = OWN * j
    d = {}
    xpad = np.zeros((S_TOK + 2048, 1024), f)
    xpad[1024:1024 + S_TOK] = x[b]
    d["xe"] = np.ascontiguousarray(xpad[t0:t0 + EXT])
    d["xb"] = np.ascontiguousarray(x[b])
    pos_e = np.arange(EXT) + (t0 - 1024)
    Ce, Se = _rot_tables(pos_e, 500000.0, 16)
    ones = np.ones((EXT, 48), f)
    tabk = np.concatenate([ones, Ce, Ce, Se, Se], 1).T
    tabq = np.concatenate([ones, Ce, Se, Ce, Se], 1).T[:, 1024:1024 + OWN]
    d["tabk"] = np.ascontiguousarray(tabk)
    d["tabq"] = np.ascontiguousarray(tabq)
    Cb, Sb = _rot_tables(np.arange(OWN) + t0, 10000.0, 32)
    d["tabqb"] = np.ascontiguousarray(np.concatenate([np.ones((OWN, 64), f), Cb, Sb], 1).T)
    vm = np.zeros((128, NVB), f)
    for (g, r, i), n in VIDX.items():
        dd = DILS[g]
        mlo = 1024 // dd - 64
        e = (mlo + 128 * i + np.arange(128)) * dd + r
        t = e + t0 - 1024
        vm[:, n] = ((t >= 0) & (t < S_TOK)).astype(f)
    d["vmask"] = vm
    return d


def kernel(**inputs):
    x = np.asarray(inputs["x"], np.float32)
    sh = _prep_shared(inputs)
    nc = build_nc()
    in_maps = []
    for c in range(8):
        m = dict(sh)
        m.update(_prep_core(x, c))
        in_maps.append(m)
    res = run_bass_kernel_spmd(nc, in_maps, core_ids=list(range(8)))
    out = np.zeros((2, S_TOK, 1024), np.float32)
    for c in range(8):
        b, j = c // 4, c % 4
        out[b, OWN * j:OWN * (j + 1)] = np.asarray(res.results[c]["y"], np.float32)
    kernel.last_results = res
    return out
```

```python
import numpy as np
from contextlib import ExitStack
import concourse.bass as bass
import concourse.mybir as mybir
from concourse.bass_utils import run_bass_kernel_spmd

F32 = mybir.dt.float32
BF16 = mybir.dt.bfloat16
U8 = mybir.dt.uint8
AF = mybir.ActivationFunctionType
ALU = mybir.AluOpType

S_TOK = 8192
OWN = 2048
EXT = 4096
EPS = 1e-6
DILS = (1, 4, 16)
NEG = -30000.0
DEBUG = False


class Buf:
    __slots__ = ("w", "r", "name")

    def __init__(self, name=""):
        self.w = None
        self.r = []
        self.name = name


class Sched:
    ENGS = ("pe", "act", "dve", "pool", "sp")

    def __init__(self):
        self.ops = {e: [] for e in self.ENGS}
        self.seen = {e: {} for e in self.ENGS}
        self.dma_cnt = {}

    def _need(self, eng, tok, raw):
        if tok is None:
            return None
        if tok[0] == "eng":
            _, e, idx = tok
            if e == eng:
                if eng in ("pe", "sp") or not raw:
                    return None
            key = ("eng", e)
        else:
            _, s, idx = tok
            key = ("dma", s)
        if self.seen[eng].get(key, -1) >= idx:
            return None
        self.seen[eng][key] = idx
        return tok

    def add(self, eng, fn, reads=(), writes=(), dma=None):
        waits = []
        for b in reads:
            t = self._need(eng, b.w, True)
            if t:
                waits.append(t)
        for b in writes:
            t = self._need(eng, b.w, False)
            if t:
                waits.append(t)
            for rt in b.r:
                t = self._need(eng, rt, False)
                if t:
                    waits.append(t)
        idx = len(self.ops[eng])
        if dma is not None:
            n = self.dma_cnt.get(dma, 0) + 1
            self.dma_cnt[dma] = n
            tok = ("dma", dma, n)
        else:
            tok = ("eng", eng, idx)
        self.ops[eng].append({"fn": fn, "waits": waits, "dma": dma, "sig": False})
        for b in reads:
            key = (tok[0], tok[1])
            b.r = [t for t in b.r if (t[0], t[1]) != key] + [tok]
        for b in writes:
            b.w = tok
            b.r = []
        return tok

    def barrier(self):
        lasts = {}
        for e in self.ENGS:
            i = len(self.ops[e]) - 1
            while i >= 0 and (self.ops[e][i]["fn"] is None or self.ops[e][i]["dma"] is not None):
                i -= 1
            lasts[e] = i
        dm = dict(self.dma_cnt)
        for e in self.ENGS:
            waits = []
            for e2 in self.ENGS:
                if e2 != e and e2 != "sp" and lasts[e2] >= 0:
                    t = self._need(e, ("eng", e2, lasts[e2]), True)
                    if t:
                        waits.append(t)
            for s, n in dm.items():
                t = self._need(e, ("dma", s, n), True)
                if t:
                    waits.append(t)
            self.ops[e].append({"fn": None, "waits": waits, "dma": None, "sig": False})

    def emit(self, nc, stack):
        sems = {e: stack.enter_context(nc.semaphore("s_" + e)) for e in self.ENGS}
        dsems = {s: stack.enter_context(nc.semaphore("d_" + s)) for s in self.dma_cnt}
        for e in self.ENGS:
            for op in self.ops[e]:
                for t in op["waits"]:
                    if t[0] == "eng":
                        self.ops[t[1]][t[2]]["sig"] = True
        sigc = {}
        for e in self.ENGS:
            c = 0
            arr = []
            for op in self.ops[e]:
                if op["sig"]:
                    assert op["dma"] is None and op["fn"] is not None
                    c += 1
                arr.append(c)
            sigc[e] = arr
        block = stack.enter_context(nc.Block())

        def run(e, eng):
            for op in self.ops[e]:
                for t in op["waits"]:
                    if t[0] == "eng":
                        eng.wait_ge(sems[t[1]], sigc[t[1]][t[2]])
                    else:
                        eng.wait_ge(dsems[t[1]], 16 * t[2])
                if op["fn"] is None:
                    continue
                ins = op["fn"](eng)
                if op["dma"] is not None:
                    ins.then_inc(dsems[op["dma"]], 16)
                elif op["sig"]:
                    ins.then_inc(sems[e], 1)

        @block.tensor
        def _(eng):
            run("pe", eng)

        @block.scalar
        def _(eng):
            run("act", eng)

        @block.vector
        def _(eng):
            run("dve", eng)

        @block.gpsimd
        def _(eng):
            run("pool", eng)

        @block.sync
        def _(eng):
            run("sp", eng)


def vblocks():
    idx = {}
    n = 0
    goff = []
    for g, d in enumerate(DILS):
        goff.append(n)
        nb = 16 // d + 1
        for r in range(d):
            for i in range(nb):
                idx[(g, r, i)] = n
                n += 1
    return idx, goff, n


VIDX, VGOFF, NVB = vblocks()


def build_nc():
    nc = bass.Bass("TRN2", target_bir_lowering=False)

    def din(name, shape):
        return nc.dram_tensor(name, list(shape), F32, kind="ExternalInput").ap()

    xe = din("xe", [EXT, 1024])
    xb = din("xb", [S_TOK, 1024])
    vmask_d = din("vmask", [128, NVB])
    tabq_d = din("tabq", [112, OWN])
    tabk_d = din("tabk", [112, EXT])
    tabqb_d = din("tabqb", [128, OWN])
    cosk_d = din("cosk", [128, 64 * 32])
    sink_d = din("sink", [128, 64 * 32])
    ident_d = din("ident", [128, 128])
    maskb_d = din("maskb", [128, 1024])
    g1b_d = din("g1b", [128, 1024])
    g2b_d = din("g2b", [128, 1024])
    gqb_d = din("gqb", [128, 512])
    gkvb_d = din("gkvb", [128, 256])
    gp1_d = din("gp1", [128, 1024])
    gp3_d = din("gp3", [128, 1024])
    bg_d = din("bg", [128, 16])
    waq_d = din("waq", [6, 1024, 224])
    wak_d = din("wak", [6, 1024, 224])
    wav_d = din("wav", [6, 1024, 128])
    wkv_d = din("wkv", [1024, 320])
    wqc_d = din("wqc", [1024, 512])
    wg_d = din("wg", [1024, 2048])
    wuq_d = din("wuq", [512, 1024])
    wuk_d = din("wuk", [256, 512])
    wuv_d = din("wuv", [256, 512])
    woa_d = din("woa", [256, 1024])
    wob_d = din("wob", [512, 1024])
    wout_d = din("wout", [1024, 1024])
    w1_d = din("w1", [1024, 4096])
    w2_d = din("w2", [4096, 1024])
    y_d = nc.dram_tensor("y", [OWN, 1024], F32, kind="ExternalOutput").ap()
    hs_d = nc.dram_tensor("hscr", [OWN, 1024], F32, kind="Internal").ap()
    if DEBUG:
        dbgA_d = nc.dram_tensor("dbgA", [64, 4 * OWN], F32, kind="ExternalOutput").ap()
        dbgB_d = nc.dram_tensor("dbgB", [64, 8 * OWN], F32, kind="ExternalOutput").ap()

    S = Sched()
    with ExitStack() as st:
        ARENA = 204 * 1024
        arena = st.enter_context(nc.sbuf_tensor("arena", [128, ARENA], U8))
        pst = [st.enter_context(nc.psum_tensor("ps%d" % i, [128, 1024], F32)) for i in range(4)]
        PB = [Buf("psum%d" % i) for i in range(8)]

        def bank(i):
            return pst[i // 2][:, (i % 2) * 512:(i % 2) * 512 + 512]

        def bank16(i):
            return bank(i).bitcast(BF16)

        def pair(i):
            return pst[i][:, :]

        class Alloc:
            def __init__(self, base=0):
                self.off = base

            def take(self, nbytes):
                o = (self.off + 63) // 64 * 64
                self.off = o + nbytes
                assert self.off <= ARENA, ("arena overflow", self.off)
                self.lim_check()
                return o

            lim = None

            def lim_check(self):
                if self.lim is not None:
                    assert self.off <= self.lim, ("region overflow", self.off, self.lim)

            def t(self, shape, dt):
                sz = 4 if dt == F32 else 2
                n = int(np.prod(shape))
                o = self.take(n * sz)
                ap = arena[:, o:o + n * sz].bitcast(dt)
                if len(shape) == 2:
                    return ap.rearrange("p (a b) -> p a b", b=shape[1])
                if len(shape) == 3:
                    return ap.rearrange("p (a b c) -> p a b c", b=shape[1], c=shape[2])
                return ap

        A0 = Alloc(0)
        ident = A0.t([128], BF16)
        ones32 = A0.t([64], F32)
        epsb = A0.t([1], F32)
        maskb = A0.t([1024], BF16)
        g1b = A0.t([8, 128], F32)
        g2b = A0.t([8, 128], F32)
        gqb = A0.t([4, 128], F32)
        gkvb = A0.t([2, 128], F32)
        bg = A0.t([16], F32)
        stat = A0.t([16], F32)
        CONST_END = A0.off
        ATOP = ARENA - 4 * OWN * 2
        attnA = Alloc(ATOP).t([4, OWN], BF16)
        bC = Buf("consts")
        S.add("pool", lambda e: e.dma_start(out=ident, in_=ident_d), writes=[bC], dma="wc")
        S.add("pool", lambda e: e.dma_start(out=maskb, in_=maskb_d), writes=[bC], dma="wc")
        S.add("sp", lambda e: e.dma_start(out=g1b, in_=g1b_d.rearrange("p (a b) -> p a b", b=128)), writes=[bC], dma="c")
        S.add("sp", lambda e: e.dma_start(out=g2b, in_=g2b_d.rearrange("p (a b) -> p a b", b=128)), writes=[bC], dma="c")
        S.add("sp", lambda e: e.dma_start(out=gqb, in_=gqb_d.rearrange("p (a b) -> p a b", b=128)), writes=[bC], dma="c")
        S.add("sp", lambda e: e.dma_start(out=gkvb, in_=gkvb_d.rearrange("p (a b) -> p a b", b=128)), writes=[bC], dma="c")
        S.add("sp", lambda e: e.dma_start(out=bg, in_=bg_d), writes=[bC], dma="c")
        S.add("dve", lambda e: e.memset(ones32, 1.0), writes=[bC])
        S.add("dve", lambda e: e.memset(epsb, EPS), writes=[bC])

        class NormT:
            def __init__(self, A, width, psum_banks, scale_eng="pool"):
                self.width = width
                self.junk = A.t([width], BF16)
                self.stage = [A.t([width], BF16) for _ in range(2)]
                self.bst = [Buf("stage0"), Buf("stage1")]
                self.bjunk = Buf("junk")
                self.bstat = [Buf("stat%d" % i) for i in range(4)]
                self.k = 0
                self.banks = psum_banks
                self.scale_eng = scale_eng

            def run1(self, src, bsrc, C):
                k = self.k
                self.k += 1
                sc = stat[:, 2 * (k % 4):2 * (k % 4) + 2]
                bs = self.bstat[k % 4]
                stg = self.stage[k % 2][:, 0:C]
                bstg = self.bst[k % 2]
                S.add("act", lambda e: e.activation(out=self.junk[:, 0:C], in_=src, func=AF.Square, accum_out=sc[:, 0:1]),
                      reads=[bsrc], writes=[self.bjunk, bs])
                S.add("act", lambda e: e.activation(out=sc[:, 1:2], in_=sc[:, 0:1], func=AF.Sqrt, scale=1.0 / C, bias=epsb[:, 0:1]),
                      reads=[bs, bC], writes=[bs])
                S.add("dve", lambda e: e.reciprocal(out=sc[:, 1:2], in_=sc[:, 1:2]), reads=[bs], writes=[bs])
                S.add("dve", lambda e: e.tensor_scalar(out=stg, in0=src, scalar1=sc[:, 1:2], scalar2=None, op0=ALU.mult),
                      reads=[bs, bsrc], writes=[bstg])
                return (k, C)

            def run2(self, ctx, gain_b, dst, bdst):
                k, C = ctx
                stg = self.stage[k % 2][:, 0:C]
                bstg = self.bst[k % 2]
                bk = self.banks[k % len(self.banks)]
                pv = bank16(bk)[:, 0:C].rearrange("p (a b) -> p a b", b=128)
                for j in range(C // 128):
                    S.add("pe", lambda e, j=j: e.transpose(out=pv[:, j, :], in_=stg[:, j * 128:(j + 1) * 128], identity=ident),
                          reads=[bstg, bC], writes=[PB[bk]])
                S.add("dve", lambda e: e.tensor_tensor(out=dst, in0=pv, in1=gain_b, op=ALU.mult),
                      reads=[PB[bk], bC], writes=[bdst])

            def run(self, src, bsrc, C, gain_b, dst, bdst, src_psum=False):
                self.run2(self.run1(src, bsrc, C), gain_b, dst, bdst)

        A = Alloc(CONST_END)
        xeT = A.t([8, EXT], BF16)
        tabq = A.t([OWN], F32)
        tabk = A.t([EXT], F32)
        vmask = A.t([NVB], F32)
        PH_A_BASE = A.off
        xst = [A.t([1024], F32) for _ in range(3)]
        bxst = [Buf("xst%d" % i) for i in range(3)]
        NT = NormT(A, 1024, [0, 1])
        bxeT = [Buf("xeT%d" % i) for i in range(32)]

        S.add("sp", lambda e: e.dma_start(out=tabq[0:112, :], in_=tabq_d), writes=[bC], dma="c")
        S.add("sp", lambda e: e.dma_start(out=tabk[0:112, :], in_=tabk_d), writes=[bC], dma="c")
        S.add("sp", lambda e: e.dma_start(out=vmask, in_=vmask_d), writes=[bC], dma="c")
        ctxs = {}
        for st_ in range(33):
            if st_ < 32:
                T = st_
                sl = T % 3
                S.add("sp", lambda e, T=T, sl=sl, xst=xst: e.dma_start(out=xst[sl], in_=xe[T * 128:(T + 1) * 128, :]),
                      writes=[bxst[sl]], dma="x%d" % sl)
                ctxs[T] = NT.run1(xst[sl], bxst[sl], 1024)
            if st_ >= 1:
                T = st_ - 1
                NT.run2(ctxs[T], g1b, xeT[:, :, T * 128:(T + 1) * 128], bxeT[T])
        S.barrier()
        bXE = Buf("xeT_all")

        A = Alloc(PH_A_BASE)
        acc = A.t([2, OWN], F32)
        wq2 = [A.t([8, 224], BF16) for _ in range(2)]
        wk2 = [A.t([8, 224], BF16) for _ in range(2)]
        wv2 = [A.t([8, 128], BF16) for _ in range(2)]
        bwA2 = [Buf("wA0"), Buf("wA1")]
        Q2 = A.t([2, OWN], BF16)
        K2 = A.t([2, EXT], BF16)
        Vg = A.t([32, 2, 65], BF16)
        Pb = [A.t([1024], BF16) for _ in range(2)]
        rinv = A.t([512], F32)
        bacc, bwA, bQ2, bK2, bVg = Buf("acc"), Buf("wA"), Buf("Q2"), Buf("K2"), Buf("Vg")
        bP = [Buf("P0"), Buf("P1")]
        brinv = Buf("rinv")
        battnA = Buf("attnA")
        for hp in range(2):
            for g, d in enumerate(DILS):
                wi = hp * 3 + g
                wq, wk, wv, bwA = wq2[wi % 2], wk2[wi % 2], wv2[wi % 2], bwA2[wi % 2]

                def load_wA(wj):
                    q_, k_, v_, b_w = wq2[wj % 2], wk2[wj % 2], wv2[wj % 2], bwA2[wj % 2]
                    S.add("pool", lambda e: e.dma_start(out=q_, in_=waq_d[wj].rearrange("(k p) n -> p k n", p=128)),
                          writes=[b_w], dma="wa%d" % (wj % 2))
                    S.add("pool", lambda e: e.dma_start(out=k_, in_=wak_d[wj].rearrange("(k p) n -> p k n", p=128)),
                          writes=[b_w], dma="wa%d" % (wj % 2))
                    S.add("pool", lambda e: e.dma_start(out=v_, in_=wav_d[wj].rearrange("(k p) n -> p k n", p=128)),
                          writes=[b_w], dma="wa%d" % (wj % 2))

                if wi == 0:
                    load_wA(0)
                if wi + 1 < 6:
                    load_wA(wi + 1)
                cnt = 0
                for c in range(4):
                    for hh in range(2):
                        bk = cnt % 2
                        cnt += 1
                        for kc in range(8):
                            S.add("pe", lambda e, kc=kc, hh=hh, c=c, bk=bk, wq=wq: e.matmul(
                                bank(bk)[0:112, :], lhsT=wq[:, kc, hh * 112:(hh + 1) * 112],
                                rhs=xeT[:, kc, 1024 + 512 * c:1024 + 512 * (c + 1)], start=(kc == 0), stop=(kc == 7)),
                                reads=[bwA, bXE], writes=[PB[bk]])
                        S.add("dve", lambda e, hh=hh, c=c, bk=bk: e.tensor_tensor(
                            out=Q2[0:112, hh, 512 * c:512 * (c + 1)], in0=bank(bk)[0:112, :],
                            in1=tabq[0:112, 512 * c:512 * (c + 1)], op=ALU.mult),
                            reads=[PB[bk], bC], writes=[bQ2])
                mlo = 1024 // d - 64
                nb = 16 // d + 1
                elo = mlo * d
                ehi = (mlo + 128 * nb) * d
                c0, c1 = elo // 512, (ehi + 511) // 512
                for c in range(c0, c1):
                    for hh in range(2):
                        bk = cnt % 2
                        cnt += 1
                        for kc in range(8):
                            S.add("pe", lambda e, kc=kc, hh=hh, c=c, bk=bk, wk=wk: e.matmul(
                                bank(bk)[0:112, :], lhsT=wk[:, kc, hh * 112:(hh + 1) * 112],
                                rhs=xeT[:, kc, 512 * c:512 * (c + 1)], start=(kc == 0), stop=(kc == 7)),
                                reads=[bwA, bXE], writes=[PB[bk]])
                        S.add("dve", lambda e, hh=hh, c=c, bk=bk: e.tensor_tensor(
                            out=K2[0:112, hh, 512 * c:512 * (c + 1)], in0=bank(bk)[0:112, :],
                            in1=tabk[0:112, 512 * c:512 * (c + 1)], op=ALU.mult),
                            reads=[PB[bk], bC], writes=[bK2])
                nblk = d * nb
                blist = [(r, i) for r in range(d) for i in range(nb)]
                for q0 in range(0, nblk, 4):
                    grp = blist[q0:q0 + 4]
                    bk = cnt % 2
                    cnt += 1
                    for jj, (r, i) in enumerate(grp):
                        e0 = (mlo + 128 * i) * d + r
                        for kc in range(8):
                            S.add("pe", lambda e, kc=kc, jj=jj, e0=e0, bk=bk, d=d, wv=wv: e.matmul(
                                bank(bk)[:, jj * 128:(jj + 1) * 128], lhsT=xeT[:, kc, e0:e0 + 127 * d + 1:d],
                                rhs=wv[:, kc, :], start=(kc == 0), stop=(kc == 7)),
                                reads=[bwA, bXE], writes=[PB[bk]])
                    n = len(grp)
                    S.add("act", lambda e, q0=q0, n=n, bk=bk: e.activation(
                        out=Vg[:, q0:q0 + n, :, 0:64],
                        in_=bank(bk)[:, 0:n * 128].rearrange("p (a b c) -> p a b c", b=2, c=64), func=AF.Copy),
                        reads=[PB[bk]], writes=[bVg])
                for hh in range(2):
                    S.add("dve", lambda e, hh=hh, nblk=nblk, g=g: e.tensor_copy(
                        out=Vg[:, 0:nblk, hh, 64], in_=vmask[:, VGOFF[g]:VGOFF[g] + nblk]),
                        reads=[bC], writes=[bVg])
                qbs = [(r, j) for r in range(d) for j in range(16 // d)]
                for ui in range(8):
                    sp_ = 1 + (ui % 2)
                    ob = 6 + (ui % 2)
                    Pt = Pb[ui % 2]
                    bPt = bP[ui % 2]
                    bS = [PB[2 * sp_], PB[2 * sp_ + 1]]
                    for u in range(2):
                        S.add("pe", lambda e, sp_=sp_, u=u: e.matmul(
                            pair(sp_)[:, u * 512:(u + 1) * 512], lhsT=ident, rhs=maskb[:, 0:512], start=True, stop=False),
                            reads=[bC], writes=[bS[u]])
                    for u in range(2):
                        r, j = qbs[2 * ui + u]
                        qs = 128 * j * d + r
                        for ab in range(2):
                            ks = (mlo + 128 * (j + ab)) * d + r
                            for hh in range(2):
                                off = u * 512 + ab * 256 + hh * 128
                                last = (ab == 1 and hh == 1)
                                S.add("pe", lambda e, sp_=sp_, off=off, hh=hh, ks=ks, qs=qs, last=last, d=d: e.matmul(
                                    pair(sp_)[:, off:off + 128], lhsT=K2[0:112, hh, ks:ks + 127 * d + 1:d],
                                    rhs=Q2[0:112, hh, qs:qs + 127 * d + 1:d], start=False, stop=last),
                                    reads=[bK2, bQ2], writes=[bS[u]])
                    S.add("act", lambda e, sp_=sp_, Pt=Pt: e.activation(out=Pt, in_=pair(sp_), func=AF.Exp, scale=0.125),
                          reads=bS, writes=[bPt])
                    for u in range(2):
                        r, j = qbs[2 * ui + u]
                        for hh in range(2):
                            for ab in range(2):
                                lb = r * nb + j + ab
                                off = u * 512 + ab * 256 + hh * 128
                                S.add("pe", lambda e, ob=ob, u=u, hh=hh, lb=lb, off=off, ab=ab, Pt=Pt: e.matmul(
                                    bank(ob)[0:65, (u * 2 + hh) * 128:(u * 2 + hh + 1) * 128], lhsT=Vg[:, lb, hh, :],
                                    rhs=Pt[:, off:off + 128], start=(ab == 0), stop=(ab == 1)),
                                    reads=[bVg, bPt], writes=[PB[ob]])
                    for u in range(2):
                        r, j = qbs[2 * ui + u]
                        qs = 128 * j * d + r
                        src = bank(ob)[0:65, u * 256:(u + 1) * 256].rearrange("p (a b) -> p a b", b=128)
                        dstv = acc[0:65, :, qs:qs + 127 * d + 1:d]
                        if g == 0:
                            S.add("dve", lambda e, src=src, dstv=dstv: e.tensor_copy(out=dstv, in_=src),
                                  reads=[PB[ob]], writes=[bacc])
                        else:
                            S.add("dve", lambda e, src=src, dstv=dstv: e.tensor_tensor(out=dstv, in0=src, in1=dstv, op=ALU.add),
                                  reads=[PB[ob], bacc], writes=[bacc])
            for c in range(4):
                for hh in range(2):
                    S.add("pe", lambda e, hh=hh, c=c: e.matmul(
                        bank(0)[0:64, :], lhsT=ones32[64:65, 0:64], rhs=acc[64:65, hh, 512 * c:512 * (c + 1)],
                        start=True, stop=True), reads=[bacc, bC], writes=[PB[0]])
                    S.add("dve", lambda e: e.reciprocal(out=rinv[0:64, :], in_=bank(0)[0:64, :]),
                          reads=[PB[0]], writes=[brinv])
                    S.add("dve", lambda e, hh=hh, c=c, hp=hp: e.tensor_tensor(
                        out=attnA[0:64, 2 * hp + hh, 512 * c:512 * (c + 1)], in0=acc[0:64, hh, 512 * c:512 * (c + 1)],
                        in1=rinv[0:64, :], op=ALU.mult), reads=[bacc, brinv], writes=[battnA])
        S.barrier()

        A = Alloc(CONST_END)
        kvnT = A.t([2, S_TOK], BF16)
        qcnT = A.t([4, OWN], BF16)
        wuq = A.t([4, 1024], BF16)
        wuk = A.t([2, 512], BF16)
        wuv = A.t([2, 512], BF16)
        tabqb = A.t([OWN], F32)
        KT_OFF = A.take(0)
        KT = [A.t([S_TOK], BF16) for _ in range(2)]
        KT1_OFF = KT_OFF + S_TOK * 2
        U0 = A.take(0)
        VT = [A.t([64, 65], BF16) for _ in range(2)]
        QT = [A.t([OWN], BF16) for _ in range(2)]
        U1 = A.off
        PBm = [A.t([1024], BF16) for _ in range(3)]
        lsb = A.t([512], F32)
        rinvB = A.t([512], F32)
        ATT_B_OFF = A.take(0)
        attnB = A.t([8, OWN], BF16)
        assert A.off <= ATOP, (A.off, ATOP)
        AB = Alloc(ATT_B_OFF)
        AB.lim = A.off
        xst = [AB.t([1024], F32) for _ in range(2)]
        NT = NormT(AB, 1024, [0, 1])
        wkv = AB.t([8, 320], BF16)
        xsT = [AB.t([8, 128], BF16) for _ in range(2)]
        kstg = [AB.t([128], BF16) for _ in range(2)]
        t1 = [AB.t([32], F32) for _ in range(2)]
        t2 = [AB.t([32], F32) for _ in range(2)]
        NT2 = NormT(AB, 256, [4, 5])
        NT3 = NormT(AB, 512, [4, 5])
        AC = Alloc(U0)
        AC.lim = U1
        cosk = AC.t([64, 32], F32)
        sink = AC.t([64, 32], F32)
        wqc = AC.t([8, 512], BF16)
        bxst = [Buf("xst%d" % i) for i in range(2)]
        bwkv, bxsT = Buf("wkv"), [Buf("xsT0"), Buf("xsT1")]
        bkvn, bKT, bVT, bQT = Buf("kvnT"), [Buf("KT0"), Buf("KT1")], [Buf("VT0"), Buf("VT1")], [Buf("QT0"), Buf("QT1")]
        bkstg, bt = [Buf("kstg0"), Buf("kstg1")], [Buf("t12a"), Buf("t12b")]
        bqcn, bwu, battnB, btab = Buf("qcnT"), Buf("wu"), Buf("attnB"), Buf("tabqb")

        S.add("pool", lambda e: e.dma_start(out=wkv, in_=wkv_d.rearrange("(k p) n -> p k n", p=128)), writes=[bwkv], dma="wb")
        S.add("pool", lambda e: e.dma_start(out=wqc, in_=wqc_d.rearrange("(k p) n -> p k n", p=128)), writes=[bwkv], dma="wb")
        S.add("pool", lambda e: e.dma_start(out=wuq, in_=wuq_d.rearrange("(k p) n -> p k n", p=128)), writes=[bwu], dma="wb")
        S.add("pool", lambda e: e.dma_start(out=wuk, in_=wuk_d.rearrange("(k p) n -> p k n", p=128)), writes=[bwu], dma="wb")
        S.add("pool", lambda e: e.dma_start(out=wuv, in_=wuv_d.rearrange("(k p) n -> p k n", p=128)), writes=[bwu], dma="wb")
        S.add("sp", lambda e: e.dma_start(out=cosk, in_=cosk_d.rearrange("p (a b) -> p a b", b=32)), writes=[bC], dma="c")
        S.add("sp", lambda e: e.dma_start(out=sink, in_=sink_d.rearrange("p (a b) -> p a b", b=32)), writes=[bC], dma="c")
        S.add("sp", lambda e: e.dma_start(out=tabqb, in_=tabqb_d), writes=[btab], dma="c")
        for i_ in range(2):
            S.add("dve", lambda e, i_=i_: e.memset(kstg[i_], 0.0), writes=[bkstg[i_]])

        cA, cB = {}, {}

        def p1_A1(T, src_d, xst=xst):
            sl = T % 2
            S.add("sp", lambda e: e.dma_start(out=xst[sl], in_=src_d), writes=[bxst[sl]], dma="x%d" % sl)
            cA[T] = NT.run1(xst[sl], bxst[sl], 1024)

        def p1_A2(T, NT=NT):
            NT.run2(cA[T], g1b, xsT[T % 2], bxsT[T % 2])

        def p1_B(T):
            xs_, bxs_, pk, i_ = xsT[T % 2], bxsT[T % 2], 2 + (T % 2), T % 2
            for kc in range(8):
                S.add("pe", lambda e, kc=kc: e.matmul(
                    bank(pk)[:, 0:320], lhsT=xs_[:, kc, :], rhs=wkv[:, kc, :], start=(kc == 0), stop=(kc == 7)),
                    reads=[bxs_, bwkv], writes=[PB[pk]])
            cB[T] = NT2.run1(bank(pk)[:, 0:256], PB[pk], 256)
            S.add("dve", lambda e: e.tensor_tensor(out=t1[i_], in0=bank(pk)[:, 256:288], in1=cosk[:, T, :], op=ALU.mult),
                  reads=[PB[pk], bC], writes=[bt[i_]])
            S.add("dve", lambda e: e.tensor_tensor(out=t2[i_], in0=bank(pk)[:, 288:320], in1=sink[:, T, :], op=ALU.mult),
                  reads=[PB[pk], bC], writes=[bt[i_]])
            S.add("dve", lambda e: e.tensor_tensor(out=kstg[i_][:, 64:96], in0=t1[i_], in1=t2[i_], op=ALU.add),
                  reads=[bt[i_]], writes=[bkstg[i_]])
            S.add("dve", lambda e: e.tensor_tensor(out=kstg[i_][:, 96:128], in0=t1[i_], in1=t2[i_], op=ALU.add),
                  reads=[bt[i_]], writes=[bkstg[i_]])

        def p1_C(T):
            i_, p6 = T % 2, 6 + (T % 2)
            NT2.run2(cB[T], gkvb, kvnT[:, :, T * 128:(T + 1) * 128], bkvn)
            S.add("pe", lambda e: e.transpose(out=bank16(p6)[:, 0:128], in_=kstg[i_], identity=ident),
                  reads=[bkstg[i_], bC], writes=[PB[p6]])
            for b_ in range(2):
                S.add("act", lambda e, b_=b_: e.activation(
                    out=KT[b_][64:128, T * 128:(T + 1) * 128], in_=bank16(p6)[64:128, 0:128], func=AF.Copy),
                    reads=[PB[p6]], writes=[bKT[b_]])

        for st_ in range(64 + 3):
            if st_ < 64:
                p1_A1(st_, xb[st_ * 128:(st_ + 1) * 128, :])
            if 0 <= st_ - 1 < 64:
                p1_A2(st_ - 1)
            if 0 <= st_ - 2 < 64:
                p1_B(st_ - 2)
            if 0 <= st_ - 3 < 64:
                p1_C(st_ - 3)

        cQ = {}

        def p3_B(T):
            xs_, bxs_, pk = xsT[T % 2], bxsT[T % 2], 2 + (T % 2)
            for kc in range(8):
                S.add("pe", lambda e, kc=kc: e.matmul(
                    bank(pk), lhsT=xs_[:, kc, :], rhs=wqc[:, kc, :], start=(kc == 0), stop=(kc == 7)),
                    reads=[bxs_, bwkv], writes=[PB[pk]])
            cQ[T] = NT3.run1(bank(pk), PB[pk], 512)

        def p3_C(T):
            NT3.run2(cQ[T], gqb, qcnT[:, :, T * 128:(T + 1) * 128], bqcn)

        for st_ in range(16 + 3):
            if st_ < 16:
                p1_A1(100 + st_, xe[1024 + st_ * 128:1024 + (st_ + 1) * 128, :])
            if 0 <= st_ - 1 < 16:
                p1_A2(100 + st_ - 1)
            if 0 <= st_ - 2 < 16:
                p3_B(st_ - 2)
            if 0 <= st_ - 3 < 16:
                p3_C(st_ - 3)
        S.barrier()
        for b_ in range(2):
            S.add("dve", lambda e, b_=b_: e.memset(VT[b_][:, :, 64:65], 1.0), writes=[bVT[b_]])

        SC_B = 96 ** -0.5

        def build_head(h):
            b_ = h % 2
            cnt = 0
            for c in range(16):
                bk = 6 + (cnt % 2)
                cnt += 1
                for kc in range(2):
                    S.add("pe", lambda e, kc=kc, c=c, bk=bk: e.matmul(
                        bank(bk)[0:64, :], lhsT=wuk[:, kc, h * 64:(h + 1) * 64], rhs=kvnT[:, kc, 512 * c:512 * (c + 1)],
                        start=(kc == 0), stop=(kc == 1)), reads=[bwu, bkvn], writes=[PB[bk]])
                S.add("dve", lambda e, c=c, bk=bk: e.tensor_copy(out=KT[b_][0:64, 512 * c:512 * (c + 1)], in_=bank(bk)[0:64, :]),
                      reads=[PB[bk]], writes=[bKT[b_]])
            for k8 in range(8):
                bk = 6 + (cnt % 2)
                cnt += 1
                for j in range(8):
                    kb = 8 * k8 + j
                    for kc in range(2):
                        S.add("pe", lambda e, kc=kc, kb=kb, j=j, bk=bk: e.matmul(
                            bank(bk)[:, j * 64:(j + 1) * 64], lhsT=kvnT[:, kc, kb * 128:(kb + 1) * 128],
                            rhs=wuv[:, kc, h * 64:(h + 1) * 64], start=(kc == 0), stop=(kc == 1)),
                            reads=[bwu, bkvn], writes=[PB[bk]])
                S.add("dve", lambda e, k8=k8, bk=bk: e.tensor_copy(
                    out=VT[b_][:, 8 * k8:8 * k8 + 8, 0:64], in_=bank(bk).rearrange("p (a b) -> p a b", b=64)),
                    reads=[PB[bk]], writes=[bVT[b_]])
            for c in range(4):
                bk = 6 + (cnt % 2)
                cnt += 1
                for kc in range(4):
                    S.add("pe", lambda e, kc=kc, c=c, bk=bk: e.matmul(
                        bank(bk), lhsT=wuq[:, kc, h * 128:(h + 1) * 128], rhs=qcnT[:, kc, 512 * c:512 * (c + 1)],
                        start=(kc == 0), stop=(kc == 3)), reads=[bwu, bqcn], writes=[PB[bk]])
                S.add("dve", lambda e, c=c, bk=bk: e.tensor_tensor(
                    out=QT[b_][:, 512 * c:512 * (c + 1)], in0=bank(bk), in1=tabqb[:, 512 * c:512 * (c + 1)], op=ALU.mult),
                    reads=[PB[bk], btab], writes=[bQT[b_]])

        bPm = [Buf("Pm%d" % i) for i in range(3)]
        blsb, brB = Buf("lsb"), Buf("rinvB")
        it_ctr = [0]

        def attn_chunk(h, qc):
            b_ = h % 2
            ob = 6 + ((h * 4 + qc) % 2)
            q_ap = QT[b_][:, 512 * qc:512 * (qc + 1)]

            def qk(kp):
                it = it_ctr[0] + kp
                sp_ = it % 3
                for u in range(2):
                    kb = 2 * kp + u
                    S.add("pe", lambda e, sp_=sp_, u=u, kb=kb: e.matmul(
                        pair(sp_)[:, u * 512:(u + 1) * 512], lhsT=KT[b_][:, kb * 128:(kb + 1) * 128], rhs=q_ap,
                        start=True, stop=True), reads=[bKT[b_], bQT[b_]], writes=[PB[2 * sp_ + u]])

            def ex(kp):
                it = it_ctr[0] + kp
                sp_ = it % 3
                pm = it % 3
                S.add("act", lambda e, sp_=sp_, pm=pm: e.activation(out=PBm[pm], in_=pair(sp_), func=AF.Exp, scale=SC_B),
                      reads=[PB[2 * sp_], PB[2 * sp_ + 1]], writes=[bPm[pm]])

            def pv(kp):
                it = it_ctr[0] + kp
                pm = it % 3
                for u in range(2):
                    kb = 2 * kp + u
                    S.add("pe", lambda e, pm=pm, u=u, kb=kb: e.matmul(
                        bank(ob)[0:65, :], lhsT=VT[b_][:, kb, :], rhs=PBm[pm][:, u * 512:(u + 1) * 512],
                        start=(kb == 0), stop=(kb == 63)), reads=[bVT[b_], bPm[pm]], writes=[PB[ob]])

            qk(0)
            qk(1)
            for kp in range(32):
                ex(kp)
                if kp + 2 < 32:
                    qk(kp + 2)
                pv(kp)
            it_ctr[0] += 32
            S.add("act", lambda e: e.activation(out=lsb[64:65, :], in_=bank(ob)[64:65, :], func=AF.Copy),
                  reads=[PB[ob]], writes=[blsb])
            nb_ = 0 + 2 * (it_ctr[0] % 3)
            S.add("pe", lambda e, nb_=nb_: e.matmul(bank(nb_)[0:64, :], lhsT=ones32[64:65, 0:64], rhs=lsb[64:65, :],
                                                    start=True, stop=True), reads=[blsb, bC], writes=[PB[nb_]])
            S.add("dve", lambda e, nb_=nb_: e.reciprocal(out=rinvB[0:64, :], in_=bank(nb_)[0:64, :]),
                  reads=[PB[nb_]], writes=[brB])
            S.add("dve", lambda e: e.tensor_tensor(out=attnB[0:64, h, 512 * qc:512 * (qc + 1)], in0=bank(ob)[0:64, :],
                                                   in1=rinvB[0:64, :], op=ALU.mult),
                  reads=[PB[ob], brB], writes=[battnB])


        AC_ = Alloc(CONST_END)
        wg = AC_.t([8, 2048], BF16)
        wob = AC_.t([8, 1024], BF16)
        woa = AC_.t([4, 1024], BF16)
        xsTc = AC_.t([8, 512], BF16)
        W1_END = AC_.off
        assert W1_END - CONST_END >= 8 * 4096 * 2
        wout = AC_.t([8, 1024], BF16)
        assert AC_.off <= KT1_OFF, (AC_.off, KT1_OFF)
        bwg, bwob, bwoa, bwout, bxsTc = Buf("wg"), Buf("wob"), Buf("woa"), Buf("wout"), Buf("xsTc")

        bwgh = [Buf("wg_h0"), Buf("wg_h1")]

        def load_wg_half(hf):
            for pg in range(2):
                c0 = pg * 1024 + hf * 512
                S.add("pool", lambda e, c0=c0: e.dma_start(
                    out=wg[:, :, c0:c0 + 512], in_=wg_d[:, c0:c0 + 512].rearrange("(k p) n -> p k n", p=128)),
                    writes=[bwgh[hf]], dma="wmg%d" % hf)

        def prefetch_mix_weights():
            S.add("pool", lambda e: e.dma_start(out=woa[0:64], in_=woa_d.rearrange("(k p) n -> p k n", p=64)),
                  writes=[bwoa], dma="wmoa")
            S.add("pool", lambda e: e.dma_start(out=wob[0:64], in_=wob_d.rearrange("(k p) n -> p k n", p=64)),
                  writes=[bwob], dma="wmob")
            load_wg_half(0)
            S.add("pool", lambda e: e.dma_start(out=wout, in_=wout_d.rearrange("(k p) n -> p k n", p=128)),
                  writes=[bwout], dma="wmout")
            load_wg_half(1)

        build_head(0)
        for h in range(8):
            for qc in range(3):
                attn_chunk(h, qc)
            if h + 1 < 8:
                build_head(h + 1)
            attn_chunk(h, 3)
        S.barrier()
        prefetch_mix_weights()

        A = AC_
        A.lim = ATT_B_OFF
        gp1 = A.t([1024], F32)
        xres = [A.t([1024], F32) for _ in range(4)]
        xr2 = [A.t([1024], F32) for _ in range(2)]
        ytmp = A.t([1024], F32)
        NT = NormT(A, 1024, [0, 1])
        uT = A.t([8, 512], BF16)
        ga = A.t([512], F32)
        gb = A.t([512], F32)
        tm1 = A.t([512], F32)
        tm2 = A.t([512], F32)
        bxres, buT = [Buf("xres%d" % i) for i in range(4)], Buf("uT")
        bxr2, bytmp = [Buf("xr2_0"), Buf("xr2_1")], Buf("ytmp")
        bga, bgb, btm1, btm2 = Buf("ga"), Buf("gb"), Buf("tm1"), Buf("tm2")
        bst2 = Buf("stat2")
        S.add("sp", lambda e: e.dma_start(out=gp1, in_=gp1_d), writes=[bC], dma="c")

        def normC(c, NT=NT):
            cc = {}
            for st_ in range(5):
                if st_ < 4:
                    i = st_
                    S.add("sp", lambda e, i=i: e.dma_start(
                        out=xres[i], in_=xe[1024 + 512 * c + 128 * i:1024 + 512 * c + 128 * (i + 1), :]),
                        writes=[bxres[i]], dma="xr%d" % i)
                    cc[i] = NT.run1(xres[i], bxres[i], 1024)
                if st_ >= 1:
                    i = st_ - 1
                    NT.run2(cc[i], g1b, xsTc[:, :, 128 * i:128 * (i + 1)], bxsTc)

        def gatesC(c):
            for ft in range(8):
                for hh in range(4):
                    S.add("pe", lambda e, hh=hh, ft=ft: e.matmul(
                        bank(2), lhsT=woa[0:64, hh, ft * 128:(ft + 1) * 128], rhs=attnA[0:64, hh, 512 * c:512 * (c + 1)],
                        start=(hh == 0), stop=(hh == 3)), reads=[bwoa, battnA], writes=[PB[2]])
                for h in range(8):
                    S.add("pe", lambda e, h=h, ft=ft: e.matmul(
                        bank(3), lhsT=wob[0:64, h, ft * 128:(ft + 1) * 128], rhs=attnB[0:64, h, 512 * c:512 * (c + 1)],
                        start=(h == 0), stop=(h == 7)), reads=[bwob, battnB], writes=[PB[3]])
                for kc in range(8):
                    S.add("pe", lambda e, kc=kc, ft=ft: e.matmul(
                        bank(4), lhsT=wg[:, kc, ft * 128:(ft + 1) * 128], rhs=xsTc[:, kc, :],
                        start=(kc == 0), stop=(kc == 7)), reads=[bwgh[ft // 4], bxsTc], writes=[PB[4]])
                for kc in range(8):
                    S.add("pe", lambda e, kc=kc, ft=ft: e.matmul(
                        bank(5), lhsT=wg[:, kc, 1024 + ft * 128:1024 + (ft + 1) * 128], rhs=xsTc[:, kc, :],
                        start=(kc == 0), stop=(kc == 7)), reads=[bwgh[ft // 4], bxsTc], writes=[PB[5]])
                S.add("act", lambda e, ft=ft: e.activation(out=ga, in_=bank(4), func=AF.Sigmoid, bias=bg[:, ft:ft + 1]),
                      reads=[PB[4], bC], writes=[bga])
                S.add("act", lambda e, ft=ft: e.activation(out=gb, in_=bank(5), func=AF.Sigmoid, bias=bg[:, 8 + ft:9 + ft]),
                      reads=[PB[5], bC], writes=[bgb])
                S.add("dve", lambda e: e.tensor_tensor(out=tm1, in0=bank(2), in1=ga, op=ALU.mult),
                      reads=[PB[2], bga], writes=[btm1])
                S.add("dve", lambda e: e.tensor_tensor(out=tm2, in0=bank(3), in1=gb, op=ALU.mult),
                      reads=[PB[3], bgb], writes=[btm2])
                S.add("dve", lambda e, ft=ft: e.tensor_tensor(out=uT[:, ft, :], in0=tm1, in1=tm2, op=ALU.add),
                      reads=[btm1, btm2], writes=[buT])

        def tokC(c, NT=NT):
            sc = stat[:, 8:10]
            for i in range(4):
                xr = xr2[i % 2]
                bxr = bxr2[i % 2]
                S.add("sp", lambda e, i=i, xr=xr: e.dma_start(
                    out=xr, in_=xe[1024 + 512 * c + 128 * i:1024 + 512 * c + 128 * (i + 1), :]),
                    writes=[bxr], dma="xq%d" % (i % 2))
                for half in range(2):
                    for ft in range(8):
                        S.add("pe", lambda e, i=i, half=half, ft=ft: e.matmul(
                            pair(3)[:, half * 512:(half + 1) * 512], lhsT=uT[:, ft, 128 * i:128 * (i + 1)],
                            rhs=wout[:, ft, half * 512:(half + 1) * 512], start=(ft == 0), stop=(ft == 7)),
                            reads=[buT, bwout], writes=[PB[6 + half]])
                S.add("act", lambda e: e.activation(out=NT.junk, in_=pair(3), func=AF.Square, accum_out=sc[:, 0:1]),
                      reads=[PB[6], PB[7]], writes=[NT.bjunk, bst2])
                S.add("act", lambda e: e.activation(out=sc[:, 1:2], in_=sc[:, 0:1], func=AF.Sqrt, scale=1.0 / 1024, bias=epsb[:, 0:1]),
                      reads=[bst2, bC], writes=[bst2])
                S.add("dve", lambda e: e.reciprocal(out=sc[:, 1:2], in_=sc[:, 1:2]), reads=[bst2], writes=[bst2])
                S.add("dve", lambda e: e.scalar_tensor_tensor(out=ytmp, in0=pair(3), scalar=sc[:, 1:2], in1=gp1,
                                                              op0=ALU.mult, op1=ALU.mult),
                      reads=[PB[6], PB[7], bst2, bC], writes=[bytmp])
                S.add("dve", lambda e, xr=xr: e.tensor_tensor(out=xr, in0=ytmp, in1=xr, op=ALU.add),
                      reads=[bytmp, bxr], writes=[bxr])
                S.add("sp", lambda e, xr=xr, i=i: e.dma_start(
                    out=hs_d[512 * c + 128 * i:512 * c + 128 * (i + 1), :], in_=xr), reads=[bxr], dma="hst%d" % (i % 2))

        AD_ = Alloc(CONST_END)
        w1 = AD_.t([8, 4096], BF16)
        assert AD_.off <= W1_END
        bw1, bw2 = Buf("w1"), Buf("w2")

        bw1q = [Buf("w1q%d" % q) for q in range(4)]

        def prefetch_w1():
            for q in range(4):
                S.add("pool", lambda e, q=q: e.dma_start(
                    out=w1[:, :, 1024 * q:1024 * (q + 1)],
                    in_=w1_d[:, 1024 * q:1024 * (q + 1)].rearrange("(k p) n -> p k n", p=128)),
                    writes=[bw1q[q]], dma="w1q%d" % q)

        for c in range(4):
            normC(c)
            gatesC(c)
            tokC(c)
        S.barrier()
        prefetch_w1()

        A = AD_
        A.lim = ARENA
        w2 = A.t([32, 1024], BF16)
        gp3 = A.t([1024], F32)
        hres = [A.t([1024], F32) for _ in range(4)]
        NT = NormT(A, 1024, [0, 1])
        hsT = [A.t([8, 256], BF16) for _ in range(2)]
        aT = A.t([32, 256], BF16)
        rl = [A.t([256], F32) for _ in range(4)]
        otile = A.t([1024], F32)
        bhres, bhsT, baT = [Buf("hres%d" % i) for i in range(4)], [Buf("hsT0"), Buf("hsT1")], Buf("aT")
        brl, bot, bst3 = [Buf("rl%d" % i) for i in range(4)], Buf("ot"), Buf("stat3")
        for k4 in range(8):
            S.add("pool", lambda e, k4=k4: e.dma_start(
                out=w2[:, 4 * k4:4 * k4 + 4, :], in_=w2_d[512 * k4:512 * (k4 + 1), :].rearrange("(k p) n -> p k n", p=128)),
                writes=[bw2], dma="w2")
        S.add("sp", lambda e: e.dma_start(out=gp3, in_=gp3_d), writes=[bC], dma="c")

        def normD(c, NT=NT):
            hp_ = c % 2
            cc = {}
            for st_ in range(3):
                if st_ < 2:
                    i = st_
                    hi = 2 * hp_ + i
                    S.add("sp", lambda e, i=i, hi=hi: e.dma_start(
                        out=hres[hi], in_=hs_d[256 * c + 128 * i:256 * c + 128 * (i + 1), :]),
                        writes=[bhres[hi]], dma="hr%d" % hi)
                    cc[i] = NT.run1(hres[hi], bhres[hi], 1024)
                if st_ >= 1:
                    i = st_ - 1
                    NT.run2(cc[i], g2b, hsT[hp_][:, :, 128 * i:128 * (i + 1)], bhsT[hp_])

        def ff1(c):
            hp_ = c % 2
            for m in range(32):
                bk = 2 + (m % 4)
                for kc in range(8):
                    S.add("pe", lambda e, kc=kc, m=m, bk=bk: e.matmul(
                        bank(bk)[:, 0:256], lhsT=w1[:, kc, m * 128:(m + 1) * 128], rhs=hsT[hp_][:, kc, :],
                        start=(kc == 0), stop=(kc == 7)), reads=[bw1q[m // 8], bhsT[hp_]], writes=[PB[bk]])
                r_ = rl[m % 4]
                S.add("act", lambda e, bk=bk, r_=r_: e.activation(out=r_, in_=bank(bk)[:, 0:256], func=AF.Relu),
                      reads=[PB[bk]], writes=[brl[m % 4]])
                S.add("dve", lambda e, m=m, r_=r_: e.tensor_tensor(out=aT[:, m, :], in0=r_, in1=r_, op=ALU.mult),
                      reads=[brl[m % 4]], writes=[baT])

        def ff2(c, NT=NT):
            hp_ = c % 2
            sc = stat[:, 8:10]
            for i in range(2):
                hi = 2 * hp_ + i
                for half in range(2):
                    for m in range(32):
                        S.add("pe", lambda e, i=i, half=half, m=m: e.matmul(
                            pair(3)[:, half * 512:(half + 1) * 512], lhsT=aT[:, m, 128 * i:128 * (i + 1)],
                            rhs=w2[:, m, half * 512:(half + 1) * 512], start=(m == 0), stop=(m == 31)),
                            reads=[baT, bw2], writes=[PB[6 + half]])
                S.add("act", lambda e: e.activation(out=NT.junk, in_=pair(3), func=AF.Square, accum_out=sc[:, 0:1]),
                      reads=[PB[6], PB[7]], writes=[NT.bjunk, bst3])
                S.add("act", lambda e: e.activation(out=sc[:, 1:2], in_=sc[:, 0:1], func=AF.Sqrt, scale=1.0 / 1024, bias=epsb[:, 0:1]),
                      reads=[bst3, bC], writes=[bst3])
                S.add("dve", lambda e: e.reciprocal(out=sc[:, 1:2], in_=sc[:, 1:2]), reads=[bst3], writes=[bst3])
                S.add("dve", lambda e: e.scalar_tensor_tensor(out=otile, in0=pair(3), scalar=sc[:, 1:2], in1=gp3,
                                                              op0=ALU.mult, op1=ALU.mult),
                      reads=[PB[6], PB[7], bst3, bC], writes=[bot])
                S.add("dve", lambda e, hi=hi: e.tensor_tensor(out=hres[hi], in0=otile, in1=hres[hi], op=ALU.add),
                      reads=[bot, bhres[hi]], writes=[bhres[hi]])
                S.add("sp", lambda e, hi=hi, i=i: e.dma_start(
                    out=y_d[256 * c + 128 * i:256 * c + 128 * (i + 1), :], in_=hres[hi]), reads=[bhres[hi]], dma="yst%d" % hi)

        normD(0)
        for c in range(8):
            ff1(c)
            if c + 1 < 8:
                normD(c + 1)
            ff2(c)
        S.barrier()
        S.emit(nc, st)
    return nc


def _rot_tables(pos, theta, rot_dim):
    half = rot_dim // 2
    inv = np.float32(theta) ** (-(np.arange(half, dtype=np.float32) * np.float32(2.0) / np.float32(rot_dim)))
    ang = pos.astype(np.float32)[:, None] * inv.astype(np.float32)[None, :]
    c = np.cos(ang).astype(np.float32)
    s = np.sin(ang).astype(np.float32)
    C = np.concatenate([c, c], 1)
    Sg = np.concatenate([-s, s], 1)
    return C, Sg


def _prep_shared(inp):
    f = np.float32
    w_in = np.asarray(inp["w_in"], f)
    sh = {}
    waq = np.zeros((6, 1024, 224), f)
    wak = np.zeros((6, 1024, 224), f)
    wav = np.zeros((6, 1024, 128), f)
    sw = np.concatenate([np.arange(8, 16), np.arange(0, 8)])
    for hp in range(2):
        for g in range(3):
            wi = hp * 3 + g
            for hh in range(2):
                h = 4 * g + 2 * hp + hh
                qb = h * 64
                kb = 768 + h * 64
                vb = 1536 + h * 64
                nope = np.arange(16, 64)
                rot = np.arange(0, 16)
                qcols = np.concatenate([qb + nope, qb + rot, qb + sw, qb + rot, qb + sw])
                kcols = np.concatenate([kb + nope, kb + rot, kb + rot, kb + sw, kb + sw])
                waq[wi][:, hh * 112:(hh + 1) * 112] = w_in[:, qcols]
                wak[wi][:, hh * 112:(hh + 1) * 112] = w_in[:, kcols]
                wav[wi][:, hh * 64:(hh + 1) * 64] = w_in[:, vb:vb + 64]
    sh["waq"], sh["wak"], sh["wav"] = waq, wak, wav
    swb = np.concatenate([np.arange(16, 32), np.arange(0, 16)])
    sh["wkv"] = np.ascontiguousarray(np.concatenate([w_in[:, 2816:3072], w_in[:, 3072:3104], w_in[:, 3072 + swb]], 1))
    sh["wqc"] = np.ascontiguousarray(w_in[:, 2304:2816])
    sh["wg"] = np.ascontiguousarray(w_in[:, 3104:5152])
    w_uq = np.asarray(inp["mla_w_uq"], f)
    cols = []
    for h in range(8):
        b = h * 96
        cols += [b + np.arange(64), b + 64 + np.arange(32), b + 64 + swb]
    sh["wuq"] = np.ascontiguousarray(w_uq[:, np.concatenate(cols)])
    w_ukv = np.asarray(inp["mla_w_ukv"], f)
    sh["wuk"] = np.ascontiguousarray(w_ukv[:, np.concatenate([h * 128 + np.arange(64) for h in range(8)])])
    sh["wuv"] = np.ascontiguousarray(w_ukv[:, np.concatenate([h * 128 + 64 + np.arange(64) for h in range(8)])])
    sh["woa"] = np.asarray(inp["w_o_a"], f)
    sh["wob"] = np.asarray(inp["w_o_b"], f)
    sh["wout"] = np.asarray(inp["w_out"], f)
    sh["w1"] = np.asarray(inp["w_ff1"], f)
    sh["w2"] = np.asarray(inp["w_ff2"], f)

    def gb(v, kc):
        v = np.asarray(v, f).reshape(kc, 128)
        return np.ascontiguousarray(np.repeat(v.T[:, :, None], 128, axis=2).reshape(128, kc * 128))

    sh["g1b"] = gb(inp["norm_mix_pre"], 8)
    sh["g2b"] = gb(inp["norm_mlp_pre"], 8)
    sh["gqb"] = gb(inp["mla_q_norm"], 4)
    sh["gkvb"] = gb(inp["mla_kv_norm"], 2)
    sh["gp1"] = np.ascontiguousarray(np.broadcast_to(np.asarray(inp["norm_mix_post"], f)[None, :], (128, 1024)))
    sh["gp3"] = np.ascontiguousarray(np.broadcast_to(np.asarray(inp["norm_mlp_post"], f)[None, :], (128, 1024)))
    sh["bg"] = np.ascontiguousarray(np.asarray(inp["b_gate"], f).reshape(16, 128).T)
    sh["ident"] = np.eye(128, dtype=f)
    k = np.arange(128)[:, None]
    q = np.arange(128)[None, :]
    mA = np.where(k >= q, 0.0, NEG).astype(f)
    mB = np.where(k <= q, 0.0, NEG).astype(f)
    m512 = np.concatenate([mA, mA, mB, mB], 1)
    sh["maskb"] = np.ascontiguousarray(np.concatenate([m512, m512], 1))
    C, Sg = _rot_tables(np.arange(S_TOK), 10000.0, 32)
    sh["cosk"] = np.ascontiguousarray(C.reshape(64, 128, 32).transpose(1, 0, 2).reshape(128, 64 * 32))
    sh["sink"] = np.ascontiguousarray(Sg.reshape(64, 128, 32).transpose(1, 0, 2).reshape(128, 64 * 32))
    return sh


def _prep_core(x, c):
    f = np.float32
    b, j = c // 4, c % 4
    t0 = OWN * j
    d = {}
    xpad = np.zeros((S_TOK + 2048, 1024), f)
    xpad[1024:1024 + S_TOK] = x[b]
    d["xe"] = np.ascontiguousarray(xpad[t0:t0 + EXT])
    d["xb"] = np.ascontiguousarray(x[b])
    pos_e = np.arange(EXT) + (t0 - 1024)
    Ce, Se = _rot_tables(pos_e, 500000.0, 16)
    ones = np.ones((EXT, 48), f)
    tabk = np.concatenate([ones, Ce, Ce, Se, Se], 1).T
    tabq = np.concatenate([ones, Ce, Se, Ce, Se], 1).T[:, 1024:1024 + OWN]
    d["tabk"] = np.ascontiguousarray(tabk)
    d["tabq"] = np.ascontiguousarray(tabq)
    Cb, Sb = _rot_tables(np.arange(OWN) + t0, 10000.0, 32)
    d["tabqb"] = np.ascontiguousarray(np.concatenate([np.ones((OWN, 64), f), Cb, Sb], 1).T)
    vm = np.zeros((128, NVB), f)
    for (g, r, i), n in VIDX.items():
        dd = DILS[g]
        mlo = 1024 // dd - 64
        e = (mlo + 128 * i + np.arange(128)) * dd + r
        t = e + t0 - 1024
        vm[:, n] = ((t >= 0) & (t < S_TOK)).astype(f)
    d["vmask"] = vm
    return d


def kernel(**inputs):
    x = np.asarray(inputs["x"], np.float32)
    sh = _prep_shared(inputs)
    nc = build_nc()
    in_maps = []
    for c in range(8):
        m = dict(sh)
        m.update(_prep_core(x, c))
        in_maps.append(m)
    res = run_bass_kernel_spmd(nc, in_maps, core_ids=list(range(8)))
    out = np.zeros((2, S_TOK, 1024), np.float32)
    for c in range(8):
        b, j = c // 4, c % 4
        out[b, OWN * j:OWN * (j + 1)] = np.asarray(res.results[c]["y"], np.float32)
    kernel.last_results = res
    return out
```

```python
import numpy as np
from contextlib import ExitStack
import concourse.bass as bass
import concourse.mybir as mybir
from concourse.bass_utils import run_bass_kernel_spmd

F32 = mybir.dt.float32
BF16 = mybir.dt.bfloat16
U8 = mybir.dt.uint8
AF = mybir.ActivationFunctionType
ALU = mybir.AluOpType

S_TOK = 8192
OWN = 2048
EXT = 4096
EPS = 1e-6
DILS = (1, 4, 16)
NEG = -30000.0
DEBUG = False


class Buf:
    __slots__ = ("w", "r", "name")

    def __init__(self, name=""):
        self.w = None
        self.r = []
        self.name = name


class Sched:
    ENGS = ("pe", "act", "dve", "pool", "sp")

    def __init__(self):
        self.ops = {e: [] for e in self.ENGS}
        self.seen = {e: {} for e in self.ENGS}
        self.dma_cnt = {}

    def _need(self, eng, tok, raw):
        if tok is None:
            return None
        if tok[0] == "eng":
            _, e, idx = tok
            if e == eng:
                if eng in ("pe", "sp") or not raw:
                    return None
            key = ("eng", e)
        else:
            _, s, idx = tok
            key = ("dma", s)
        if self.seen[eng].get(key, -1) >= idx:
            return None
        self.seen[eng][key] = idx
        return tok

    def add(self, eng, fn, reads=(), writes=(), dma=None):
        waits = []
        for b in reads:
            t = self._need(eng, b.w, True)
            if t:
                waits.append(t)
        for b in writes:
            t = self._need(eng, b.w, False)
            if t:
                waits.append(t)
            for rt in b.r:
                t = self._need(eng, rt, False)
                if t:
                    waits.append(t)
        idx = len(self.ops[eng])
        if dma is not None:
            n = self.dma_cnt.get(dma, 0) + 1
            self.dma_cnt[dma] = n
            tok = ("dma", dma, n)
        else:
            tok = ("eng", eng, idx)
        self.ops[eng].append({"fn": fn, "waits": waits, "dma": dma, "sig": False})
        for b in reads:
            key = (tok[0], tok[1])
            b.r = [t for t in b.r if (t[0], t[1]) != key] + [tok]
        for b in writes:
            b.w = tok
            b.r = []
        return tok

    def barrier(self):
        lasts = {}
        for e in self.ENGS:
            i = len(self.ops[e]) - 1
            while i >= 0 and (self.ops[e][i]["fn"] is None or self.ops[e][i]["dma"] is not None):
                i -= 1
            lasts[e] = i
        dm = dict(self.dma_cnt)
        for e in self.ENGS:
            waits = []
            for e2 in self.ENGS:
                if e2 != e and e2 != "sp" and lasts[e2] >= 0:
                    t = self._need(e, ("eng", e2, lasts[e2]), True)
                    if t:
                        waits.append(t)
            for s, n in dm.items():
                t = self._need(e, ("dma", s, n), True)
                if t:
                    waits.append(t)
            self.ops[e].append({"fn": None, "waits": waits, "dma": None, "sig": False})

    def emit(self, nc, stack):
        sems = {e: stack.enter_context(nc.semaphore("s_" + e)) for e in self.ENGS}
        dsems = {s: stack.enter_context(nc.semaphore("d_" + s)) for s in self.dma_cnt}
        for e in self.ENGS:
            for op in self.ops[e]:
                for t in op["waits"]:
                    if t[0] == "eng":
                        self.ops[t[1]][t[2]]["sig"] = True
        sigc = {}
        for e in self.ENGS:
            c = 0
            arr = []
            for op in self.ops[e]:
                if op["sig"]:
                    assert op["dma"] is None and op["fn"] is not None
                    c += 1
                arr.append(c)
            sigc[e] = arr
        block = stack.enter_context(nc.Block())

        def run(e, eng):
            for op in self.ops[e]:
                for t in op["waits"]:
                    if t[0] == "eng":
                        eng.wait_ge(sems[t[1]], sigc[t[1]][t[2]])
                    else:
                        eng.wait_ge(dsems[t[1]], 16 * t[2])
                if op["fn"] is None:
                    continue
                ins = op["fn"](eng)
                if op["dma"] is not None:
                    ins.then_inc(dsems[op["dma"]], 16)
                elif op["sig"]:
                    ins.then_inc(sems[e], 1)

        @block.tensor
        def _(eng):
            run("pe", eng)

        @block.scalar
        def _(eng):
            run("act", eng)

        @block.vector
        def _(eng):
            run("dve", eng)

        @block.gpsimd
        def _(eng):
            run("pool", eng)

        @block.sync
        def _(eng):
            run("sp", eng)


def vblocks():
    idx = {}
    n = 0
    goff = []
    for g, d in enumerate(DILS):
        goff.append(n)
        nb = 16 // d + 1
        for r in range(d):
            for i in range(nb):
                idx[(g, r, i)] = n
                n += 1
    return idx, goff, n


VIDX, VGOFF, NVB = vblocks()


def build_nc():
    nc = bass.Bass("TRN2", target_bir_lowering=False)

    def din(name, shape):
        return nc.dram_tensor(name, list(shape), F32, kind="ExternalInput").ap()

    xe = din("xe", [EXT, 1024])
    xb = din("xb", [S_TOK, 1024])
    vmask_d = din("vmask", [128, NVB])
    tabq_d = din("tabq", [112, OWN])
    tabk_d = din("tabk", [112, EXT])
    tabqb_d = din("tabqb", [128, OWN])
    cosk_d = din("cosk", [128, 64 * 32])
    sink_d = din("sink", [128, 64 * 32])
    ident_d = din("ident", [128, 128])
    maskb_d = din("maskb", [128, 1024])
    g1b_d = din("g1b", [128, 1024])
    g2b_d = din("g2b", [128, 1024])
    gqb_d = din("gqb", [128, 512])
    gkvb_d = din("gkvb", [128, 256])
    gp1_d = din("gp1", [128, 1024])
    gp3_d = din("gp3", [128, 1024])
    bg_d = din("bg", [128, 16])
    waq_d = din("waq", [6, 1024, 224])
    wak_d = din("wak", [6, 1024, 224])
    wav_d = din("wav", [6, 1024, 128])
    wkv_d = din("wkv", [1024, 320])
    wqc_d = din("wqc", [1024, 512])
    wg_d = din("wg", [1024, 2048])
    wuq_d = din("wuq", [512, 1024])
    wuk_d = din("wuk", [256, 512])
    wuv_d = din("wuv", [256, 512])
    woa_d = din("woa", [256, 1024])
    wob_d = din("wob", [512, 1024])
    wout_d = din("wout", [1024, 1024])
    w1_d = din("w1", [1024, 4096])
    w2_d = din("w2", [4096, 1024])
    y_d = nc.dram_tensor("y", [OWN, 1024], F32, kind="ExternalOutput").ap()
    hs_d = nc.dram_tensor("hscr", [OWN, 1024], F32, kind="Internal").ap()
    if DEBUG:
        dbgA_d = nc.dram_tensor("dbgA", [64, 4 * OWN], F32, kind="ExternalOutput").ap()
        dbgB_d = nc.dram_tensor("dbgB", [64, 8 * OWN], F32, kind="ExternalOutput").ap()

    S = Sched()
    with ExitStack() as st:
        ARENA = 204 * 1024
        arena = st.enter_context(nc.sbuf_tensor("arena", [128, ARENA], U8))
        pst = [st.enter_context(nc.psum_tensor("ps%d" % i, [128, 1024], F32)) for i in range(4)]
        PB = [Buf("psum%d" % i) for i in range(8)]

        def bank(i):
            return pst[i // 2][:, (i % 2) * 512:(i % 2) * 512 + 512]

        def bank16(i):
            return bank(i).bitcast(BF16)

        def pair(i):
            return pst[i][:, :]

        class Alloc:
            def __init__(self, base=0):
                self.off = base

            def take(self, nbytes):
                o = (self.off + 63) // 64 * 64
                self.off = o + nbytes
                assert self.off <= ARENA, ("arena overflow", self.off)
                self.lim_check()
                return o

            lim = None

            def lim_check(self):
                if self.lim is not None:
                    assert self.off <= self.lim, ("region overflow", self.off, self.lim)

            def t(self, shape, dt):
                sz = 4 if dt == F32 else 2
                n = int(np.prod(shape))
                o = self.take(n * sz)
                ap = arena[:, o:o + n * sz].bitcast(dt)
                if len(shape) == 2:
                    return ap.rearrange("p (a b) -> p a b", b=shape[1])
                if len(shape) == 3:
                    return ap.rearrange("p (a b c) -> p a b c", b=shape[1], c=shape[2])
                return ap

        A0 = Alloc(0)
        ident = A0.t([128], BF16)
        ones32 = A0.t([64], F32)
        epsb = A0.t([1], F32)
        maskb = A0.t([1024], BF16)
        g1b = A0.t([8, 128], F32)
        g2b = A0.t([8, 128], F32)
        gqb = A0.t([4, 128], F32)
        gkvb = A0.t([2, 128], F32)
        bg = A0.t([16], F32)
        stat = A0.t([16], F32)
        CONST_END = A0.off
        ATOP = ARENA - 4 * OWN * 2
        attnA = Alloc(ATOP).t([4, OWN], BF16)
        bC = Buf("consts")
        S.add("pool", lambda e: e.dma_start(out=ident, in_=ident_d), writes=[bC], dma="wc")
        S.add("pool", lambda e: e.dma_start(out=maskb, in_=maskb_d), writes=[bC], dma="wc")
        S.add("sp", lambda e: e.dma_start(out=g1b, in_=g1b_d.rearrange("p (a b) -> p a b", b=128)), writes=[bC], dma="c")
        S.add("sp", lambda e: e.dma_start(out=g2b, in_=g2b_d.rearrange("p (a b) -> p a b", b=128)), writes=[bC], dma="c")
        S.add("sp", lambda e: e.dma_start(out=gqb, in_=gqb_d.rearrange("p (a b) -> p a b", b=128)), writes=[bC], dma="c")
        S.add("sp", lambda e: e.dma_start(out=gkvb, in_=gkvb_d.rearrange("p (a b) -> p a b", b=128)), writes=[bC], dma="c")
        S.add("sp", lambda e: e.dma_start(out=bg, in_=bg_d), writes=[bC], dma="c")
        S.add("dve", lambda e: e.memset(ones32, 1.0), writes=[bC])
        S.add("dve", lambda e: e.memset(epsb, EPS), writes=[bC])

        class NormT:
            def __init__(self, A, width, psum_banks, scale_eng="pool"):
                self.width = width
                self.junk = A.t([width], BF16)
                self.stage = [A.t([width], BF16) for _ in range(2)]
                self.bst = [Buf("stage0"), Buf("stage1")]
                self.bjunk = Buf("junk")
                self.bstat = [Buf("stat%d" % i) for i in range(4)]
                self.k = 0
                self.banks = psum_banks
                self.scale_eng = scale_eng

            def run1(self, src, bsrc, C):
                k = self.k
                self.k += 1
                sc = stat[:, 2 * (k % 4):2 * (k % 4) + 2]
                bs = self.bstat[k % 4]
                stg = self.stage[k % 2][:, 0:C]
                bstg = self.bst[k % 2]
                S.add("act", lambda e: e.activation(out=self.junk[:, 0:C], in_=src, func=AF.Square, accum_out=sc[:, 0:1]),
                      reads=[bsrc], writes=[self.bjunk, bs])
                S.add("act", lambda e: e.activation(out=sc[:, 1:2], in_=sc[:, 0:1], func=AF.Sqrt, scale=1.0 / C, bias=epsb[:, 0:1]),
                      reads=[bs, bC], writes=[bs])
                S.add("dve", lambda e: e.reciprocal(out=sc[:, 1:2], in_=sc[:, 1:2]), reads=[bs], writes=[bs])
                S.add("dve", lambda e: e.tensor_scalar(out=stg, in0=src, scalar1=sc[:, 1:2], scalar2=None, op0=ALU.mult),
                      reads=[bs, bsrc], writes=[bstg])
                return (k, C)

            def run2(self, ctx, gain_b, dst, bdst):
                k, C = ctx
                stg = self.stage[k % 2][:, 0:C]
                bstg = self.bst[k % 2]
                bk = self.banks[k % len(self.banks)]
                pv = bank16(bk)[:, 0:C].rearrange("p (a b) -> p a b", b=128)
                for j in range(C // 128):
                    S.add("pe", lambda e, j=j: e.transpose(out=pv[:, j, :], in_=stg[:, j * 128:(j + 1) * 128], identity=ident),
                          reads=[bstg, bC], writes=[PB[bk]])
                S.add("dve", lambda e: e.tensor_tensor(out=dst, in0=pv, in1=gain_b, op=ALU.mult),
                      reads=[PB[bk], bC], writes=[bdst])

            def run(self, src, bsrc, C, gain_b, dst, bdst, src_psum=False):
                self.run2(self.run1(src, bsrc, C), gain_b, dst, bdst)

        A = Alloc(CONST_END)
        xeT = A.t([8, EXT], BF16)
        tabq = A.t([OWN], F32)
        tabk = A.t([EXT], F32)
        vmask = A.t([NVB], F32)
        PH_A_BASE = A.off
        xst = [A.t([1024], F32) for _ in range(3)]
        bxst = [Buf("xst%d" % i) for i in range(3)]
        NT = NormT(A, 1024, [0, 1])
        bxeT = [Buf("xeT%d" % i) for i in range(32)]

        S.add("sp", lambda e: e.dma_start(out=tabq[0:112, :], in_=tabq_d), writes=[bC], dma="c")
        S.add("sp", lambda e: e.dma_start(out=tabk[0:112, :], in_=tabk_d), writes=[bC], dma="c")
        S.add("sp", lambda e: e.dma_start(out=vmask, in_=vmask_d), writes=[bC], dma="c")
        ctxs = {}
        for st_ in range(33):
            if st_ < 32:
                T = st_
                sl = T % 3
                S.add("sp", lambda e, T=T, sl=sl, xst=xst: e.dma_start(out=xst[sl], in_=xe[T * 128:(T + 1) * 128, :]),
                      writes=[bxst[sl]], dma="x%d" % sl)
                ctxs[T] = NT.run1(xst[sl], bxst[sl], 1024)
            if st_ >= 1:
                T = st_ - 1
                NT.run2(ctxs[T], g1b, xeT[:, :, T * 128:(T + 1) * 128], bxeT[T])
        S.barrier()
        bXE = Buf("xeT_all")

        A = Alloc(PH_A_BASE)
        acc = A.t([2, OWN], F32)
        wq2 = [A.t([8, 224], BF16) for _ in range(2)]
        wk2 = [A.t([8, 224], BF16) for _ in range(2)]
        wv2 = [A.t([8, 128], BF16) for _ in range(2)]
        bwA2 = [Buf("wA0"), Buf("wA1")]
        Q2 = A.t([2, OWN], BF16)
        K2 = A.t([2, EXT], BF16)
        Vg = A.t([32, 2, 65], BF16)
        Pb = [A.t([1024], BF16) for _ in range(2)]
        rinv = A.t([512], F32)
        bacc, bwA, bQ2, bK2, bVg = Buf("acc"), Buf("wA"), Buf("Q2"), Buf("K2"), Buf("Vg")
        bP = [Buf("P0"), Buf("P1")]
        brinv = Buf("rinv")
        battnA = Buf("attnA")
        for hp in range(2):
            for g, d in enumerate(DILS):
                wi = hp * 3 + g
                wq, wk, wv, bwA = wq2[wi % 2], wk2[wi % 2], wv2[wi % 2], bwA2[wi % 2]

                def load_wA(wj):
                    q_, k_, v_, b_w = wq2[wj % 2], wk2[wj % 2], wv2[wj % 2], bwA2[wj % 2]
                    S.add("pool", lambda e: e.dma_start(out=q_, in_=waq_d[wj].rearrange("(k p) n -> p k n", p=128)),
                          writes=[b_w], dma="wa%d" % (wj % 2))
                    S.add("pool", lambda e: e.dma_start(out=k_, in_=wak_d[wj].rearrange("(k p) n -> p k n", p=128)),
                          writes=[b_w], dma="wa%d" % (wj % 2))
                    S.add("pool", lambda e: e.dma_start(out=v_, in_=wav_d[wj].rearrange("(k p) n -> p k n", p=128)),
                          writes=[b_w], dma="wa%d" % (wj % 2))

                if wi == 0:
                    load_wA(0)
                if wi + 1 < 6:
                    load_wA(wi + 1)
                cnt = 0
                for c in range(4):
                    for hh in range(2):
                        bk = cnt % 2
                        cnt += 1
                        for kc in range(8):
                            S.add("pe", lambda e, kc=kc, hh=hh, c=c, bk=bk, wq=wq: e.matmul(
                                bank(bk)[0:112, :], lhsT=wq[:, kc, hh * 112:(hh + 1) * 112],
                                rhs=xeT[:, kc, 1024 + 512 * c:1024 + 512 * (c + 1)], start=(kc == 0), stop=(kc == 7)),
                                reads=[bwA, bXE], writes=[PB[bk]])
                        S.add("dve", lambda e, hh=hh, c=c, bk=bk: e.tensor_tensor(
                            out=Q2[0:112, hh, 512 * c:512 * (c + 1)], in0=bank(bk)[0:112, :],
                            in1=tabq[0:112, 512 * c:512 * (c + 1)], op=ALU.mult),
                            reads=[PB[bk], bC], writes=[bQ2])
                mlo = 1024 // d - 64
                nb = 16 // d + 1
                elo = mlo * d
                ehi = (mlo + 128 * nb) * d
                c0, c1 = elo // 512, (ehi + 511) // 512
                for c in range(c0, c1):
                    for hh in range(2):
                        bk = cnt % 2
                        cnt += 1
                        for kc in range(8):
                            S.add("pe", lambda e, kc=kc, hh=hh, c=c, bk=bk, wk=wk: e.matmul(
                                bank(bk)[0:112, :], lhsT=wk[:, kc, hh * 112:(hh + 1) * 112],
                                rhs=xeT[:, kc, 512 * c:512 * (c + 1)], start=(kc == 0), stop=(kc == 7)),
                                reads=[bwA, bXE], writes=[PB[bk]])
                        S.add("dve", lambda e, hh=hh, c=c, bk=bk: e.tensor_tensor(
                            out=K2[0:112, hh, 512 * c:512 * (c + 1)], in0=bank(bk)[0:112, :],
                            in1=tabk[0:112, 512 * c:512 * (c + 1)], op=ALU.mult),
                            reads=[PB[bk], bC], writes=[bK2])
                nblk = d * nb
                blist = [(r, i) for r in range(d) for i in range(nb)]
                for q0 in range(0, nblk, 4):
                    grp = blist[q0:q0 + 4]
                    bk = cnt % 2
                    cnt += 1
                    for jj, (r, i) in enumerate(grp):
                        e0 = (mlo + 128 * i) * d + r
                        for kc in range(8):
                            S.add("pe", lambda e, kc=kc, jj=jj, e0=e0, bk=bk, d=d, wv=wv: e.matmul(
                                bank(bk)[:, jj * 128:(jj + 1) * 128], lhsT=xeT[:, kc, e0:e0 + 127 * d + 1:d],
                                rhs=wv[:, kc, :], start=(kc == 0), stop=(kc == 7)),
                                reads=[bwA, bXE], writes=[PB[bk]])
                    n = len(grp)
                    S.add("act", lambda e, q0=q0, n=n, bk=bk: e.activation(
                        out=Vg[:, q0:q0 + n, :, 0:64],
                        in_=bank(bk)[:, 0:n * 128].rearrange("p (a b c) -> p a b c", b=2, c=64), func=AF.Copy),
                        reads=[PB[bk]], writes=[bVg])
                for hh in range(2):
                    S.add("dve", lambda e, hh=hh, nblk=nblk, g=g: e.tensor_copy(
                        out=Vg[:, 0:nblk, hh, 64], in_=vmask[:, VGOFF[g]:VGOFF[g] + nblk]),
                        reads=[bC], writes=[bVg])
                qbs = [(r, j) for r in range(d) for j in range(16 // d)]
                for ui in range(8):
                    sp_ = 1 + (ui % 2)
                    ob = 6 + (ui % 2)
                    Pt = Pb[ui % 2]
                    bPt = bP[ui % 2]
                    bS = [PB[2 * sp_], PB[2 * sp_ + 1]]
                    for u in range(2):
                        S.add("pe", lambda e, sp_=sp_, u=u: e.matmul(
                            pair(sp_)[:, u * 512:(u + 1) * 512], lhsT=ident, rhs=maskb[:, 0:512], start=True, stop=False),
                            reads=[bC], writes=[bS[u]])
                    for u in range(2):
                        r, j = qbs[2 * ui + u]
                        qs = 128 * j * d + r
                        for ab in range(2):
                            ks = (mlo + 128 * (j + ab)) * d + r
                            for hh in range(2):
                                off = u * 512 + ab * 256 + hh * 128
                                last = (ab == 1 and hh == 1)
                                S.add("pe", lambda e, sp_=sp_, off=off, hh=hh, ks=ks, qs=qs, last=last, d=d: e.matmul(
                                    pair(sp_)[:, off:off + 128], lhsT=K2[0:112, hh, ks:ks + 127 * d + 1:d],
                                    rhs=Q2[0:112, hh, qs:qs + 127 * d + 1:d], start=False, stop=last),
                                    reads=[bK2, bQ2], writes=[bS[u]])
                    S.add("act", lambda e, sp_=sp_, Pt=Pt: e.activation(out=Pt, in_=pair(sp_), func=AF.Exp, scale=0.125),
                          reads=bS, writes=[bPt])
                    for u in range(2):
                        r, j = qbs[2 * ui + u]
                        for hh in range(2):
                            for ab in range(2):
                                lb = r * nb + j + ab
                                off = u * 512 + ab * 256 + hh * 128
                                S.add("pe", lambda e, ob=ob, u=u, hh=hh, lb=lb, off=off, ab=ab, Pt=Pt: e.matmul(
                                    bank(ob)[0:65, (u * 2 + hh) * 128:(u * 2 + hh + 1) * 128], lhsT=Vg[:, lb, hh, :],
                                    rhs=Pt[:, off:off + 128], start=(ab == 0), stop=(ab == 1)),
                                    reads=[bVg, bPt], writes=[PB[ob]])
                    for u in range(2):
                        r, j = qbs[2 * ui + u]
                        qs = 128 * j * d + r
                        src = bank(ob)[0:65, u * 256:(u + 1) * 256].rearrange("p (a b) -> p a b", b=128)
                        dstv = acc[0:65, :, qs:qs + 127 * d + 1:d]
                        if g == 0:
                            S.add("dve", lambda e, src=src, dstv=dstv: e.tensor_copy(out=dstv, in_=src),
                                  reads=[PB[ob]], writes=[bacc])
                        else:
                            S.add("dve", lambda e, src=src, dstv=dstv: e.tensor_tensor(out=dstv, in0=src, in1=dstv, op=ALU.add),
                                  reads=[PB[ob], bacc], writes=[bacc])
            for c in range(4):
                for hh in range(2):
                    S.add("pe", lambda e, hh=hh, c=c: e.matmul(
                        bank(0)[0:64, :], lhsT=ones32[64:65, 0:64], rhs=acc[64:65, hh, 512 * c:512 * (c + 1)],
                        start=True, stop=True), reads=[bacc, bC], writes=[PB[0]])
                    S.add("dve", lambda e: e.reciprocal(out=rinv[0:64, :], in_=bank(0)[0:64, :]),
                          reads=[PB[0]], writes=[brinv])
                    S.add("dve", lambda e, hh=hh, c=c, hp=hp: e.tensor_tensor(
                        out=attnA[0:64, 2 * hp + hh, 512 * c:512 * (c + 1)], in0=acc[0:64, hh, 512 * c:512 * (c + 1)],
                        in1=rinv[0:64, :], op=ALU.mult), reads=[bacc, brinv], writes=[battnA])
        S.barrier()

        A = Alloc(CONST_END)
        kvnT = A.t([2, S_TOK], BF16)
        qcnT = A.t([4, OWN], BF16)
        wuq = A.t([4, 1024], BF16)
        wuk = A.t([2, 512], BF16)
        wuv = A.t([2, 512], BF16)
        tabqb = A.t([OWN], F32)
        KT_OFF = A.take(0)
        KT = [A.t([S_TOK], BF16) for _ in range(2)]
        KT1_OFF = KT_OFF + S_TOK * 2
        U0 = A.take(0)
        VT = [A.t([64, 65], BF16) for _ in range(2)]
        QT = [A.t([OWN], BF16) for _ in range(2)]
        U1 = A.off
        PBm = [A.t([1024], BF16) for _ in range(3)]
        lsb = A.t([512], F32)
        rinvB = A.t([512], F32)
        ATT_B_OFF = A.take(0)
        attnB = A.t([8, OWN], BF16)
        assert A.off <= ATOP, (A.off, ATOP)
        AB = Alloc(ATT_B_OFF)
        AB.lim = A.off
        xst = [AB.t([1024], F32) for _ in range(2)]
        NT = NormT(AB, 1024, [0, 1])
        wkv = AB.t([8, 320], BF16)
        xsT = [AB.t([8, 128], BF16) for _ in range(2)]
        kstg = [AB.t([128], BF16) for _ in range(2)]
        t1 = [AB.t([32], F32) for _ in range(2)]
        t2 = [AB.t([32], F32) for _ in range(2)]
        NT2 = NormT(AB, 256, [4, 5])
        NT3 = NormT(AB, 512, [4, 5])
        AC = Alloc(U0)
        AC.lim = U1
        cosk = AC.t([64, 32], F32)
        sink = AC.t([64, 32], F32)
        wqc = AC.t([8, 512], BF16)
        bxst = [Buf("xst%d" % i) for i in range(2)]
        bwkv, bxsT = Buf("wkv"), [Buf("xsT0"), Buf("xsT1")]
        bkvn, bKT, bVT, bQT = Buf("kvnT"), [Buf("KT0"), Buf("KT1")], [Buf("VT0"), Buf("VT1")], [Buf("QT0"), Buf("QT1")]
        bkstg, bt = [Buf("kstg0"), Buf("kstg1")], [Buf("t12a"), Buf("t12b")]
        bqcn, bwu, battnB, btab = Buf("qcnT"), Buf("wu"), Buf("attnB"), Buf("tabqb")

        S.add("pool", lambda e: e.dma_start(out=wkv, in_=wkv_d.rearrange("(k p) n -> p k n", p=128)), writes=[bwkv], dma="wb")
        S.add("pool", lambda e: e.dma_start(out=wqc, in_=wqc_d.rearrange("(k p) n -> p k n", p=128)), writes=[bwkv], dma="wb")
        S.add("pool", lambda e: e.dma_start(out=wuq, in_=wuq_d.rearrange("(k p) n -> p k n", p=128)), writes=[bwu], dma="wb")
        S.add("pool", lambda e: e.dma_start(out=wuk, in_=wuk_d.rearrange("(k p) n -> p k n", p=128)), writes=[bwu], dma="wb")
        S.add("pool", lambda e: e.dma_start(out=wuv, in_=wuv_d.rearrange("(k p) n -> p k n", p=128)), writes=[bwu], dma="wb")
        S.add("sp", lambda e: e.dma_start(out=cosk, in_=cosk_d.rearrange("p (a b) -> p a b", b=32)), writes=[bC], dma="c")
        S.add("sp", lambda e: e.dma_start(out=sink, in_=sink_d.rearrange("p (a b) -> p a b", b=32)), writes=[bC], dma="c")
        S.add("sp", lambda e: e.dma_start(out=tabqb, in_=tabqb_d), writes=[btab], dma="c")
        for i_ in range(2):
            S.add("dve", lambda e, i_=i_: e.memset(kstg[i_], 0.0), writes=[bkstg[i_]])

        cA, cB = {}, {}

        def p1_A1(T, src_d, xst=xst):
            sl = T % 2
            S.add("sp", lambda e: e.dma_start(out=xst[sl], in_=src_d), writes=[bxst[sl]], dma="x%d" % sl)
            cA[T] = NT.run1(xst[sl], bxst[sl], 1024)

        def p1_A2(T, NT=NT):
            NT.run2(cA[T], g1b, xsT[T % 2], bxsT[T % 2])

        def p1_B(T):
            xs_, bxs_, pk, i_ = xsT[T % 2], bxsT[T % 2], 2 + (T % 2), T % 2
            for kc in range(8):
                S.add("pe", lambda e, kc=kc: e.matmul(
                    bank(pk)[:, 0:320], lhsT=xs_[:, kc, :], rhs=wkv[:, kc, :], start=(kc == 0), stop=(kc == 7)),
                    reads=[bxs_, bwkv], writes=[PB[pk]])
            cB[T] = NT2.run1(bank(pk)[:, 0:256], PB[pk], 256)
            S.add("dve", lambda e: e.tensor_tensor(out=t1[i_], in0=bank(pk)[:, 256:288], in1=cosk[:, T, :], op=ALU.mult),
                  reads=[PB[pk], bC], writes=[bt[i_]])
            S.add("dve", lambda e: e.tensor_tensor(out=t2[i_], in0=bank(pk)[:, 288:320], in1=sink[:, T, :], op=ALU.mult),
                  reads=[PB[pk], bC], writes=[bt[i_]])
            S.add("dve", lambda e: e.tensor_tensor(out=kstg[i_][:, 64:96], in0=t1[i_], in1=t2[i_], op=ALU.add),
                  reads=[bt[i_]], writes=[bkstg[i_]])
            S.add("dve", lambda e: e.tensor_tensor(out=kstg[i_][:, 96:128], in0=t1[i_], in1=t2[i_], op=ALU.add),
                  reads=[bt[i_]], writes=[bkstg[i_]])

        def p1_C(T):
            i_, p6 = T % 2, 6 + (T % 2)
            NT2.run2(cB[T], gkvb, kvnT[:, :, T * 128:(T + 1) * 128], bkvn)
            S.add("pe", lambda e: e.transpose(out=bank16(p6)[:, 0:128], in_=kstg[i_], identity=ident),
                  reads=[bkstg[i_], bC], writes=[PB[p6]])
            for b_ in range(2):
                S.add("act", lambda e, b_=b_: e.activation(
                    out=KT[b_][64:128, T * 128:(T + 1) * 128], in_=bank16(p6)[64:128, 0:128], func=AF.Copy),
                    reads=[PB[p6]], writes=[bKT[b_]])

        for st_ in range(64 + 3):
            if st_ < 64:
                p1_A1(st_, xb[st_ * 128:(st_ + 1) * 128, :])
            if 0 <= st_ - 1 < 64:
                p1_A2(st_ - 1)
            if 0 <= st_ - 2 < 64:
                p1_B(st_ - 2)
            if 0 <= st_ - 3 < 64:
                p1_C(st_ - 3)

        cQ = {}

        def p3_B(T):
            xs_, bxs_, pk = xsT[T % 2], bxsT[T % 2], 2 + (T % 2)
            for kc in range(8):
                S.add("pe", lambda e, kc=kc: e.matmul(
                    bank(pk), lhsT=xs_[:, kc, :], rhs=wqc[:, kc, :], start=(kc == 0), stop=(kc == 7)),
                    reads=[bxs_, bwkv], writes=[PB[pk]])
            cQ[T] = NT3.run1(bank(pk), PB[pk], 512)

        def p3_C(T):
            NT3.run2(cQ[T], gqb, qcnT[:, :, T * 128:(T + 1) * 128], bqcn)

        for st_ in range(16 + 3):
            if st_ < 16:
                p1_A1(100 + st_, xe[1024 + st_ * 128:1024 + (st_ + 1) * 128, :])
            if 0 <= st_ - 1 < 16:
                p1_A2(100 + st_ - 1)
            if 0 <= st_ - 2 < 16:
                p3_B(st_ - 2)
            if 0 <= st_ - 3 < 16:
                p3_C(st_ - 3)
        S.barrier()
        for b_ in range(2):
            S.add("dve", lambda e, b_=b_: e.memset(VT[b_][:, :, 64:65], 1.0), writes=[bVT[b_]])

        SC_B = 96 ** -0.5

        def build_head(h):
            b_ = h % 2
            cnt = 0
            for c in range(16):
                bk = 6 + (cnt % 2)
                cnt += 1
                for kc in range(2):
                    S.add("pe", lambda e, kc=kc, c=c, bk=bk: e.matmul(
                        bank(bk)[0:64, :], lhsT=wuk[:, kc, h * 64:(h + 1) * 64], rhs=kvnT[:, kc, 512 * c:512 * (c + 1)],
                        start=(kc == 0), stop=(kc == 1)), reads=[bwu, bkvn], writes=[PB[bk]])
                S.add("dve", lambda e, c=c, bk=bk: e.tensor_copy(out=KT[b_][0:64, 512 * c:512 * (c + 1)], in_=bank(bk)[0:64, :]),
                      reads=[PB[bk]], writes=[bKT[b_]])
            for k8 in range(8):
                bk = 6 + (cnt % 2)
                cnt += 1
                for j in range(8):
                    kb = 8 * k8 + j
                    for kc in range(2):
                        S.add("pe", lambda e, kc=kc, kb=kb, j=j, bk=bk: e.matmul(
                            bank(bk)[:, j * 64:(j + 1) * 64], lhsT=kvnT[:, kc, kb * 128:(kb + 1) * 128],
                            rhs=wuv[:, kc, h * 64:(h + 1) * 64], start=(kc == 0), stop=(kc == 1)),
                            reads=[bwu, bkvn], writes=[PB[bk]])
                S.add("dve", lambda e, k8=k8, bk=bk: e.tensor_copy(
                    out=VT[b_][:, 8 * k8:8 * k8 + 8, 0:64], in_=bank(bk).rearrange("p (a b) -> p a b", b=64)),
                    reads=[PB[bk]], writes=[bVT[b_]])
            for c in range(4):
                bk = 6 + (cnt % 2)
                cnt += 1
                for kc in range(4):
                    S.add("pe", lambda e, kc=kc, c=c, bk=bk: e.matmul(
                        bank(bk), lhsT=wuq[:, kc, h * 128:(h + 1) * 128], rhs=qcnT[:, kc, 512 * c:512 * (c + 1)],
                        start=(kc == 0), stop=(kc == 3)), reads=[bwu, bqcn], writes=[PB[bk]])
                S.add("dve", lambda e, c=c, bk=bk: e.tensor_tensor(
                    out=QT[b_][:, 512 * c:512 * (c + 1)], in0=bank(bk), in1=tabqb[:, 512 * c:512 * (c + 1)], op=ALU.mult),
                    reads=[PB[bk], btab], writes=[bQT[b_]])

        bPm = [Buf("Pm%d" % i) for i in range(3)]
        blsb, brB = Buf("lsb"), Buf("rinvB")
        it_ctr = [0]

        pending_norm = []

        def flush_norm():
            while pending_norm:
                pending_norm.pop(0)()

        def attn_chunk(h, qc):
            b_ = h % 2
            ob = 6 + ((h * 4 + qc) % 2)
            q_ap = QT[b_][:, 512 * qc:512 * (qc + 1)]

            def qk(kp):
                it = it_ctr[0] + kp
                sp_ = it % 3
                for u in range(2):
                    kb = 2 * kp + u
                    S.add("pe", lambda e, sp_=sp_, u=u, kb=kb: e.matmul(
                        pair(sp_)[:, u * 512:(u + 1) * 512], lhsT=KT[b_][:, kb * 128:(kb + 1) * 128], rhs=q_ap,
                        start=True, stop=True), reads=[bKT[b_], bQT[b_]], writes=[PB[2 * sp_ + u]])

            def ex(kp):
                it = it_ctr[0] + kp
                sp_ = it % 3
                pm = it % 3
                S.add("act", lambda e, sp_=sp_, pm=pm: e.activation(out=PBm[pm], in_=pair(sp_), func=AF.Exp, scale=SC_B),
                      reads=[PB[2 * sp_], PB[2 * sp_ + 1]], writes=[bPm[pm]])

            def pv(kp):
                it = it_ctr[0] + kp
                pm = it % 3
                for u in range(2):
                    kb = 2 * kp + u
                    S.add("pe", lambda e, pm=pm, u=u, kb=kb: e.matmul(
                        bank(ob)[0:65, :], lhsT=VT[b_][:, kb, :], rhs=PBm[pm][:, u * 512:(u + 1) * 512],
                        start=(kb == 0), stop=(kb == 63)), reads=[bVT[b_], bPm[pm]], writes=[PB[ob]])

            qk(0)
            qk(1)
            flush_norm()
            for kp in range(32):
                ex(kp)
                if kp + 2 < 32:
                    qk(kp + 2)
                pv(kp)
            it_ctr[0] += 32
            nb_ = 2 * ((it_ctr[0] + 2) % 3)

            def norm():
                S.add("act", lambda e: e.activation(out=lsb[64:65, :], in_=bank(ob)[64:65, :], func=AF.Copy),
                      reads=[PB[ob]], writes=[blsb])
                S.add("pe", lambda e: e.matmul(bank(nb_)[0:64, :], lhsT=ones32[64:65, 0:64], rhs=lsb[64:65, :],
                                               start=True, stop=True), reads=[blsb, bC], writes=[PB[nb_]])
                S.add("dve", lambda e: e.reciprocal(out=rinvB[0:64, :], in_=bank(nb_)[0:64, :]),
                      reads=[PB[nb_]], writes=[brB])
                S.add("dve", lambda e: e.tensor_tensor(out=attnB[0:64, h, 512 * qc:512 * (qc + 1)], in0=bank(ob)[0:64, :],
                                                       in1=rinvB[0:64, :], op=ALU.mult),
                      reads=[PB[ob], brB], writes=[battnB])

            pending_norm.append(norm)

        AC_ = Alloc(CONST_END)
        wg = AC_.t([8, 2048], BF16)
        wob = AC_.t([8, 1024], BF16)
        woa = AC_.t([4, 1024], BF16)
        xsTc = AC_.t([8, 512], BF16)
        W1_END = AC_.off
        assert W1_END - CONST_END >= 8 * 4096 * 2
        wout = AC_.t([8, 1024], BF16)
        assert AC_.off <= KT1_OFF, (AC_.off, KT1_OFF)
        bwg, bwob, bwoa, bwout, bxsTc = Buf("wg"), Buf("wob"), Buf("woa"), Buf("wout"), Buf("xsTc")

        bwgh = [Buf("wg_h0"), Buf("wg_h1")]

        def load_wg_half(hf):
            for pg in range(2):
                c0 = pg * 1024 + hf * 512
                S.add("pool", lambda e, c0=c0: e.dma_start(
                    out=wg[:, :, c0:c0 + 512], in_=wg_d[:, c0:c0 + 512].rearrange("(k p) n -> p k n", p=128)),
                    writes=[bwgh[hf]], dma="wmg%d" % hf)

        def prefetch_mix_weights():
            S.add("pool", lambda e: e.dma_start(out=woa[0:64], in_=woa_d.rearrange("(k p) n -> p k n", p=64)),
                  writes=[bwoa], dma="wmoa")
            S.add("pool", lambda e: e.dma_start(out=wob[0:64], in_=wob_d.rearrange("(k p) n -> p k n", p=64)),
                  writes=[bwob], dma="wmob")
            load_wg_half(0)
            S.add("pool", lambda e: e.dma_start(out=wout, in_=wout_d.rearrange("(k p) n -> p k n", p=128)),
                  writes=[bwout], dma="wmout")
            load_wg_half(1)

        build_head(0)
        for h in range(8):
            for qc in range(3):
                attn_chunk(h, qc)
            if h + 1 < 8:
                flush_norm()
                build_head(h + 1)
            attn_chunk(h, 3)
        flush_norm()
        S.barrier()
        prefetch_mix_weights()

        A = AC_
        A.lim = ATT_B_OFF
        gp1 = A.t([1024], F32)
        xres = [A.t([1024], F32) for _ in range(4)]
        xr2 = [A.t([1024], F32) for _ in range(2)]
        ytmp = A.t([1024], F32)
        NT = NormT(A, 1024, [0, 1])
        uT = A.t([8, 512], BF16)
        ga = A.t([512], F32)
        gb = A.t([512], F32)
        tm1 = A.t([512], F32)
        tm2 = A.t([512], F32)
        bxres, buT = [Buf("xres%d" % i) for i in range(4)], Buf("uT")
        bxr2, bytmp = [Buf("xr2_0"), Buf("xr2_1")], Buf("ytmp")
        bga, bgb, btm1, btm2 = Buf("ga"), Buf("gb"), Buf("tm1"), Buf("tm2")
        bst2 = Buf("stat2")
        S.add("sp", lambda e: e.dma_start(out=gp1, in_=gp1_d), writes=[bC], dma="c")

        def normC(c, NT=NT):
            cc = {}
            for st_ in range(5):
                if st_ < 4:
                    i = st_
                    S.add("sp", lambda e, i=i: e.dma_start(
                        out=xres[i], in_=xe[1024 + 512 * c + 128 * i:1024 + 512 * c + 128 * (i + 1), :]),
                        writes=[bxres[i]], dma="xr%d" % i)
                    cc[i] = NT.run1(xres[i], bxres[i], 1024)
                if st_ >= 1:
                    i = st_ - 1
                    NT.run2(cc[i], g1b, xsTc[:, :, 128 * i:128 * (i + 1)], bxsTc)

        def gatesC(c):
            for ft in range(8):
                for hh in range(4):
                    S.add("pe", lambda e, hh=hh, ft=ft: e.matmul(
                        bank(2), lhsT=woa[0:64, hh, ft * 128:(ft + 1) * 128], rhs=attnA[0:64, hh, 512 * c:512 * (c + 1)],
                        start=(hh == 0), stop=(hh == 3)), reads=[bwoa, battnA], writes=[PB[2]])
                for h in range(8):
                    S.add("pe", lambda e, h=h, ft=ft: e.matmul(
                        bank(3), lhsT=wob[0:64, h, ft * 128:(ft + 1) * 128], rhs=attnB[0:64, h, 512 * c:512 * (c + 1)],
                        start=(h == 0), stop=(h == 7)), reads=[bwob, battnB], writes=[PB[3]])
                for kc in range(8):
                    S.add("pe", lambda e, kc=kc, ft=ft: e.matmul(
                        bank(4), lhsT=wg[:, kc, ft * 128:(ft + 1) * 128], rhs=xsTc[:, kc, :],
                        start=(kc == 0), stop=(kc == 7)), reads=[bwgh[ft // 4], bxsTc], writes=[PB[4]])
                for kc in range(8):
                    S.add("pe", lambda e, kc=kc, ft=ft: e.matmul(
                        bank(5), lhsT=wg[:, kc, 1024 + ft * 128:1024 + (ft + 1) * 128], rhs=xsTc[:, kc, :],
                        start=(kc == 0), stop=(kc == 7)), reads=[bwgh[ft // 4], bxsTc], writes=[PB[5]])
                S.add("act", lambda e, ft=ft: e.activation(out=ga, in_=bank(4), func=AF.Sigmoid, bias=bg[:, ft:ft + 1]),
                      reads=[PB[4], bC], writes=[bga])
                S.add("act", lambda e, ft=ft: e.activation(out=gb, in_=bank(5), func=AF.Sigmoid, bias=bg[:, 8 + ft:9 + ft]),
                      reads=[PB[5], bC], writes=[bgb])
                S.add("dve", lambda e: e.tensor_tensor(out=tm1, in0=bank(2), in1=ga, op=ALU.mult),
                      reads=[PB[2], bga], writes=[btm1])
                S.add("dve", lambda e: e.tensor_tensor(out=tm2, in0=bank(3), in1=gb, op=ALU.mult),
                      reads=[PB[3], bgb], writes=[btm2])
                S.add("dve", lambda e, ft=ft: e.tensor_tensor(out=uT[:, ft, :], in0=tm1, in1=tm2, op=ALU.add),
                      reads=[btm1, btm2], writes=[buT])

        def tokC(c, NT=NT):
            sc = stat[:, 8:10]
            for i in range(4):
                xr = xr2[i % 2]
                bxr = bxr2[i % 2]
                S.add("sp", lambda e, i=i, xr=xr: e.dma_start(
                    out=xr, in_=xe[1024 + 512 * c + 128 * i:1024 + 512 * c + 128 * (i + 1), :]),
                    writes=[bxr], dma="xq%d" % (i % 2))
                for half in range(2):
                    for ft in range(8):
                        S.add("pe", lambda e, i=i, half=half, ft=ft: e.matmul(
                            pair(3)[:, half * 512:(half + 1) * 512], lhsT=uT[:, ft, 128 * i:128 * (i + 1)],
                            rhs=wout[:, ft, half * 512:(half + 1) * 512], start=(ft == 0), stop=(ft == 7)),
                            reads=[buT, bwout], writes=[PB[6 + half]])
                S.add("act", lambda e: e.activation(out=NT.junk, in_=pair(3), func=AF.Square, accum_out=sc[:, 0:1]),
                      reads=[PB[6], PB[7]], writes=[NT.bjunk, bst2])
                S.add("act", lambda e: e.activation(out=sc[:, 1:2], in_=sc[:, 0:1], func=AF.Sqrt, scale=1.0 / 1024, bias=epsb[:, 0:1]),
                      reads=[bst2, bC], writes=[bst2])
                S.add("dve", lambda e: e.reciprocal(out=sc[:, 1:2], in_=sc[:, 1:2]), reads=[bst2], writes=[bst2])
                S.add("dve", lambda e: e.scalar_tensor_tensor(out=ytmp, in0=pair(3), scalar=sc[:, 1:2], in1=gp1,
                                                              op0=ALU.mult, op1=ALU.mult),
                      reads=[PB[6], PB[7], bst2, bC], writes=[bytmp])
                S.add("dve", lambda e, xr=xr: e.tensor_tensor(out=xr, in0=ytmp, in1=xr, op=ALU.add),
                      reads=[bytmp, bxr], writes=[bxr])
                S.add("sp", lambda e, xr=xr, i=i: e.dma_start(
                    out=hs_d[512 * c + 128 * i:512 * c + 128 * (i + 1), :], in_=xr), reads=[bxr], dma="hst%d" % (i % 2))

        AD_ = Alloc(CONST_END)
        w1 = AD_.t([8, 4096], BF16)
        assert AD_.off <= W1_END
        bw1, bw2 = Buf("w1"), Buf("w2")

        bw1q = [Buf("w1q%d" % q) for q in range(4)]

        def prefetch_w1():
            for q in range(4):
                S.add("pool", lambda e, q=q: e.dma_start(
                    out=w1[:, :, 1024 * q:1024 * (q + 1)],
                    in_=w1_d[:, 1024 * q:1024 * (q + 1)].rearrange("(k p) n -> p k n", p=128)),
                    writes=[bw1q[q]], dma="w1q%d" % q)

        for c in range(4):
            normC(c)
            gatesC(c)
            tokC(c)
        S.barrier()
        prefetch_w1()

        A = AD_
        A.lim = ARENA
        w2 = A.t([32, 1024], BF16)
        gp3 = A.t([1024], F32)
        hres = [A.t([1024], F32) for _ in range(4)]
        NT = NormT(A, 1024, [0, 1])
        hsT = [A.t([8, 256], BF16) for _ in range(2)]
        aT = A.t([32, 256], BF16)
        rl = [A.t([256], F32) for _ in range(4)]
        otile = A.t([1024], F32)
        bhres, bhsT, baT = [Buf("hres%d" % i) for i in range(4)], [Buf("hsT0"), Buf("hsT1")], Buf("aT")
        brl, bot, bst3 = [Buf("rl%d" % i) for i in range(4)], Buf("ot"), Buf("stat3")
        for k4 in range(8):
            S.add("pool", lambda e, k4=k4: e.dma_start(
                out=w2[:, 4 * k4:4 * k4 + 4, :], in_=w2_d[512 * k4:512 * (k4 + 1), :].rearrange("(k p) n -> p k n", p=128)),
                writes=[bw2], dma="w2")
        S.add("sp", lambda e: e.dma_start(out=gp3, in_=gp3_d), writes=[bC], dma="c")

        def normD(c, NT=NT):
            hp_ = c % 2
            cc = {}
            for st_ in range(3):
                if st_ < 2:
                    i = st_
                    hi = 2 * hp_ + i
                    S.add("sp", lambda e, i=i, hi=hi: e.dma_start(
                        out=hres[hi], in_=hs_d[256 * c + 128 * i:256 * c + 128 * (i + 1), :]),
                        writes=[bhres[hi]], dma="hr%d" % hi)
                    cc[i] = NT.run1(hres[hi], bhres[hi], 1024)
                if st_ >= 1:
                    i = st_ - 1
                    NT.run2(cc[i], g2b, hsT[hp_][:, :, 128 * i:128 * (i + 1)], bhsT[hp_])

        def ff1(c):
            hp_ = c % 2
            for m in range(32):
                bk = 2 + (m % 4)
                for kc in range(8):
                    S.add("pe", lambda e, kc=kc, m=m, bk=bk: e.matmul(
                        bank(bk)[:, 0:256], lhsT=w1[:, kc, m * 128:(m + 1) * 128], rhs=hsT[hp_][:, kc, :],
                        start=(kc == 0), stop=(kc == 7)), reads=[bw1q[m // 8], bhsT[hp_]], writes=[PB[bk]])
                r_ = rl[m % 4]
                S.add("act", lambda e, bk=bk, r_=r_: e.activation(out=r_, in_=bank(bk)[:, 0:256], func=AF.Relu),
                      reads=[PB[bk]], writes=[brl[m % 4]])
                S.add("dve", lambda e, m=m, r_=r_: e.tensor_tensor(out=aT[:, m, :], in0=r_, in1=r_, op=ALU.mult),
                      reads=[brl[m % 4]], writes=[baT])

        def ff2(c, NT=NT):
            hp_ = c % 2
            sc = stat[:, 8:10]
            for i in range(2):
                hi = 2 * hp_ + i
                for half in range(2):
                    for m in range(32):
                        S.add("pe", lambda e, i=i, half=half, m=m: e.matmul(
                            pair(3)[:, half * 512:(half + 1) * 512], lhsT=aT[:, m, 128 * i:128 * (i + 1)],
                            rhs=w2[:, m, half * 512:(half + 1) * 512], start=(m == 0), stop=(m == 31)),
                            reads=[baT, bw2], writes=[PB[6 + half]])
                S.add("act", lambda e: e.activation(out=NT.junk, in_=pair(3), func=AF.Square, accum_out=sc[:, 0:1]),
                      reads=[PB[6], PB[7]], writes=[NT.bjunk, bst3])
                S.add("act", lambda e: e.activation(out=sc[:, 1:2], in_=sc[:, 0:1], func=AF.Sqrt, scale=1.0 / 1024, bias=epsb[:, 0:1]),
                      reads=[bst3, bC], writes=[bst3])
                S.add("dve", lambda e: e.reciprocal(out=sc[:, 1:2], in_=sc[:, 1:2]), reads=[bst3], writes=[bst3])
                S.add("dve", lambda e: e.scalar_tensor_tensor(out=otile, in0=pair(3), scalar=sc[:, 1:2], in1=gp3,
                                                              op0=ALU.mult, op1=ALU.mult),
                      reads=[PB[6], PB[7], bst3, bC], writes=[bot])
                S.add("dve", lambda e, hi=hi: e.tensor_tensor(out=hres[hi], in0=otile, in1=hres[hi], op=ALU.add),
                      reads=[bot, bhres[hi]], writes=[bhres[hi]])
                S.add("sp", lambda e, hi=hi, i=i: e.dma_start(
                    out=y_d[256 * c + 128 * i:256 * c + 128 * (i + 1), :], in_=hres[hi]), reads=[bhres[hi]], dma="yst%d" % hi)

        normD(0)
        for c in range(8):
            ff1(c)
            if c + 1 < 8:
                normD(c + 1)
            ff2(c)
        S.barrier()
        S.emit(nc, st)
    return nc


def _rot_tables(pos, theta, rot_dim):
    half = rot_dim // 2
    inv = np.float32(theta) ** (-(np.arange(half, dtype=np.float32) * np.float32(2.0) / np.float32(rot_dim)))
    ang = pos.astype(np.float32)[:, None] * inv.astype(np.float32)[None, :]
    c = np.cos(ang).astype(np.float32)
    s = np.sin(ang).astype(np.float32)
    C = np.concatenate([c, c], 1)
    Sg = np.concatenate([-s, s], 1)
    return C, Sg


def _prep_shared(inp):
    f = np.float32
    w_in = np.asarray(inp["w_in"], f)
    sh = {}
    waq = np.zeros((6, 1024, 224), f)
    wak = np.zeros((6, 1024, 224), f)
    wav = np.zeros((6, 1024, 128), f)
    sw = np.concatenate([np.arange(8, 16), np.arange(0, 8)])
    for hp in range(2):
        for g in range(3):
            wi = hp * 3 + g
            for hh in range(2):
                h = 4 * g + 2 * hp + hh
                qb = h * 64
                kb = 768 + h * 64
                vb = 1536 + h * 64
                nope = np.arange(16, 64)
                rot = np.arange(0, 16)
                qcols = np.concatenate([qb + nope, qb + rot, qb + sw, qb + rot, qb + sw])
                kcols = np.concatenate([kb + nope, kb + rot, kb + rot, kb + sw, kb + sw])
                waq[wi][:, hh * 112:(hh + 1) * 112] = w_in[:, qcols]
                wak[wi][:, hh * 112:(hh + 1) * 112] = w_in[:, kcols]
                wav[wi][:, hh * 64:(hh + 1) * 64] = w_in[:, vb:vb + 64]
    sh["waq"], sh["wak"], sh["wav"] = waq, wak, wav
    swb = np.concatenate([np.arange(16, 32), np.arange(0, 16)])
    sh["wkv"] = np.ascontiguousarray(np.concatenate([w_in[:, 2816:3072], w_in[:, 3072:3104], w_in[:, 3072 + swb]], 1))
    sh["wqc"] = np.ascontiguousarray(w_in[:, 2304:2816])
    sh["wg"] = np.ascontiguousarray(w_in[:, 3104:5152])
    w_uq = np.asarray(inp["mla_w_uq"], f)
    cols = []
    for h in range(8):
        b = h * 96
        cols += [b + np.arange(64), b + 64 + np.arange(32), b + 64 + swb]
    sh["wuq"] = np.ascontiguousarray(w_uq[:, np.concatenate(cols)])
    w_ukv = np.asarray(inp["mla_w_ukv"], f)
    sh["wuk"] = np.ascontiguousarray(w_ukv[:, np.concatenate([h * 128 + np.arange(64) for h in range(8)])])
    sh["wuv"] = np.ascontiguousarray(w_ukv[:, np.concatenate([h * 128 + 64 + np.arange(64) for h in range(8)])])
    sh["woa"] = np.asarray(inp["w_o_a"], f)
    sh["wob"] = np.asarray(inp["w_o_b"], f)
    sh["wout"] = np.asarray(inp["w_out"], f)
    sh["w1"] = np.asarray(inp["w_ff1"], f)
    sh["w2"] = np.asarray(inp["w_ff2"], f)

    def gb(v, kc):
        v = np.asarray(v, f).reshape(kc, 128)
        return np.ascontiguousarray(np.repeat(v.T[:, :, None], 128, axis=2).reshape(128, kc * 128))

    sh["g1b"] = gb(inp["norm_mix_pre"], 8)
    sh["g2b"] = gb(inp["norm_mlp_pre"], 8)
    sh["gqb"] = gb(inp["mla_q_norm"], 4)
    sh["gkvb"] = gb(inp["mla_kv_norm"], 2)
    sh["gp1"] = np.ascontiguousarray(np.broadcast_to(np.asarray(inp["norm_mix_post"], f)[None, :], (128, 1024)))
    sh["gp3"] = np.ascontiguousarray(np.broadcast_to(np.asarray(inp["norm_mlp_post"], f)[None, :], (128, 1024)))
    sh["bg"] = np.ascontiguousarray(np.asarray(inp["b_gate"], f).reshape(16, 128).T)
    sh["ident"] = np.eye(128, dtype=f)
    k = np.arange(128)[:, None]
    q = np.arange(128)[None, :]
    mA = np.where(k >= q, 0.0, NEG).astype(f)
    mB = np.where(k <= q, 0.0, NEG).astype(f)
    m512 = np.concatenate([mA, mA, mB, mB], 1)
    sh["maskb"] = np.ascontiguousarray(np.concatenate([m512, m512], 1))
    C, Sg = _rot_tables(np.arange(S_TOK), 10000.0, 32)
    sh["cosk"] = np.ascontiguousarray(C.reshape(64, 128, 32).transpose(1, 0, 2).reshape(128, 64 * 32))
    sh["sink"] = np.ascontiguousarray(Sg.reshape(64, 128, 32).transpose(1, 0, 2).reshape(128, 64 * 32))
    return sh


def _prep_core(x, c):
    f = np.float32
    b, j = c // 4, c % 4
    t0 = OWN * j
    d = {}
    xpad = np.zeros((S_TOK + 2048, 1024), f)
    xpad[1024:1024 + S_TOK] = x[b]
    d["xe"] = np.ascontiguousarray(xpad[t0:t0 + EXT])
    d["xb"] = np.ascontiguousarray(x[b])
    pos_e = np.arange(EXT) + (t0 - 1024)
    Ce, Se = _rot_tables(pos_e, 500000.0, 16)
    ones = np.ones((EXT, 48), f)
    tabk = np.concatenate([ones, Ce, Ce, Se, Se], 1).T
    tabq = np.concatenate([ones, Ce, Se, Ce, Se], 1).T[:, 1024:1024 + OWN]
    d["tabk"] = np.ascontiguousarray(tabk)
    d["tabq"] = np.ascontiguousarray(tabq)
    Cb, Sb = _rot_tables(np.arange(OWN) + t0, 10000.0, 32)
    d["tabqb"] = np.ascontiguousarray(np.concatenate([np.ones((OWN, 64), f), Cb, Sb], 1).T)
    vm = np.zeros((128, NVB), f)
    for (g, r, i), n in VIDX.items():
        dd = DILS[g]
        mlo = 1024 // dd - 64
        e = (mlo + 128 * i + np.arange(128)) * dd + r
        t = e + t0 - 1024
        vm[:, n] = ((t >= 0) & (t < S_TOK)).astype(f)
    d["vmask"] = vm
    return d


def kernel(**inputs):
    x = np.asarray(inputs["x"], np.float32)
    sh = _prep_shared(inputs)
    nc = build_nc()
    in_maps = []
    for c in range(8):
        m = dict(sh)
        m.update(_prep_core(x, c))
        in_maps.append(m)
    res = run_bass_kernel_spmd(nc, in_maps, core_ids=list(range(8)))
    out = np.zeros((2, S_TOK, 1024), np.float32)
    for c in range(8):
        b, j = c // 4, c % 4
        out[b, OWN * j:OWN * (j + 1)] = np.asarray(res.results[c]["y"], np.float32)
    kernel.last_results = res
    return out
```
